# Optimizing a Trainium2 kernel written in Bass

```python
import math
import jax, jax.numpy as jnp
from jax import lax
import numpy as np

D_MODEL = 1024
BATCH = 4
SEQ = 4096
DEPTH = 2

GRID_W = 64
CTX_LEN = 256
EPS = 1e-6
ROPE_BASE = 10000.0

CONV_CH = 256
CONV_K = 31
RET_HEADS = 4
RET_DK = 64
RET_DV = 64
RET_CHUNK = 64
MLA_HEADS = 8
MLA_Q_RANK = 256
MLA_KV_RANK = 128
MLA_NOPE = 64
MLA_ROPE = 32
MLA_V = 64
D_MIX = CONV_CH + RET_HEADS * RET_DV + MLA_HEADS * MLA_V
PROJ_SIZES = (2 * CONV_CH, RET_HEADS * RET_DK, RET_HEADS * RET_DK, RET_HEADS * RET_DV, RET_HEADS * RET_DV,
              MLA_Q_RANK, MLA_KV_RANK, MLA_ROPE)
D_IN = 2 * CONV_CH + 2 * RET_HEADS * RET_DK + 2 * RET_HEADS * RET_DV + MLA_Q_RANK + MLA_KV_RANK + MLA_ROPE
D_FF = ((8 * D_MODEL // 3 + 255) // 256) * 256
ATTN_BLOCK = 128

kernel_name = "hybrid_conv_retention_mla_dit_block"


def rmsnorm(x, g):
    xf = x.astype(jnp.float32)
    y = xf * lax.rsqrt(jnp.mean(xf * xf, axis=-1, keepdims=True) + EPS)
    return (y * g.astype(jnp.float32)).astype(x.dtype)


def layernorm(x, g, b):
    xf = x.astype(jnp.float32)
    mu = jnp.mean(xf, axis=-1, keepdims=True)
    var = jnp.mean(jnp.square(xf - mu), axis=-1, keepdims=True)
    y = (xf - mu) * lax.rsqrt(var + EPS)
    return (y * g.astype(jnp.float32) + b.astype(jnp.float32)).astype(x.dtype)


def modulate(h, shift, scale):
    return h * (1.0 + scale[:, None, :]) + shift[:, None, :]


def _rotate_half_axis(x, pos):
    f = x.shape[-1] // 2
    inv = ROPE_BASE ** (-jnp.arange(f, dtype=jnp.float32) / f)
    ang = pos.astype(jnp.float32)[:, None] * inv[None, :]
    cos = jnp.cos(ang)[None, :, None, :]
    sin = jnp.sin(ang)[None, :, None, :]
    xf = x.astype(jnp.float32)
    x1, x2 = xf[..., :f], xf[..., f:]
    return jnp.concatenate([x1 * cos - x2 * sin, x1 * sin + x2 * cos], axis=-1).astype(x.dtype)


def rope_2d(x, row, col):
    h = x.shape[-1] // 2
    return jnp.concatenate([_rotate_half_axis(x[..., :h], row), _rotate_half_axis(x[..., h:], col)], axis=-1)


def split_proj(p):
    B, T, _ = p.shape
    parts = []
    off = 0
    for s in PROJ_SIZES:
        parts.append(p[..., off:off + s])
        off += s
    a, q, k, v, g, cq, ckv, kr = parts
    q = q.reshape(B, T, RET_HEADS, RET_DK)
    k = k.reshape(B, T, RET_HEADS, RET_DK) * (RET_DK ** -0.5)
    v = v.reshape(B, T, RET_HEADS, RET_DV)
    return a, q, k, v, g, cq, ckv, kr


def conv_module(a, w_dw, b_dw, ln_g, ln_b):
    u, gt = jnp.split(a, 2, axis=-1)
    y = u * jax.nn.sigmoid(gt)
    y = lax.conv_general_dilated(y, w_dw[:, None, :], window_strides=(1,),
                                 padding=[(CONV_K // 2, CONV_K // 2)],
                                 dimension_numbers=('NWC', 'WIO', 'NWC'),
                                 feature_group_count=CONV_CH) + b_dw
    return jax.nn.silu(layernorm(y, ln_g, ln_b))


def retention_dir(q, k, v, log_g, s0, strict):
    B, T, H, DK = q.shape
    DV = v.shape[-1]
    C = RET_CHUNK
    N = T // C

    def chunks(t):
        return t.astype(jnp.float32).reshape(B, N, C, H, t.shape[-1]).transpose(1, 0, 3, 2, 4)

    qc, kc, vc = chunks(q), chunks(k), chunks(v)
    lg = log_g.astype(jnp.float32)[:, None]
    idx = jnp.arange(C, dtype=jnp.float32)
    diff = idx[:, None] - idx[None, :]
    mask = (diff > 0) if strict else (diff >= 0)
    dmat = jnp.where(mask[None], jnp.exp(lg[:, :, None] * jnp.maximum(diff, 0.0)[None]), 0.0)
    xi = jnp.exp(lg * (idx + 1.0))[:, :, None]
    zeta = jnp.exp(lg * (C - 1.0 - idx))[:, :, None]
    g_chunk = jnp.exp(lg * C)[:, :, None]

    def step(s, inp):
        qi, ki, vi = inp
        inner = jnp.einsum('bhcd,bhmd->bhcm', qi, ki) * dmat
        o = jnp.einsum('bhcm,bhmv->bhcv', inner, vi) + jnp.einsum('bhcd,bhdv->bhcv', qi, s) * xi
        s = s * g_chunk + jnp.einsum('bhmd,bhmv->bhdv', ki * zeta, vi)
        return s, o

    s, o = lax.scan(step, s0, (qc, kc, vc))
    o = o.transpose(1, 0, 3, 2, 4).reshape(B, T, H, DV)
    return o, s


def retention_bidir(q, k, v, log_g, s_fwd, s_bwd):
    o_f, sf = retention_dir(q, k, v, log_g[0], s_fwd, False)
    o_b, sb = retention_dir(q[:, ::-1], k[:, ::-1], v[:, ::-1], log_g[1], s_bwd, True)
    return o_f + o_b[:, ::-1], sf, sb


def retention_final_states(k, v, log_g):
    T = k.shape[1]
    pos = jnp.arange(T, dtype=jnp.float32)
    lg = log_g.astype(jnp.float32)
    wf = jnp.exp(lg[0][None, :] * (T - 1.0 - pos)[:, None])
    wb = jnp.exp(lg[1][None, :] * pos[:, None])
    kf = k.astype(jnp.float32)
    vf = v.astype(jnp.float32)
    sf = jnp.einsum('bthd,th,bthv->bhdv', kf, wf, vf)
    sb = jnp.einsum('bthd,th,bthv->bhdv', kf, wb, vf)
    return sf, sb


def head_groupnorm(o, g):
    B, T, H, DV = o.shape
    mu = jnp.mean(o, axis=-1, keepdims=True)
    var = jnp.mean(jnp.square(o - mu), axis=-1, keepdims=True)
    y = ((o - mu) * lax.rsqrt(var + EPS)).reshape(B, T, H * DV)
    return y * g.astype(jnp.float32)


def mla_q(cq, q_norm_g, w_uq, row, col):
    B, T, _ = cq.shape
    q = (rmsnorm(cq, q_norm_g) @ w_uq).reshape(B, T, MLA_HEADS, MLA_NOPE + MLA_ROPE)
    q_nope, q_rope = q[..., :MLA_NOPE], q[..., MLA_NOPE:]
    if row is not None:
        q_rope = rope_2d(q_rope, row, col)
    return jnp.concatenate([q_nope, q_rope], axis=-1)


def mla_kv(ckv, kr, kv_norm_g, w_ukv, row, col):
    B, T, _ = ckv.shape
    kv = (rmsnorm(ckv, kv_norm_g) @ w_ukv).reshape(B, T, MLA_HEADS, MLA_NOPE + MLA_V)
    k_nope, v = kv[..., :MLA_NOPE], kv[..., MLA_NOPE:]
    kr = kr[:, :, None, :]
    if row is not None:
        kr = rope_2d(kr, row, col)
    k = jnp.concatenate([k_nope, jnp.broadcast_to(kr, (B, T, MLA_HEADS, MLA_ROPE))], axis=-1)
    return k, v


def block_attention(q, k, v):
    B, T, H, dq = q.shape
    dv = v.shape[-1]
    nb = T // ATTN_BLOCK
    scale = dq ** -0.5
    qb = q.reshape(B, nb, ATTN_BLOCK, H, dq).transpose(1, 0, 2, 3, 4)

    def one(qi):
        s = jnp.einsum('bqhd,bkhd->bhqk', qi, k).astype(jnp.float32) * scale
        p = jax.nn.softmax(s, axis=-1).astype(v.dtype)
        return jnp.einsum('bhqk,bkhd->bqhd', p, v)

    o = lax.map(one, qb)
    return o.transpose(1, 0, 2, 3, 4).reshape(B, T, H * dv)


def mixer_out(a, o_ret, g, att, conv_w, conv_b, conv_ln_g, conv_ln_b, ret_gn_g, w_out):
    y_conv = conv_module(a, conv_w, conv_b, conv_ln_g, conv_ln_b)
    y_ret = head_groupnorm(o_ret, ret_gn_g).astype(g.dtype) * jax.nn.silu(g)
    return jnp.concatenate([y_conv, y_ret, att], axis=-1) @ w_out


def swiglu(h, w1, w2):
    u, gt = jnp.split(h @ w1, 2, axis=-1)
    return (jax.nn.silu(gt) * u) @ w2


def setup_inputs(seed: int = 0) -> dict:
    key = jax.random.key(seed)
    ks = iter(jax.random.split(key, 32))
    L = DEPTH

    def nrm(shape, scale):
        return jax.random.normal(next(ks), shape, jnp.float32) * scale

    def gain(n):
        return 1.0 + nrm((L, n), 0.02)

    base = jnp.log1p(-jnp.power(2.0, -5.0 - jnp.arange(RET_HEADS, dtype=jnp.float32)))
    return {
        "x": nrm((BATCH, SEQ, D_MODEL), 1.0),
        "c": nrm((BATCH, D_MODEL), 1.0),
        "ctx": nrm((BATCH, CTX_LEN, D_MODEL), 1.0),
        "c_ctx": nrm((D_MODEL,), 1.0),
        "mod_w": nrm((L, D_MODEL, 6 * D_MODEL), 0.5 * D_MODEL ** -0.5),
        "mod_b": nrm((L, 6 * D_MODEL), 0.02),
        "pre1_g": gain(D_MODEL),
        "post1_g": gain(D_MODEL),
        "pre2_g": gain(D_MODEL),
        "post2_g": gain(D_MODEL),
        "w_in": nrm((L, D_MODEL, D_IN), D_MODEL ** -0.5),
        "conv_w": nrm((L, CONV_K, CONV_CH), CONV_K ** -0.5),
        "conv_b": nrm((L, CONV_CH), 0.02),
        "conv_ln_g": gain(CONV_CH),
        "conv_ln_b": nrm((L, CONV_CH), 0.02),
        "ret_log_decay": base[None, None, :] * jnp.exp(nrm((L, 2, RET_HEADS), 0.1)),
        "ret_gn_g": gain(RET_HEADS * RET_DV),
        "mla_q_norm_g": gain(MLA_Q_RANK),
        "mla_w_uq": nrm((L, MLA_Q_RANK, MLA_HEADS * (MLA_NOPE + MLA_ROPE)), MLA_Q_RANK ** -0.5),
        "mla_kv_norm_g": gain(MLA_KV_RANK),
        "mla_w_ukv": nrm((L, MLA_KV_RANK, MLA_HEADS * (MLA_NOPE + MLA_V)), MLA_KV_RANK ** -0.5),
        "w_out": nrm((L, D_MIX, D_MODEL), D_MIX ** -0.5),
        "ffn_w_in": nrm((L, D_MODEL, 2 * D_FF), D_MODEL ** -0.5),
        "ffn_w_out": nrm((L, D_FF, D_MODEL), D_FF ** -0.5),
    }


def reference(x, c, ctx, c_ctx, mod_w, mod_b, pre1_g, post1_g, pre2_g, post2_g, w_in, conv_w, conv_b,
              conv_ln_g, conv_ln_b, ret_log_decay, ret_gn_g, mla_q_norm_g, mla_w_uq, mla_kv_norm_g, mla_w_ukv,
              w_out, ffn_w_in, ffn_w_out):
    B, T, _ = x.shape
    ROWS = T // GRID_W
    row = jnp.repeat(jnp.arange(ROWS, dtype=jnp.int32), GRID_W)
    col = jnp.tile(jnp.arange(GRID_W, dtype=jnp.int32), ROWS)
    xc = ctx
    for l in range(DEPTH):
        last = l == DEPTH - 1
        sh1, sc1, g1, sh2, sc2, g2 = jnp.split(jax.nn.silu(c) @ mod_w[l] + mod_b[l], 6, axis=-1)
        csh1, csc1, cg1, csh2, csc2, cg2 = jnp.split(jax.nn.silu(c_ctx)[None] @ mod_w[l] + mod_b[l], 6, axis=-1)

        pl = modulate(rmsnorm(x, pre1_g[l]), sh1, sc1) @ w_in[l]
        pc = modulate(rmsnorm(xc, pre1_g[l]), csh1, csc1) @ w_in[l]
        aL, qL, kL, vL, gL, cqL, ckvL, krL = split_proj(pl)
        aC, qC, kC, vC, gC, cqC, ckvC, krC = split_proj(pc)

        lgd = ret_log_decay[l]
        if last:
            sf, sb = retention_final_states(kC, vC, lgd)
        else:
            zeros = jnp.zeros((B, RET_HEADS, RET_DK, RET_DV), jnp.float32)
            oC_ret, sf, sb = retention_bidir(qC, kC, vC, lgd, zeros, zeros)
        oL_ret, _, _ = retention_bidir(rope_2d(qL, row, col), rope_2d(kL, row, col), vL, lgd, sf, sb)

        kmC, vmC = mla_kv(ckvC, krC, mla_kv_norm_g[l], mla_w_ukv[l], None, None)
        kmL, vmL = mla_kv(ckvL, krL, mla_kv_norm_g[l], mla_w_ukv[l], row, col)
        qmL = mla_q(cqL, mla_q_norm_g[l], mla_w_uq[l], row, col)
        attL = block_attention(qmL, jnp.concatenate([kmC, kmL], axis=1), jnp.concatenate([vmC, vmL], axis=1))

        yL = mixer_out(aL, oL_ret, gL, attL, conv_w[l], conv_b[l], conv_ln_g[l], conv_ln_b[l], ret_gn_g[l], w_out[l])
        x = x + g1[:, None, :] * rmsnorm(yL, post1_g[l])
        if not last:
            qmC = mla_q(cqC, mla_q_norm_g[l], mla_w_uq[l], None, None)
            attC = block_attention(qmC, kmC, vmC)
            yC = mixer_out(aC, oC_ret, gC, attC, conv_w[l], conv_b[l], conv_ln_g[l], conv_ln_b[l], ret_gn_g[l], w_out[l])
            xc = xc + cg1[:, None, :] * rmsnorm(yC, post1_g[l])

        hL = swiglu(modulate(rmsnorm(x, pre2_g[l]), sh2, sc2), ffn_w_in[l], ffn_w_out[l])
        x = x + g2[:, None, :] * rmsnorm(hL, post2_g[l])
        if not last:
            hC = swiglu(modulate(rmsnorm(xc, pre2_g[l]), csh2, csc2), ffn_w_in[l], ffn_w_out[l])
            xc = xc + cg2[:, None, :] * rmsnorm(hC, post2_g[l])
    return x
```

```python
import contextlib
import math
import numpy as np
import concourse.bass as bass
import concourse.mybir as mybir
from concourse.bass_utils import run_bass_kernel_spmd

F32 = mybir.dt.float32
BF16 = mybir.dt.bfloat16
AF = mybir.ActivationFunctionType
ALU = mybir.AluOpType
ENGINES = ("tensor", "vector", "scalar", "gpsimd", "sync")
PE, V, S, G, SP = ENGINES
EPS = 1e-6
ROPE_BASE = 10000.0
DUP_ST_MLA = 0
DUP_ST_RET = 0
FAST_RSQRT = 0
RET_SPLIT = 0
CONV_K = 31
HALO = 15
NWC = 21


class Cfg:
    def __init__(s, D=1024, TH=2048, CTX=256, DFF=2816, GW=64, BATCH=4, DEPTH=2, QG=512):
        s.D, s.TH, s.CTX, s.DFF, s.GW, s.BATCH, s.DEPTH = D, TH, CTX, DFF, GW, BATCH, DEPTH
        s.KC, s.NT, s.NC, s.FC = D // 128, TH // 128, CTX // 128, DFF // 128
        s.T = 2 * TH
        s.QG = min(QG, TH)
        s.NG = TH // s.QG
        s.GT = s.QG // 128
        s.NK = CTX + 2 * TH
        s.NKT = s.NK // 128
        s.MG = 6 * D // 512
        o = 0
        s.vo = {}
        for name, n in (("pre1", s.KC), ("post1", s.KC), ("pre2", s.KC), ("post2", s.KC), ("modb", 6 * s.KC),
                        ("convb", 2), ("lng", 2), ("lnb", 2), ("convw", 2 * CONV_K), ("qg", 2), ("kvg", 1)):
            s.vo[name] = o
            o += n
        s.NV = o


class Res:
    __slots__ = ("name", "w", "r")

    def __init__(self, name=""):
        self.name = name
        self.w = None
        self.r = []


class Emit:
    def __init__(self, nc, stack):
        self.nc = nc
        self.stack = stack
        self.q = {e: [] for e in ENGINES}
        self.sems = {}
        self.count = {}
        self.seen = {e: {} for e in ENGINES}
        for e in ENGINES:
            self._mksem("E_" + e)
        self.nblk = 0
        self.phase_keys = set()

    def _mksem(self, key):
        if key not in self.sems:
            self.sems[key] = self.stack.enter_context(self.nc.semaphore("s_" + key))
            self.count[key] = 0
        return key

    def _wait(self, eng, ev):
        if ev is None:
            return
        key, val = ev
        if eng == PE and key == "E_tensor":
            return
        if self.seen[eng].get(key, 0) >= val:
            return
        self.seen[eng][key] = val
        sem = self.sems[key]
        self.q[eng].append(lambda e, sem=sem, val=val: e.wait_ge(sem, val))

    def _deps(self, eng, reads, writes):
        for r in reads:
            self._wait(eng, r.w)
        for w in writes:
            self._wait(eng, w.w)
            for ev in w.r:
                self._wait(eng, ev)

    def _commit(self, ev, reads, writes):
        for r in reads:
            r.r.append(ev)
        for w in writes:
            w.w = ev
            w.r = []

    def op(self, eng, fn, reads=(), writes=()):
        self._deps(eng, reads, writes)
        key = "E_" + eng
        self.count[key] += 1
        val = self.count[key]
        sem = self.sems[key]
        self.q[eng].append(lambda e, fn=fn, sem=sem: fn(e).then_inc(sem, 1))
        self._commit((key, val), reads, writes)

    def dma(self, queue, out, in_, reads=(), writes=(), semkey="d", **kw):
        self._deps(queue, reads, writes)
        key = self._mksem("D_" + semkey)
        self.phase_keys.add(key)
        self.count[key] += 16
        val = self.count[key]
        sem = self.sems[key]
        self.q[queue].append(
            lambda e, out=out, in_=in_, sem=sem, kw=kw: e.dma_start(out=out, in_=in_, **kw).then_inc(sem, 16))
        ev = (key, val)
        self._commit(ev, reads, writes)
        return ev

    def dma_group(self, queue, pairs, reads=(), writes=(), semkey="d", **kw):
        self._deps(queue, reads, writes)
        key = self._mksem("D_" + semkey)
        self.phase_keys.add(key)
        sem = self.sems[key]
        for (out, in_) in pairs:
            self.count[key] += 16
            self.q[queue].append(
                lambda e, out=out, in_=in_, sem=sem, kw=kw: e.dma_start(out=out, in_=in_, **kw).then_inc(sem, 16))
        ev = (key, self.count[key])
        self._commit(ev, reads, writes)
        return ev

    def wait_event(self, eng, ev):
        self._wait(eng, ev)

    def flush(self):
        nc = self.nc
        if not any(self.q[e] for e in ENGINES):
            return
        for key in sorted(self.phase_keys):
            self._wait(SP, (key, self.count[key]))
        self.phase_keys = set()
        with nc.Block() as block:
            for e in ENGINES:
                lst = self.q[e]
                if not lst:
                    continue

                def body(eng, lst=lst):
                    for f in lst:
                        f(eng)
                getattr(block, e)(body)
        self.q = {e: [] for e in ENGINES}
        self.nblk += 1


def _perm(n_head_dim):
    half = n_head_dim // 2
    q = half // 2
    p = np.zeros(n_head_dim, np.int64)
    for i in range(n_head_dim):
        j = i % half
        base = i - j
        p[i] = base + (j + q if j < q else j - q)
    return p


def _rope_tables(cfg, pos_list, hd):
    half = hd // 2
    f = half // 2
    n = len(pos_list)
    cos = np.ones((hd, n), np.float64)
    sin = np.zeros((hd, n), np.float64)
    pos = np.asarray(pos_list)
    lat = pos >= 0
    row = (pos // cfg.GW).astype(np.float64)
    col = (pos % cfg.GW).astype(np.float64)
    for i in range(hd):
        j = i % half
        part = i // half
        fi = j % f
        sign = -1.0 if j < f else 1.0
        inv = np.float32(ROPE_BASE) ** (-np.float32(fi) / np.float32(f))
        ang = (row if part == 0 else col).astype(np.float32) * np.float32(inv)
        cos[i, lat] = np.cos(ang[lat])
        sin[i, lat] = sign * np.sin(ang[lat])
    return cos.astype(np.float32), sin.astype(np.float32)


def _kt_lists(cfg):
    c = cfg
    lat = []
    nF = nB = nO = 0
    for g in range(c.NG):
        lst = []
        for j in range(c.NC):
            lst.append((j, "F", nF)); nF += 1
        for j in range(c.NC):
            lst.append((j, "B", nB)); nB += 1
        for i in range(c.NT):
            lst.append((c.NC + i, "O", nO)); nO += 1
        for i in range(c.NT):
            kt = c.NC + c.NT + i
            if i < g * c.GT:
                lst.append((kt, "F", nF)); nF += 1
            elif i >= (g + 1) * c.GT:
                lst.append((kt, "B", nB)); nB += 1
            else:
                lst.append((kt, "D", i - g * c.GT))
        lat.append(lst)
    ctxl = [(j, "D", j) for j in range(c.NC)]
    return lat, ctxl, nF, nB, nO


def _core_tables(cfg, h):
    c = cfg
    P0 = h * c.TH
    Q0 = (1 - h) * c.TH
    pos = [-1] * c.CTX + list(range(Q0, Q0 + c.TH)) + list(range(P0, P0 + c.TH))
    rc, rs = _rope_tables(c, pos, 64)
    mc, ms = _rope_tables(c, pos, 32)
    rope = np.zeros((128, 4, c.NK), np.float32)
    rope[:, 0] = np.concatenate([rc, rc], 0)
    rope[:, 1] = np.concatenate([rs, rs], 0)
    rope[:, 2] = 1.0
    rope[64:96, 2] = mc
    rope[64:96, 3] = ms
    s = np.arange(128)[:, None, None]
    j = np.arange(c.GT)[None, :, None]
    t = np.arange(c.QG)[None, None, :]
    dtab = (t - 128 * j - s).astype(np.float32)
    lat, ctxl, nF, nB, nO = _kt_lists(c)
    cx = np.zeros(nF + nB + nO, np.float32)
    for g in range(c.NG):
        gb = P0 + g * c.QG
        for (kt, kind, e) in lat[g]:
            if kt < c.NC:
                kbF = -c.CTX + 128 * kt
                kbB = c.T + 128 * kt
            elif kt < c.NC + c.NT:
                kbF = kbB = Q0 + 128 * (kt - c.NC)
            else:
                kbF = kbB = P0 + 128 * (kt - c.NC - c.NT)
            if kind == "F":
                cx[e] = gb - kbF
            elif kind == "B":
                cx[nF + e] = kbB - gb
            elif kind == "O":
                cx[nF + nB + e] = abs(gb - kbF)
    cxr = np.broadcast_to(cx[None, :], (128, cx.size)).copy()
    fl = np.zeros((128, 2), np.float32)
    fl[:, 0] = 1.0 if h == 1 else 0.0
    fl[:, 1] = 1.0 if h == 0 else 0.0
    rowtab = np.broadcast_to(np.arange(c.QG, dtype=np.float32)[None, :], (128, c.QG)).copy()
    return dict(ropetab=rope, dtab=dtab, cx=cxr, fl=fl, rowtab=rowtab)


def _fm(v):
    return np.ascontiguousarray(v.reshape(-1, 128).T)


def _layer_weights(cfg, inp, l):
    c = cfg
    w_in = inp["w_in"][l]
    p64, p32 = _perm(64), _perm(32)
    a = w_in[:, 0:512]
    q = w_in[:, 512:768]
    k = w_in[:, 768:1024]
    v = w_in[:, 1024:1280]
    g = w_in[:, 1280:1536]
    cq = w_in[:, 1536:1792]
    ckv = w_in[:, 1792:1920]
    kr = w_in[:, 1920:1952]
    hp = np.concatenate([h * 64 + p64 for h in range(4)])
    krc = np.concatenate([ckv[:, 0:64], kr, kr], 1)
    krs = np.concatenate([ckv[:, 0:64], kr[:, p32], kr[:, p32]], 1)
    cat = np.concatenate([a, q, q[:, hp], k, k[:, hp], v, g, cq, ckv, krc, krs], 1)
    assert cat.shape[1] == NWC * 128
    win = np.ascontiguousarray(cat.reshape(c.KC, 128, NWC, 128).transpose(2, 1, 0, 3)).reshape(NWC, 128, c.KC * 128)
    uq = inp["mla_w_uq"][l].reshape(256, 8, 96)
    uqs = uq.copy()
    uqs[:, :, 64:96] = uq[:, :, 64:96][:, :, p32]
    pad = np.zeros((256, 8, 32), np.float32) + uq[:, :, 0:32]
    uqc = np.stack([np.concatenate([uq, pad], 2), np.concatenate([uqs, pad], 2)], 2)
    wuq = np.ascontiguousarray(uqc.reshape(2, 128, 16 * 128))
    wukv = np.ascontiguousarray(inp["mla_w_ukv"][l])
    wout = np.ascontiguousarray(inp["w_out"][l].reshape(8, 128, c.D))
    f1 = inp["ffn_w_in"][l]
    u1, g1 = f1[:, :c.DFF], f1[:, c.DFF:]
    w1 = np.stack([u1.reshape(c.KC, 128, c.FC, 128), g1.reshape(c.KC, 128, c.FC, 128)], 3)
    w1 = np.ascontiguousarray(w1.transpose(2, 1, 0, 3, 4)).reshape(c.FC, 128, c.KC * 256)
    w2 = np.ascontiguousarray(inp["ffn_w_out"][l].reshape(c.FC, 128, c.D))
    mw = inp["mod_w"][l]
    modw = np.ascontiguousarray(mw.reshape(c.KC, 128, c.MG, 512).transpose(2, 1, 0, 3)).reshape(c.MG, 128, c.KC * 512)
    vecs = np.zeros((128, c.NV), np.float32)
    vo = c.vo
    vecs[:, vo["pre1"]:vo["pre1"] + c.KC] = _fm(inp["pre1_g"][l])
    vecs[:, vo["post1"]:vo["post1"] + c.KC] = _fm(inp["post1_g"][l])
    vecs[:, vo["pre2"]:vo["pre2"] + c.KC] = _fm(inp["pre2_g"][l])
    vecs[:, vo["post2"]:vo["post2"] + c.KC] = _fm(inp["post2_g"][l])
    vecs[:, vo["modb"]:vo["modb"] + 6 * c.KC] = _fm(inp["mod_b"][l])
    vecs[:, vo["convb"]:vo["convb"] + 2] = _fm(inp["conv_b"][l])
    vecs[:, vo["lng"]:vo["lng"] + 2] = _fm(inp["conv_ln_g"][l])
    vecs[:, vo["lnb"]:vo["lnb"] + 2] = _fm(inp["conv_ln_b"][l])
    cw = inp["conv_w"][l]
    vecs[:, vo["convw"]:vo["convw"] + 2 * CONV_K] = np.ascontiguousarray(
        cw.T.reshape(2, 128, CONV_K).transpose(1, 0, 2)).reshape(128, 2 * CONV_K)
    vecs[:, vo["qg"]:vo["qg"] + 2] = _fm(inp["mla_q_norm_g"][l])
    vecs[:, vo["kvg"]:vo["kvg"] + 1] = _fm(inp["mla_kv_norm_g"][l])
    rows = np.broadcast_to(inp["ret_gn_g"][l][None, :], (128, 256)).copy()
    lgv = np.broadcast_to(inp["ret_log_decay"][l].reshape(1, 8), (128, 8)).copy()
    d = dict(win=win, wuq=wuq, wukv=wukv, wout=wout, w1=w1, w2=w2, modw=modw, vecs=vecs, rows=rows, lg=lgv)
    return {f"{k}{l}": np.ascontiguousarray(v_, dtype=np.float32) for k, v_ in d.items()}


class Builder:
    def __init__(self, cfg, layers, n_out_ctx, fused=False):
        self.c = cfg
        self.layers = layers
        self.n_out_ctx = n_out_ctx
        self.fused = fused
        self.nc = bass.Bass("TRN2", target_bir_lowering=False)
        self.uid = 0

    def T(self, st, shape, dt, name=None):
        self.uid += 1
        return st.enter_context(self.nc.sbuf_tensor(f"{name or 't'}_{self.uid}", shape, dt))

    def P(self, st, shape, dt, name=None):
        self.uid += 1
        return st.enter_context(self.nc.psum_tensor(f"{name or 'p'}_{self.uid}", shape, dt))

    def dram_in(self, name, shape, dt=F32):
        return self.nc.dram_tensor(name, list(shape), dt, kind="ExternalInput").ap()

    def mm(self, out, lhsT, rhs, start, stop, reads, writes):
        self.em.op(PE, lambda e: e.matmul(out, lhsT=lhsT, rhs=rhs, start=start, stop=stop), reads, writes)

    def tr(self, out, in_, ident, reads, writes):
        self.em.op(PE, lambda e: e.transpose(out=out, in_=in_, identity=ident), reads, writes)

    def act(self, out, in_, func, reads, writes, **kw):
        self.em.op(S, lambda e: e.activation(out=out, in_=in_, func=func, **kw), reads, writes)

    def tt(self, eng, out, in0, in1, op, reads, writes):
        self.em.op(eng, lambda e: e.tensor_tensor(out=out, in0=in0, in1=in1, op=op), reads, writes)

    def ts(self, eng, out, in0, s1, s2, op0, op1, reads, writes):
        if op1 is None:
            self.em.op(eng, lambda e: e.tensor_scalar(out=out, in0=in0, scalar1=s1, scalar2=None, op0=op0), reads, writes)
        else:
            self.em.op(eng, lambda e: e.tensor_scalar(out=out, in0=in0, scalar1=s1, scalar2=s2, op0=op0, op1=op1),
                       reads, writes)

    def stt(self, out, in0, scalar, in1, op0, op1, reads, writes):
        self.em.op(V, lambda e: e.scalar_tensor_tensor(out=out, in0=in0, scalar=scalar, in1=in1, op0=op0, op1=op1),
                   reads, writes)

    def cp(self, eng, out, in_, reads, writes):
        if eng == S:
            self.em.op(S, lambda e: e.copy(out=out, in_=in_), reads, writes)
        else:
            self.em.op(eng, lambda e: e.tensor_copy(out=out, in_=in_), reads, writes)

    def rsqrt_(self, out, in_, width, reads, writes):
        self.em.op(S, lambda e: e.activation(out=out, in_=in_, func=AF.Sqrt), list(reads), writes)
        self.em.op(V, lambda e: e.reciprocal(out=out, in_=out), list(writes), writes)

    def build(self):
        c, nc = self.c, self.nc
        with contextlib.ExitStack() as top:
            self.em = Emit(nc, top)
            em = self.em
            self.d_xown = self.dram_in("x_own", [c.TH, c.D])
            self.d_xoth = self.dram_in("x_oth", [c.TH, c.D])
            self.d_xc = self.dram_in("xc", [c.CTX, c.D])
            self.d_cc = self.dram_in("cc", [128, c.KC, 2])
            self.d_fl = self.dram_in("fl", [128, 2])
            self.d_rope = self.dram_in("ropetab", [128, 4, c.NK])
            self.d_dtab = self.dram_in("dtab", [128, c.GT, c.QG])
            self.d_rowtab = self.dram_in("rowtab", [128, c.QG])
            lat, ctxl, nF, nB, nO = _kt_lists(c)
            self.ktl_lat, self.ktl_ctx, self.nF, self.nB, self.nO = lat, ctxl, nF, nB, nO
            self.d_cx = self.dram_in("cx", [128, nF + nB + nO])
            if self.fused:
                self.d_flB = self.dram_in("flB", [128, 2])
                self.d_ropeB = self.dram_in("ropetabB", [128, 4, c.NK])
                self.d_cxB = self.dram_in("cxB", [128, nF + nB + nO])
            self.dw = {}
            for l in self.layers:
                self.dw[l] = dict(
                    win=self.dram_in(f"win{l}", [NWC, 128, c.KC * 128]),
                    wuq=self.dram_in(f"wuq{l}", [2, 128, 2048]),
                    wukv=self.dram_in(f"wukv{l}", [128, 1024]),
                    wout=self.dram_in(f"wout{l}", [8, 128, c.D]),
                    w1=self.dram_in(f"w1{l}", [c.FC, 128, c.KC * 256]),
                    w2=self.dram_in(f"w2{l}", [c.FC, 128, c.D]),
                    modw=self.dram_in(f"modw{l}", [c.MG, 128, c.KC * 512]),
                    vecs=self.dram_in(f"vecs{l}", [128, c.NV]),
                    rows=self.dram_in(f"rows{l}", [128, 256]),
                    lg=self.dram_in(f"lg{l}", [128, 8]),
                )
            self.d_out = nc.dram_tensor("x_out", [c.TH, c.D], F32, kind="ExternalOutput").ap()
            if self.n_out_ctx:
                self.d_outc = nc.dram_tensor("xc_out", [c.CTX, c.D], F32, kind="ExternalOutput").ap()

            self.hc_ng = 1 + 2 * c.NG
            self.hcache = nc.dram_tensor("hcache", [self.hc_ng, 128, c.KC * c.QG], BF16, kind="Internal").ap()
            self.Rhc = [Res() for _ in range(self.hc_ng)]
            self.x = self.T(top, [128, c.NT, c.D], F32, "x")
            self.xc = self.T(top, [128, c.NC, c.D], F32, "xc")
            self.Rx = [Res(f"x{i}") for i in range(c.NT)]
            self.Rxc = [Res(f"xc{i}") for i in range(c.NC)]
            self.ident = self.T(top, [128, 128], BF16, "ident")
            self.identf = self.T(top, [128, 128], F32, "identf")
            self.onesb = self.T(top, [128, 128], BF16, "onesb")
            self.onesf = self.T(top, [128, 128], F32, "onesf")
            self.epsc = self.T(top, [128, 1], F32, "epsc")
            self.fl = self.T(top, [128, 2], F32, "fl")
            self.flB = self.T(top, [128, 2], F32, "flB")
            self.cc = self.T(top, [128, c.KC, 2], F32, "cc")
            self.Rc = Res("const")
            self.modT = self.T(top, [128, 6 * c.KC, 2], F32, "modT")
            self.AB = self.T(top, [128, 6, c.KC, 2], F32, "AB")
            self.Rmod = Res("mod")
            self.vecs = self.T(top, [128, c.NV], F32, "vecs")
            self.lg = self.T(top, [128, 8], F32, "lg")
            self.lgx = self.T(top, [128, 24], F32, "lgx")
            self.Rvec = Res("vecs")

            self.emit_consts()
            em.flush()
            self.Roth = [Res() for _ in range(c.NT)]
            tabA = (self.d_rope, self.d_cx, self.fl)
            if not self.fused:
                self.load_x(self.d_xown)
                self.d_oth_cur = self.d_xoth
                self.cur_rope, self.cur_cx, self.cur_fl = tabA
                for li, l in enumerate(self.layers):
                    self.emit_layer(l, l == c.DEPTH - 1, first=(li == 0))
            else:
                assert c.DEPTH == 2
                tabB = (self.d_ropeB, self.d_cxB, self.flB)
                xoth1 = nc.dram_tensor("xoth1", [c.TH, c.D], F32, kind="Internal").ap()
                self.load_x(self.d_xoth)
                self.d_oth_cur = self.d_xown
                self.cur_rope, self.cur_cx, self.cur_fl = tabB

                def handover(t):
                    em.dma(SP, xoth1[t * 128:(t + 1) * 128, :], self.x[:, t, :], reads=[self.Rx[t]], writes=[self.Roth[t]],
                           semkey=f"xst{t % 4}")
                    em.dma(SP, self.x[:, t, :], self.d_xown[t * 128:(t + 1) * 128, :], writes=[self.Rx[t]], semkey=f"x{t}")
                self.emit_layer(0, True, first=True, post_tile=handover)
                self.d_oth_cur = self.d_xoth
                self.cur_rope, self.cur_cx, self.cur_fl = tabA
                self.emit_layer(0, False, first=False)
                self.d_oth_cur = xoth1
                self.out_evs = []

                def emit_out(t):
                    self.out_evs.append(em.dma(SP, self.d_out[t * 128:(t + 1) * 128, :], self.x[:, t, :], reads=[self.Rx[t]],
                                               semkey="out"))
                self.emit_layer(1, True, first=False, post_tile=emit_out)
            evs = list(getattr(self, "out_evs", []))
            if not evs:
                for t in range(c.NT):
                    evs.append(em.dma(SP, self.d_out[t * 128:(t + 1) * 128, :], self.x[:, t, :], reads=[self.Rx[t]], semkey="out"))
            if self.n_out_ctx:
                for t in range(c.NC):
                    evs.append(em.dma(SP, self.d_outc[t * 128:(t + 1) * 128, :], self.xc[:, t, :], reads=[self.Rxc[t]],
                                      semkey="out"))
            em.wait_event(SP, evs[-1])
            em.flush()
        return nc

    def emit_consts(self):
        c, em = self.c, self.em
        Rc = self.Rc
        identf, ident, onesb, onesf = self.identf, self.ident, self.onesb, self.onesf
        em.op(G, lambda e: e.memset(identf[:], 0.0), writes=[Rc])
        em.op(G, lambda e: e.affine_select(out=identf[:], in_=identf[:], pattern=[[-1, 128]], compare_op=ALU.not_equal,
                                           fill=1.0, base=0, channel_multiplier=1), reads=[Rc], writes=[Rc])
        em.op(V, lambda e: e.tensor_copy(out=ident[:], in_=identf[:]), reads=[Rc], writes=[Rc])
        em.op(G, lambda e: e.memset(onesf[:], 1.0), writes=[Rc])
        em.op(V, lambda e: e.tensor_copy(out=onesb[:], in_=onesf[:]), reads=[Rc], writes=[Rc])
        epsc = self.epsc
        em.op(G, lambda e: e.memset(epsc[:], EPS), writes=[Rc])
        em.dma(SP, self.fl[:], self.d_fl, writes=[Rc], semkey="c0")
        em.dma(SP, self.cc[:], self.d_cc, writes=[Rc], semkey="c1")
        if self.fused:
            em.dma(SP, self.flB[:], self.d_flB, writes=[Rc], semkey="c2")
        for t in range(c.NC):
            em.dma(SP, self.xc[:, t, :], self.d_xc[t * 128:(t + 1) * 128, :], writes=[self.Rxc[t]], semkey=f"xc{t}")

    def load_x(self, src):
        c, em = self.c, self.em
        for t in range(c.NT):
            em.dma(SP, self.x[:, t, :], src[t * 128:(t + 1) * 128, :], writes=[self.Rx[t]], semkey=f"x{t}")

    def emit_layer(self, l, last, first, post_tile=None):
        c, em = self.c, self.em
        self.l = l
        self.W = self.dw[l]
        done = getattr(self, "_mod_done", set())
        self.phase_mod(full=(l not in done))
        done.add(l)
        self._mod_done = done
        self.phase_norm1()
        with contextlib.ExitStack() as ms:
            self.mixT = self.T(ms, [128, 8, c.TH], BF16, "mixT")
            self.mixTc = self.T(ms, [128, 8, c.CTX], BF16, "mixTc")
            self.Rmix = [[Res() for _ in range(c.NT)] for _ in range(8)]
            self.Rmixc = [[Res() for _ in range(c.NC)] for _ in range(8)]
            self.phase_conv(last)
            for hp in range(2):
                self.phase_ret(hp, last)
            self.phase_mla(last)
            self.phase_out(last)
        self.phase_ffn(last, post_tile)

    def seg_tiles(self, seg):
        c = self.c
        return {"ctx": c.NC, "oth": c.NT, "own": c.NT}[seg]

    def seg_kbase(self, seg):
        c = self.c
        return {"ctx": 0, "oth": c.CTX, "own": c.CTX + c.TH}[seg]

    def seg_groups(self, seg):
        c = self.c
        n = self.seg_tiles(seg)
        g = min(n, c.GT)
        return [(i, g) for i in range(0, n, g)]

    def phase_mod(self, full=True):
        c, em, W = self.c, self.em, self.W
        KC = c.KC
        with contextlib.ExitStack() as ph:
            Rvec, Rmod, Rc = self.Rvec, self.Rmod, self.Rc
            if full:
                em.dma(SP, self.vecs[:], W["vecs"], writes=[Rvec], semkey="vec")
                em.dma(SP, self.lg[:], W["lg"], writes=[Rvec], semkey="vec")
                self.mod_vectors(ph)
            self.decay_scalars()
            em.flush()

    def mod_vectors(self, ph):
        c, em, W = self.c, self.em, self.W
        KC = c.KC
        if True:
            Rvec, Rmod, Rc = self.Rvec, self.Rmod, self.Rc
            scT = self.T(ph, [128, KC, 2], BF16, "scT")
            Rs = Res()
            self.act(scT[:], self.cc[:], AF.Silu, [Rc], [Rs])
            wb = [self.T(ph, [128, KC, 512], BF16, "modw") for _ in range(2)]
            Rwb = [Res(), Res()]
            prow = [self.P(ph, [128, 512], F32, "prow") for _ in range(2)]
            Rprow = [Res(), Res()]
            rowt = [self.T(ph, [2, 512], F32, "rowt") for _ in range(2)]
            Rrow = [Res(), Res()]
            pT = self.P(ph, [128, c.MG * 4, 2], F32, "pT")
            RpT = Res()
            for j in range(c.MG):
                b = j % 2
                for k in range(0, KC, 4):
                    k2 = min(KC, k + 4)
                    em.dma(G, wb[b][:, k:k2, :], W["modw"][j].rearrange("p (k n) -> p k n", k=KC)[:, k:k2, :], writes=[Rwb[b]],
                           semkey=f"mw{b}")
                for k in range(KC):
                    self.mm(prow[b][0:2, :], scT[:, k, :], wb[b][:, k, :], k == 0, k == KC - 1, [Rs, Rwb[b]], [Rprow[b]])
                self.cp(V, rowt[b][:], prow[b][0:2, :], [Rprow[b]], [Rrow[b]])
                for q in range(4):
                    self.mm(pT[:, j * 4 + q, :], rowt[b][0:2, q * 128:(q + 1) * 128], self.identf[0:2, 0:2], True, True,
                            [Rrow[b], Rc], [RpT])
            modT = self.modT
            vo = c.vo
            for s_ in range(2):
                self.tt(V, modT[:, :, s_], pT[:, :, s_], self.vecs[:, vo["modb"]:vo["modb"] + 6 * KC], ALU.add,
                        [RpT, Rvec], [Rmod])
            AB = self.AB
            for s_ in range(2):
                for (dst, vsc, vsh, vg, gpre, gpost) in ((0, 1, 0, 2, "pre1", "post1"), (3, 4, 3, 5, "pre2", "post2")):
                    self.stt(AB[:, dst, :, s_], modT[:, vsc * KC:(vsc + 1) * KC, s_], 1.0,
                             self.vecs[:, vo[gpre]:vo[gpre] + KC], ALU.add, ALU.mult, [Rmod, Rvec], [Rmod])
                    self.cp(V, AB[:, dst + 1, :, s_], modT[:, vsh * KC:(vsh + 1) * KC, s_], [Rmod], [Rmod])
                    self.tt(V, AB[:, dst + 2, :, s_], modT[:, vg * KC:(vg + 1) * KC, s_],
                            self.vecs[:, vo[gpost]:vo[gpost] + KC], ALU.mult, [Rmod, Rvec], [Rmod])

    def decay_scalars(self):
        if True:
            Rvec, Rc = self.Rvec, self.Rc
            lg, lgx, fl = self.lg, self.lgx, self.cur_fl
            self.ts(V, lgx[:, 0:4], lg[:, 4:8], -1.0, None, ALU.mult, None, [Rvec], [Rvec])
            self.ts(V, lgx[:, 4:8], lg[:, 0:4], fl[:, 0:1], None, ALU.mult, None, [Rvec, Rc], [Rvec])
            self.ts(V, lgx[:, 12:16], lg[:, 4:8], fl[:, 1:2], None, ALU.mult, None, [Rvec, Rc], [Rvec])
            self.tt(V, lgx[:, 8:12], lgx[:, 4:8], lgx[:, 12:16], ALU.subtract, [Rvec], [Rvec])
            self.tt(V, lgx[:, 4:8], lgx[:, 4:8], lgx[:, 12:16], ALU.add, [Rvec], [Rvec])
            self.ts(V, lgx[:, 16:20], lg[:, 0:4], -1.0, None, ALU.mult, None, [Rvec], [Rvec])
            self.ts(V, lgx[:, 20:24], lgx[:, 8:12], -1.0, None, ALU.mult, None, [Rvec], [Rvec])

    def bcast_vec(self, ph, vec_idx, s_, pbank, Rp):
        c = self.c
        out = self.T(ph, [128, c.D], F32, "bc")
        Ro = Res()
        dg = [self.T(ph, [128, 128], F32, "dg") for _ in range(2)]
        Rdg = [Res(), Res()]
        for k in range(c.KC):
            b = k % 2
            self.ts(V, dg[b][:], self.identf[:], self.AB[:, vec_idx, k, s_:s_ + 1], None, ALU.mult, None,
                    [self.Rc, self.Rmod], [Rdg[b]])
            self.mm(pbank[:, 0:128], self.onesf[:], dg[b][:], True, True, [Rdg[b], self.Rc], [Rp])
            self.cp(V, out[:, k * 128:(k + 1) * 128], pbank[:, 0:128], [Rp], [Ro])
        return out, Ro

    def make_norm_ctx(self, ph):
        c = self.c
        d = dict(
            junk=self.T(ph, [128, c.D], BF16, "junk"), Rjunk=Res(),
            xn=[self.T(ph, [128, c.D], BF16, "xn") for _ in range(2)], Rxn=[Res(), Res()],
            st=[self.T(ph, [128, 4], F32, "nst") for _ in range(2)], Rst=[Res(), Res()],
            pT=[self.P(ph, [128, c.KC, 128], BF16, "pTn") for _ in range(2)], RpT=[Res(), Res()],
            xo=[self.T(ph, [128, c.D], F32, "xo") for _ in range(2)], Rxo=[Res(), Res()],
            i=0,
        )
        return d

    def norm_A(self, nctx, seg, tile):
        c, em = self.c, self.em
        i = nctx["i"]
        nctx["i"] += 1
        b = i % 2
        if seg == "own":
            xin, Rxin = self.x[:, tile, :], self.Rx[tile]
        elif seg == "ctx":
            xin, Rxin = self.xc[:, tile, :], self.Rxc[tile]
        else:
            xo, Rxo = nctx["xo"][b], nctx["Rxo"][b]
            em.dma(SP, xo[:], self.d_oth_cur[tile * 128:(tile + 1) * 128, :], reads=[self.Roth[tile]], writes=[Rxo],
                   semkey=f"xo{b}")
            xin, Rxin = xo[:], Rxo
        st, Rst = nctx["st"][b], nctx["Rst"][b]
        xn, Rxn = nctx["xn"][b], nctx["Rxn"][b]
        self.act(nctx["junk"][:], xin, AF.Square, [Rxin], [nctx["Rjunk"], Rst], accum_out=st[:, 0:1])
        if FAST_RSQRT:
            self.act(st[:, 2:3], st[:, 0:1], AF.Abs_reciprocal_sqrt, [Rst, self.Rc], [Rst], scale=1.0 / c.D, bias=self.epsc[:, 0:1])
        else:
            self.ts(V, st[:, 1:2], st[:, 0:1], 1.0 / c.D, EPS, ALU.mult, ALU.add, [Rst], [Rst])
            self.rsqrt_(st[:, 2:3], st[:, 1:2], 1, [Rst], [Rst])
        self.ts(V, xn[:], xin, st[:, 2:3], None, ALU.mult, None, [Rxin, Rst], [Rxn])
        return b

    def norm_B(self, nctx, b, seg, hT, col0, RhT, vA, s_=None):
        c = self.c
        if s_ is None:
            s_ = 1 if seg == "ctx" else 0
        xn, Rxn = nctx["xn"][b], nctx["Rxn"][b]
        pT, RpT = nctx["pT"][b], nctx["RpT"][b]
        for k in range(c.KC):
            self.tr(pT[:, k, :], xn[:, k * 128:(k + 1) * 128], self.ident[:], [Rxn, self.Rc], [RpT])
        for k in range(c.KC):
            A = self.AB[:, vA, k, s_:s_ + 1]
            B = self.AB[:, vA + 1, k, s_:s_ + 1]
            if k % 2 == 0:
                self.ts(V, hT[:, k, col0:col0 + 128], pT[:, k, :], A, B, ALU.mult, ALU.add, [RpT, self.Rmod], [RhT])
            else:
                self.act(hT[:, k, col0:col0 + 128], pT[:, k, :], AF.Identity, [RpT, self.Rmod], [RhT], scale=A, bias=B)

    def norm_items(self, nctx, items):
        prev = None
        for it in items:
            b = self.norm_A(nctx, it[0], it[1])
            if prev is not None:
                pb, pit = prev
                self.norm_B(nctx, pb, pit[0], pit[2], pit[3], pit[4], pit[5])
                if pit[6] is not None:
                    pit[6]()
            prev = (b, it)
        if prev is not None:
            pb, pit = prev
            self.norm_B(nctx, pb, pit[0], pit[2], pit[3], pit[4], pit[5])
            if pit[6] is not None:
                pit[6]()

    def hc_index(self, seg, t0):
        c = self.c
        return {"ctx": 0, "oth": 1, "own": 1 + c.NG}[seg] + (0 if seg == "ctx" else t0 // c.GT)

    def phase_norm1(self):
        c, em = self.c, self.em
        with contextlib.ExitStack() as sp:
            nctx = self.make_norm_ctx(sp)
            hT = [self.T(sp, [128, c.KC, c.QG], BF16, "hT") for _ in range(2)]
            RhT = [Res(), Res()]
            gi = 0
            items = []
            for seg in ("ctx", "oth", "own"):
                for (t0, nt) in self.seg_groups(seg):
                    b = gi % 2
                    gi += 1
                    g = self.hc_index(seg, t0)

                    def store(b=b, g=g, nt=nt):
                        dst = self.hcache[g].rearrange("p (k n) -> p k n", k=c.KC)
                        em.dma(SP, dst[:, :, 0:nt * 128], hT[b][:, :, 0:nt * 128], reads=[RhT[b]], writes=[self.Rhc[g]],
                               semkey=f"hcst{b}")
                    for i in range(nt):
                        items.append((seg, t0 + i, hT[b], i * 128, RhT[b], 0, store if i == nt - 1 else None))
            self.norm_items(nctx, items)
            em.flush()

    def load_h(self, hT, RhT, seg, t0, nt, key):
        c = self.c
        g = self.hc_index(seg, t0)
        off = (t0 % c.GT) * 128 if seg != "ctx" else t0 * 128
        src = self.hcache[g].rearrange("p (k n) -> p k n", k=c.KC)
        self.em.dma(SP, hT[:, :, 0:nt * 128], src[:, :, off:off + nt * 128], reads=[self.Rhc[g]], writes=[RhT], semkey=key)

    def load_w(self, dst, src, Rd, key):
        return self.em.dma(G, dst, src, writes=[Rd], semkey=key)

    def load_win(self, ph, chunks):
        c = self.c
        out = {}
        for i, ch in enumerate(chunks):
            t = self.T(ph, [128, c.KC, 128], BF16, f"win{ch}")
            R = Res()
            self.load_w(t[:], self.W["win"][ch].rearrange("p (k n) -> p k n", k=c.KC), R, f"win{i}")
            out[ch] = (t, R)
        return out

    def proj_fm(self, ps, Rps, wt, hT, RhT, cols, ncols, M=128):
        c = self.c
        w, Rw = wt
        for k in range(c.KC):
            self.mm(ps[0:M, 0:ncols], w[:, k, 0:M], hT[:, k, cols:cols + ncols], k == 0, k == c.KC - 1, [Rw, RhT], [Rps])

    def proj_tm(self, ps, Rps, wt, hT, RhT, col0, o0=0):
        c = self.c
        w, Rw = wt
        for k in range(c.KC):
            self.mm(ps[:, o0:o0 + 128], hT[:, k, col0:col0 + 128], w[:, k, :], k == 0, k == c.KC - 1, [Rw, RhT], [Rps])

    def phase_conv(self, last):
        c, em = self.c, self.em
        vo = c.vo
        with contextlib.ExitStack() as ph:
            wts = self.load_win(ph, [0, 1, 2, 3])
            QG = c.QG
            hT = [self.T(ph, [128, c.KC, QG], BF16, "hT") for _ in range(2)]
            RhT = [Res(), Res()]
            pu = [self.P(ph, [128, 512], F32, "pu") for _ in range(2)]
            pg = [self.P(ph, [128, 512], F32, "pg") for _ in range(2)]
            Rpu, Rpg = [Res(), Res()], [Res(), Res()]
            sg = [self.T(ph, [128, 512], F32, "sg") for _ in range(2)]
            Rsg = [Res(), Res()]
            segs = [("own", c.TH)] + ([] if last else [("ctx", c.CTX)])
            for seg, ntok in segs:
                with contextlib.ExitStack() as sp:
                    ypad = self.T(sp, [128, 2, ntok + 2 * HALO], F32, "ypad")
                    Rypad = [Res(), Res()]
                    acc = self.T(sp, [128, 2, ntok], F32, "acc")
                    Racc = [Res(), Res()]
                    for ch in range(2):
                        em.op(G, lambda e, ch=ch: e.memset(ypad[:, ch, 0:HALO], 0.0), writes=[Rypad[ch]])
                        em.op(G, lambda e, ch=ch, ntok=ntok: e.memset(ypad[:, ch, HALO + ntok:2 * HALO + ntok], 0.0),
                              writes=[Rypad[ch]])
                    gi = 0

                    def glu_group(sseg, t0, nt, dst_fn):
                        nonlocal gi
                        b = gi % 2
                        gi += 1
                        self.load_h(hT[b], RhT[b], sseg, t0, nt, f"hld{b}")
                        n = nt * 128
                        for ch in range(2):
                            bb = ch
                            self.proj_fm(pu[bb], Rpu[bb], wts[ch], hT[b], RhT[b], 0, n)
                            self.proj_fm(pg[bb], Rpg[bb], wts[2 + ch], hT[b], RhT[b], 0, n)
                            self.act(sg[bb][:, 0:n], pg[bb][:, 0:n], AF.Sigmoid, [Rpg[bb]], [Rsg[bb]])
                            dst_fn(ch, pu[bb], Rpu[bb], sg[bb], Rsg[bb], n)

                    for (t0, nt) in self.seg_groups(seg):
                        def dst(ch, pu_, Rpu_, sg_, Rsg_, n, t0=t0):
                            self.tt(V, ypad[:, ch, HALO + t0 * 128:HALO + t0 * 128 + n], pu_[:, 0:n], sg_[:, 0:n], ALU.mult,
                                    [Rpu_, Rsg_], [Rypad[ch]])
                        glu_group(seg, t0, nt, dst)
                    if seg == "own":
                        tmp = self.T(sp, [128, 128], F32, "halo")
                        Rtmp = Res()

                        def dst_l(ch, pu_, Rpu_, sg_, Rsg_, n):
                            self.tt(V, tmp[:], pu_[:, 0:128], sg_[:, 0:128], ALU.mult, [Rpu_, Rsg_], [Rtmp])
                            self.ts(V, ypad[:, ch, 0:HALO], tmp[:, 128 - HALO:128], self.cur_fl[:, 0:1], None, ALU.mult, None,
                                    [Rtmp, self.Rc], [Rypad[ch]])

                        def dst_r(ch, pu_, Rpu_, sg_, Rsg_, n, ntok=ntok):
                            self.tt(V, tmp[:], pu_[:, 0:128], sg_[:, 0:128], ALU.mult, [Rpu_, Rsg_], [Rtmp])
                            self.ts(V, ypad[:, ch, HALO + ntok:2 * HALO + ntok], tmp[:, 0:HALO], self.cur_fl[:, 1:2], None,
                                    ALU.mult, None, [Rtmp, self.Rc], [Rypad[ch]])
                        glu_group("oth", c.NT - 1, 1, dst_l)
                        glu_group("oth", 0, 1, dst_r)
                    cw0 = vo["convw"]
                    for ch in range(2):
                        for k in range(CONV_K):
                            wk = self.vecs[:, cw0 + ch * CONV_K + k:cw0 + ch * CONV_K + k + 1]
                            if k == 0:
                                self.ts(V, acc[:, ch, :], ypad[:, ch, 0:ntok], wk, None, ALU.mult, None,
                                        [Rypad[ch], self.Rvec], [Racc[ch]])
                            else:
                                self.stt(acc[:, ch, :], ypad[:, ch, k:k + ntok], wk, acc[:, ch, :], ALU.mult, ALU.add,
                                         [Rypad[ch], self.Rvec, Racc[ch]], [Racc[ch]])
                        self.ts(V, acc[:, ch, :], acc[:, ch, :], self.vecs[:, vo["convb"] + ch:vo["convb"] + ch + 1], None,
                                ALU.add, None, [Racc[ch], self.Rvec], [Racc[ch]])
                    sq = self.T(sp, [128, 2, 512], F32, "sq")
                    Rsq = Res()
                    mean = self.T(sp, [128, 512], F32, "mean")
                    var = self.T(sp, [128, 512], F32, "var")
                    Rmv = Res()
                    zt = self.T(sp, [128, 512], F32, "zt")
                    Rzt = Res()
                    mixT, Rmix = (self.mixT, self.Rmix) if seg == "own" else (self.mixTc, self.Rmixc)
                    for (t0, nt) in self.seg_groups(seg):
                        n = nt * 128
                        c0 = t0 * 128
                        p1, Rp1, p2, Rp2 = pu[0], Rpu[0], pg[0], Rpg[0]
                        for ch in range(2):
                            self.act(sq[:, ch, 0:n], acc[:, ch, c0:c0 + n], AF.Square, [Racc[ch]], [Rsq])
                        for ch in range(2):
                            self.mm(p1[:, 0:n], self.onesf[:], acc[:, ch, c0:c0 + n], ch == 0, ch == 1, [Racc[ch], self.Rc], [Rp1])
                        for ch in range(2):
                            self.mm(p2[:, 0:n], self.onesf[:], sq[:, ch, 0:n], ch == 0, ch == 1, [Rsq, self.Rc], [Rp2])
                        self.ts(V, mean[:, 0:n], p1[:, 0:n], 1.0 / 256, None, ALU.mult, None, [Rp1], [Rmv])
                        self.tt(V, var[:, 0:n], mean[:, 0:n], mean[:, 0:n], ALU.mult, [Rmv], [Rmv])
                        self.stt(var[:, 0:n], p2[:, 0:n], 1.0 / 256, var[:, 0:n], ALU.mult, ALU.subtract, [Rp2, Rmv], [Rmv])
                        self.ts(V, var[:, 0:n], var[:, 0:n], EPS, None, ALU.add, None, [Rmv], [Rmv])
                        self.rsqrt_(var[:, 0:n], var[:, 0:n], n, [Rmv], [Rmv])
                        for ch in range(2):
                            self.tt(V, zt[:, 0:n], acc[:, ch, c0:c0 + n], mean[:, 0:n], ALU.subtract, [Racc[ch], Rmv], [Rzt])
                            self.tt(V, zt[:, 0:n], zt[:, 0:n], var[:, 0:n], ALU.mult, [Rzt, Rmv], [Rzt])
                            self.ts(V, zt[:, 0:n], zt[:, 0:n], self.vecs[:, vo["lng"] + ch:vo["lng"] + ch + 1],
                                    self.vecs[:, vo["lnb"] + ch:vo["lnb"] + ch + 1], ALU.mult, ALU.add, [Rzt, self.Rvec], [Rzt])
                            self.act(mixT[:, ch, c0:c0 + n], zt[:, 0:n], AF.Silu, [Rzt], [Rmix[ch][t] for t in range(t0, t0 + nt)])
                    em.flush()

    def phase_ret(self, hp, last):
        c, em = self.c, self.em
        QG, GT = c.QG, c.GT
        with contextlib.ExitStack() as ph:
            NQ = c.TH + c.CTX
            kT = self.T(ph, [128, c.NK], BF16, "kT")
            qT = self.T(ph, [128, NQ], BF16, "qT")
            vv = self.T(ph, [128, c.NKT, 128], BF16, "vv")
            sgt = self.T(ph, [128, c.NT + c.NC, 128], BF16, "sgt")
            RkT = [Res() for _ in range(c.NKT)]
            RqT = [Res() for _ in range(c.NT + c.NC)]
            Rvv = [Res() for _ in range(c.NKT)]
            Rsgt = [Res() for _ in range(c.NT + c.NC)]
            Ed = self.T(ph, [128, GT, 2, QG], BF16, "Ed")
            nF, nB, nO = self.nF, self.nB, self.nO
            NE = nF + nB + nO
            TQ = self.T(ph, [128, 3, 2, QG], BF16, "TQ")
            LC = self.T(ph, [128, 2, NE], F32, "LC")
            rt = self.T(ph, [128, QG], F32, "rt")
            RE = Res()
            with contextlib.ExitStack() as sp:
                dt_ = self.T(sp, [128, GT, QG], F32, "dtab")
                cx = self.T(sp, [128, NE], F32, "cx")
                Rt = Res()
                em.dma(SP, dt_[:], self.d_dtab, writes=[Rt], semkey="tab")
                em.dma(SP, cx[:], self.cur_cx, writes=[Rt], semkey="tab")
                em.dma(SP, rt[:], self.d_rowtab, writes=[RE], semkey="tab2")
                dp = self.T(sp, [128, QG], F32, "dp")
                dn = self.T(sp, [128, QG], F32, "dn")
                ind = self.T(sp, [128, QG], F32, "ind")
                t1 = self.T(sp, [128, QG], F32, "t1")
                t2 = self.T(sp, [128, QG], F32, "t2")
                Rd = Res()
                lg, lgx = self.lg, self.lgx
                for hh in range(2):
                    h = 2 * hp + hh
                    self.act(TQ[:, 0, hh, :], rt[:], AF.Exp, [RE, self.Rvec], [RE], scale=lg[:, h:h + 1])
                    self.act(TQ[:, 1, hh, :], rt[:], AF.Exp, [RE, self.Rvec], [RE], scale=lgx[:, h:h + 1])
                    self.act(TQ[:, 2, hh, :], rt[:], AF.Exp, [RE, self.Rvec], [RE], scale=lgx[:, 8 + h:9 + h])
                    self.ts(V, LC[:, hh, 0:nF], cx[:, 0:nF], lg[:, h:h + 1], None, ALU.mult, None, [Rt, self.Rvec], [RE])
                    self.ts(V, LC[:, hh, nF:nF + nB], cx[:, nF:nF + nB], lg[:, 4 + h:5 + h], None, ALU.mult, None, [Rt, self.Rvec], [RE])
                    self.ts(V, LC[:, hh, nF + nB:], cx[:, nF + nB:], lgx[:, 4 + h:5 + h], None, ALU.mult, None, [Rt, self.Rvec], [RE])
                for j in range(GT):
                    self.ts(V, dp[:], dt_[:, j, :], 0.0, None, ALU.max, None, [Rt], [Rd])
                    self.ts(V, dn[:], dt_[:, j, :], 0.0, None, ALU.min, None, [Rt], [Rd])
                    self.ts(V, ind[:], dt_[:, j, :], 0.0, None, ALU.is_ge, None, [Rt], [Rd])
                    for hh in range(2):
                        h = 2 * hp + hh
                        self.act(t1[:], dp[:], AF.Exp, [Rd, self.Rvec], [Rd], scale=lg[:, h:h + 1])
                        self.act(t2[:], dn[:], AF.Exp, [Rd, self.Rvec], [Rd], scale=lgx[:, h:h + 1])
                        self.tt(V, t1[:], t1[:], t2[:], ALU.subtract, [Rd], [Rd])
                        self.tt(V, t1[:], t1[:], ind[:], ALU.mult, [Rd], [Rd])
                        self.tt(V, Ed[:, j, hh, :], t1[:], t2[:], ALU.add, [Rd], [RE])
                em.flush()
            with contextlib.ExitStack() as sp:
                wts = self.load_win(sp, [4 + hp, 6 + hp, 8 + hp, 10 + hp, 12 + hp, 14 + hp])
                hT = [self.T(sp, [128, c.KC, QG], BF16, "hT") for _ in range(2)]
                RhT = [Res(), Res()]
                rope = [self.T(sp, [128, 2, QG], F32, "rope") for _ in range(2)]
                Rrope = [Res(), Res()]
                pp = [self.P(sp, [128, 512], F32, "pp") for _ in range(4)]
                Rpp = [Res() for _ in range(4)]
                ptm = [self.P(sp, [128, 512], F32, "ptm") for _ in range(2)]
                Rptm = [Res(), Res()]
                ta = self.T(sp, [128, QG], F32, "ta")
                tb = self.T(sp, [128, QG], F32, "tb")
                Rta, Rtb = Res(), Res()
                gi = 0
                for seg in ("ctx", "oth", "own"):
                    kb = self.seg_kbase(seg)
                    for (t0, nt) in self.seg_groups(seg):
                        b = gi % 2
                        gi += 1
                        n = nt * 128
                        k0 = kb + t0 * 128
                        em.dma(SP, rope[b][:, :, 0:n], self.cur_rope[:, 0:2, k0:k0 + n], writes=[Rrope[b]], semkey=f"rope{b}")
                        self.load_h(hT[b], RhT[b], seg, t0, nt, f"hld{b}")
                        need_q = (seg == "own") or (seg == "ctx" and not last)
                        jobs = [(2, 3, 0.125, kT, k0, [RkT[kb // 128 + t0 + i] for i in range(nt)])]
                        if need_q:
                            qc0 = (0 if seg == "ctx" else c.CTX) + t0 * 128
                            qt0 = (0 if seg == "ctx" else c.NC) + t0
                            jobs.append((0, 1, 1.0, qT, qc0, [RqT[qt0 + i] for i in range(nt)]))
                        for (wa, wb_, scl, dstT, d0, Rd_) in jobs:
                            pa, Rpa = pp[(wa) % 4], Rpp[(wa) % 4]
                            pb, Rpb = pp[(wb_) % 4], Rpp[(wb_) % 4]
                            self.proj_fm(pa, Rpa, wts[[4, 6, 8, 10][wa] + hp], hT[b], RhT[b], 0, n)
                            self.proj_fm(pb, Rpb, wts[[4, 6, 8, 10][wb_] + hp], hT[b], RhT[b], 0, n)
                            self.stt(ta[:, 0:n], pa[:, 0:n], scl, rope[b][:, 0, 0:n], ALU.mult, ALU.mult, [Rpa, Rrope[b]], [Rta])
                            self.stt(tb[:, 0:n], pb[:, 0:n], scl, rope[b][:, 1, 0:n], ALU.mult, ALU.mult, [Rpb, Rrope[b]], [Rtb])
                            self.tt(V, dstT[:, d0:d0 + n], ta[:, 0:n], tb[:, 0:n], ALU.add, [Rta, Rtb], Rd_)
                        pv, Rpv = ptm[0], Rptm[0]
                        for i in range(nt):
                            self.proj_tm(pv, Rpv, wts[12 + hp], hT[b], RhT[b], i * 128, o0=i * 128)
                        kt0 = kb // 128 + t0
                        self.cp(V, vv[:, kt0:kt0 + nt, :], pv[:, 0:n].rearrange("p (t d) -> p t d", d=128), [Rpv],
                                [Rvv[kt0 + i] for i in range(nt)])
                        if need_q:
                            pg_, Rpg_ = ptm[1], Rptm[1]
                            for i in range(nt):
                                self.proj_tm(pg_, Rpg_, wts[14 + hp], hT[b], RhT[b], i * 128, o0=i * 128)
                            self.act(sgt[:, qt0:qt0 + nt, :], pg_[:, 0:n].rearrange("p (t d) -> p t d", d=128), AF.Silu,
                                     [Rpg_], [Rsgt[qt0 + i] for i in range(nt)])
                em.flush()
            with contextlib.ExitStack() as sp:
                NS_ = 3
                pst = [self.P(sp, [128, 512], F32, "pst") for _ in range(NS_)]
                Rpst = [Res() for _ in range(NS_)]
                po = [self.P(sp, [128, 512], F32, "po") for _ in range(GT)]
                Rpo = [Res() for _ in range(GT)]
                ptr = self.P(sp, [128, 128], BF16, "ptr")
                Rptr = Res()
                NP_ = 4
                pb_ = [self.T(sp, [128, QG], BF16, "pbuf") for _ in range(NP_)]
                Rpb = [Res() for _ in range(NP_)]
                NW_ = 4
                Wt = [self.T(sp, [128, 128], F32, "Wt") for _ in range(NW_)]
                RWt = [Res() for _ in range(NW_)]
                kw = [self.T(sp, [128, 128], BF16, "kw") for _ in range(NW_)]
                Rkw = [Res() for _ in range(NW_)]
                qk = [self.T(sp, [128, 3, QG], BF16, "qk") for _ in range(2)]
                Rqk = [Res(), Res()]
                blk = 0
                lg, lgx = self.lg, self.lgx
                rows = self.T(sp, [128, 256], F32, "rows")
                Rrows = Res()
                em.dma(SP, rows[:], self.W["rows"], writes=[Rrows], semkey="rows")
                bst = self.T(sp, [128, 8], F32, "bst")
                bag = self.T(sp, [128, 4], F32, "bag")
                Rb = Res()
                yt = self.T(sp, [128, 64], F32, "yt")
                Ryt = Res()
                yret = [self.T(sp, [128, 128], BF16, "yret") for _ in range(max(GT, c.NC))]
                Ryret = [Res() for _ in range(max(GT, c.NC))]
                cnt = 0
                qsegs = [("own", self.ktl_lat)] + ([] if last else [("ctx", [self.ktl_ctx])])
                ti = 0
                for seg, ktls in qsegs:
                    for gidx, ktl in enumerate(ktls):
                        nt = GT if seg == "own" else c.NC
                        n = nt * 128
                        qc0 = (c.CTX + gidx * QG) if seg == "own" else 0
                        qt0 = (c.NC + gidx * GT) if seg == "own" else 0
                        for hh in range(2):
                            h = 2 * hp + hh
                            r0 = hh * 64
                            base = cnt
                            cnt += len(ktl)
                            qb_ = blk % 2
                            blk += 1
                            for kk in range(3):
                                self.tt(V, qk[qb_][r0:r0 + 64, kk, 0:n], qT[r0:r0 + 64, qc0:qc0 + n], TQ[r0:r0 + 64, kk, hh, 0:n],
                                        ALU.mult, [RqT[qt0 + i] for i in range(nt)] + [RE], [Rqk[qb_]])

                            def st_mm(idx):
                                kt, kind, e = ktl[idx]
                                sb = (base + idx) % NS_
                                if kind == "D":
                                    self.mm(pst[sb][:, 0:n], kT[r0:r0 + 64, kt * 128:(kt + 1) * 128], qT[r0:r0 + 64, qc0:qc0 + n],
                                            True, True, [RkT[kt]] + [RqT[qt0 + i] for i in range(nt)], [Rpst[sb]])
                                    return
                                wb_ = (base + idx) % NW_
                                kk = {"F": 0, "B": 1, "O": 2}[kind]
                                eo = {"F": 0, "B": nF, "O": nF + nB}[kind] + e
                                ksc = {"F": lgx[:, 16 + h:17 + h], "B": lg[:, 4 + h:5 + h], "O": lgx[:, 20 + h:21 + h]}[kind]
                                self.act(Wt[wb_][r0:r0 + 64, :], rt[r0:r0 + 64, 0:128], AF.Exp, [RE, self.Rvec], [RWt[wb_]],
                                         scale=ksc[r0:r0 + 64, :], bias=LC[r0:r0 + 64, hh, eo:eo + 1])
                                self.tt(V, kw[wb_][r0:r0 + 64, :], kT[r0:r0 + 64, kt * 128:(kt + 1) * 128], Wt[wb_][r0:r0 + 64, :],
                                        ALU.mult, [RkT[kt], RWt[wb_]], [Rkw[wb_]])
                                self.mm(pst[sb][:, 0:n], kw[wb_][r0:r0 + 64, :], qk[qb_][r0:r0 + 64, kk, 0:n], True, True,
                                        [Rkw[wb_], Rqk[qb_]], [Rpst[sb]])
                            st_mm(0)
                            if len(ktl) > 1:
                                st_mm(1)
                            for idx, (kt, kind, e) in enumerate(ktl):
                                sb = (base + idx) % NS_
                                pbi = (base + idx) % NP_
                                if idx + 2 < len(ktl):
                                    st_mm(idx + 2)
                                if kind == "D":
                                    self.tt(V, pb_[pbi][:, 0:n], pst[sb][:, 0:n], Ed[:, e, hh, 0:n], ALU.mult, [Rpst[sb], RE], [Rpb[pbi]])
                                elif idx % 2 == 0:
                                    self.cp(S, pb_[pbi][:, 0:n], pst[sb][:, 0:n], [Rpst[sb]], [Rpb[pbi]])
                                else:
                                    self.cp(V, pb_[pbi][:, 0:n], pst[sb][:, 0:n], [Rpst[sb]], [Rpb[pbi]])
                                for i in range(nt):
                                    self.mm(po[i][:, 0:64], pb_[pbi][:, i * 128:(i + 1) * 128], vv[:, kt, r0:r0 + 64],
                                            idx == 0, idx == len(ktl) - 1, [Rpb[pbi], Rvv[kt]], [Rpo[i]])
                            for i in range(nt):
                                yb = i
                                self.em.op(V, lambda e_, i=i: e_.bn_stats(out=bst[:, 0:6], in_=po[i][:, 0:64]), [Rpo[i]], [Rb])
                                self.em.op(V, lambda e_: e_.bn_aggr(out=bag[:, 0:2], in_=bst[:, 0:6]), [Rb], [Rb])
                                self.ts(V, bag[:, 2:3], bag[:, 1:2], EPS, None, ALU.add, None, [Rb], [Rb])
                                self.rsqrt_(bag[:, 3:4], bag[:, 2:3], 1, [Rb], [Rb])
                                self.ts(V, yt[:], po[i][:, 0:64], bag[:, 0:1], bag[:, 3:4], ALU.subtract, ALU.mult,
                                        [Rpo[i], Rb], [Ryt])
                                self.tt(V, yt[:], yt[:], rows[:, h * 64:(h + 1) * 64], ALU.mult, [Ryt, Rrows], [Ryt])
                                self.tt(V, yret[yb][:, r0:r0 + 64], yt[:], sgt[:, qt0 + i, r0:r0 + 64], ALU.mult,
                                        [Ryt, Rsgt[qt0 + i]], [Ryret[yb]])
                                if hh == 1:
                                    self.tr(ptr[:], yret[yb][:], self.ident[:], [Ryret[yb], self.Rc], [Rptr])
                                    if seg == "own":
                                        tcol = (gidx * GT + i) * 128
                                        self.cp(S, self.mixT[:, 2 + hp, tcol:tcol + 128], ptr[:], [Rptr],
                                                [self.Rmix[2 + hp][gidx * GT + i]])
                                    else:
                                        self.cp(S, self.mixTc[:, 2 + hp, i * 128:(i + 1) * 128], ptr[:], [Rptr],
                                                [self.Rmixc[2 + hp][i]])
                em.flush()

    def phase_mla(self, last):
        c, em = self.c, self.em
        QG, GT = c.QG, c.GT
        vo = c.vo
        with contextlib.ExitStack() as ph:
            NQ = c.TH + c.CTX
            ckvT = self.T(ph, [128, c.NK], BF16, "ckvT")
            Rckv = [Res() for _ in range(c.NKT)]
            KT = self.T(ph, [128, c.NK], BF16, "KT")
            RKTn = Res()
            RKTr = [Res() for _ in range(c.NKT)]
            cqT = self.T(ph, [128, 2, NQ], BF16, "cqT")
            Rcq = [Res() for _ in range(c.NT + c.NC)]
            with contextlib.ExitStack() as sp:
                wts = self.load_win(sp, [16, 17, 18, 19, 20])
                hTs = [self.T(sp, [128, c.KC, QG], BF16, "hT") for _ in range(2)]
                RhTs = [Res(), Res()]
                ropes = [self.T(sp, [128, 2, QG], F32, "rope") for _ in range(2)]
                Rropes = [Res(), Res()]
                gi = 0
                pp = [self.P(sp, [128, 512], F32, "pp") for _ in range(4)]
                Rpp = [Res() for _ in range(4)]
                pss = [self.P(sp, [128, 512], F32, "pss") for _ in range(2)]
                Rpss = [Res(), Res()]
                sq = self.T(sp, [128, 2, QG], BF16, "sq")
                Rsq = Res()
                rs = self.T(sp, [128, QG], F32, "rs")
                Rrs = Res()
                ta = self.T(sp, [128, QG], F32, "ta")
                tb = self.T(sp, [128, QG], F32, "tb")
                Rta, Rtb = Res(), Res()
                for seg in ("ctx", "oth", "own"):
                    kb = self.seg_kbase(seg)
                    for (t0, nt) in self.seg_groups(seg):
                        n = nt * 128
                        k0 = kb + t0 * 128
                        kt0 = k0 // 128
                        b = gi % 2
                        gi += 1
                        hT, RhT, rope, Rrope = hTs[b], RhTs[b], ropes[b], Rropes[b]
                        em.dma(SP, rope[64:96, :, 0:n], self.cur_rope[64:96, 2:4, k0:k0 + n], writes=[Rrope], semkey=f"rope{b}")
                        self.load_h(hT, RhT, seg, t0, nt, f"hld{b}")
                        pa, Rpa = pp[0], Rpp[0]
                        self.proj_fm(pa, Rpa, wts[18], hT, RhT, 0, n)
                        self.act(sq[:, 0, 0:n], pa[:, 0:n], AF.Square, [Rpa], [Rsq])
                        self.mm(pss[0][:, 0:n], self.onesb[:], sq[:, 0, 0:n], True, True, [Rsq, self.Rc], [Rpss[0]])
                        self.ts(V, rs[:, 0:n], pss[0][:, 0:n], 1.0 / 128, EPS, ALU.mult, ALU.add, [Rpss[0]], [Rrs])
                        self.rsqrt_(rs[:, 0:n], rs[:, 0:n], n, [Rrs], [Rrs])
                        self.stt(ckvT[:, k0:k0 + n], pa[:, 0:n], self.vecs[:, vo["kvg"]:vo["kvg"] + 1], rs[:, 0:n], ALU.mult,
                                 ALU.mult, [Rpa, self.Rvec, Rrs], [Rckv[kt0 + i] for i in range(nt)])
                        pa, Rpa = pp[1], Rpp[1]
                        pb, Rpb = pp[2], Rpp[2]
                        self.proj_fm(pa, Rpa, wts[19], hT, RhT, 0, n, M=96)
                        self.proj_fm(pb, Rpb, wts[20], hT, RhT, 0, n, M=96)
                        self.tt(V, ta[64:96, 0:n], pa[64:96, 0:n], rope[64:96, 0, 0:n], ALU.mult, [Rpa, Rrope], [Rta])
                        self.tt(V, tb[64:96, 0:n], pb[64:96, 0:n], rope[64:96, 1, 0:n], ALU.mult, [Rpb, Rrope], [Rtb])
                        self.tt(V, KT[64:96, k0:k0 + n], ta[64:96, 0:n], tb[64:96, 0:n], ALU.add, [Rta, Rtb],
                                [RKTr[kt0 + i] for i in range(nt)])
                        need_q = (seg == "own") or (seg == "ctx" and not last)
                        if need_q:
                            qc0 = (0 if seg == "ctx" else c.CTX) + t0 * 128
                            qt0 = (0 if seg == "ctx" else c.NC) + t0
                            pq = [pp[3], pp[0]]
                            Rpq = [Rpp[3], Rpp[0]]
                            for ch in range(2):
                                self.proj_fm(pq[ch], Rpq[ch], wts[16 + ch], hT, RhT, 0, n)
                                self.act(sq[:, ch, 0:n], pq[ch][:, 0:n], AF.Square, [Rpq[ch]], [Rsq])
                            for ch in range(2):
                                self.mm(pss[1][:, 0:n], self.onesb[:], sq[:, ch, 0:n], ch == 0, ch == 1, [Rsq, self.Rc], [Rpss[1]])
                            self.ts(V, rs[:, 0:n], pss[1][:, 0:n], 1.0 / 256, EPS, ALU.mult, ALU.add, [Rpss[1]], [Rrs])
                            self.rsqrt_(rs[:, 0:n], rs[:, 0:n], n, [Rrs], [Rrs])
                            for ch in range(2):
                                self.stt(cqT[:, ch, qc0:qc0 + n], pq[ch][:, 0:n], self.vecs[:, vo["qg"] + ch:vo["qg"] + ch + 1],
                                         rs[:, 0:n], ALU.mult, ALU.mult, [Rpq[ch], self.Rvec, Rrs],
                                         [Rcq[qt0 + i] for i in range(nt)])
                em.flush()
            with contextlib.ExitStack() as at:
                QT = self.T(at, [128, NQ], BF16, "QT")
                RQT = [Res() for _ in range(c.NT + c.NC)]
                Vh = self.T(at, [128, c.NKT, 65], BF16, "Vh")
                RVh = Res()
                mrope = self.T(at, [128, 2, NQ], F32, "mrope")
                Rmr = Res()
                em.dma_group(SP, [(mrope[64:96, :, 0:c.CTX], self.cur_rope[64:96, 2:4, 0:c.CTX]),
                                  (mrope[64:96, :, c.CTX:NQ], self.cur_rope[64:96, 2:4, c.CTX + c.TH:c.NK])],
                             writes=[Rmr], semkey="mrope")
                Otok = self.T(at, [128, c.NT + c.NC, 512], BF16, "Otok")
                ROt = [Res() for _ in range(c.NT + c.NC)]
                wuq = self.T(at, [128, 2, 2048], BF16, "wuq")
                wukv = self.T(at, [128, 1024], BF16, "wukv")
                Rwu = Res()
                em.dma_group(G, [(wuq[:, k, :], self.W["wuq"][k]) for k in range(2)] + [(wukv[:], self.W["wukv"])],
                             writes=[Rwu], semkey="wuq")
                em.op(G, lambda e: e.memset(Vh[:, :, 64:65], 1.0), writes=[RVh])
                with contextlib.ExitStack() as sp:
                    pstp = [self.P(sp, [128, 1024], F32, "pstp") for _ in range(2)]
                    Rpst = [Res(), Res()]
                    po = [self.P(sp, [128, 512], F32, "po") for _ in range(GT)]
                    Rpo = [Res() for _ in range(GT)]
                    Rpw = Rpst
                    NP_ = 3
                    pb_ = [self.T(sp, [128, 2, 512], BF16, "pbuf") for _ in range(NP_)]
                    Rpb = [Res() for _ in range(NP_)]
                    ta = self.T(sp, [128, QG], F32, "ta")
                    tb = self.T(sp, [128, QG], F32, "tb")
                    Rta, Rtb = Res(), Res()
                    rin = self.T(sp, [128, 2], F32, "rin")
                    Rrin = Res()
                    scale = 96.0 ** -0.5
                    cnt = 0
                    wcnt = 0
                    qsegs = [("own", c.NG, list(range(c.NKT)))] + ([] if last else [("ctx", 1, list(range(c.NC)))])
                    for h in range(8):
                        for k0 in range(0, c.NK, 512):
                            n = min(512, c.NK - k0)
                            w_ = wcnt % 2
                            wcnt += 1
                            self.mm(pstp[w_][0:64, 0:n], wukv[:, h * 128:h * 128 + 64], ckvT[:, k0:k0 + n], True, True,
                                    [Rwu] + [Rckv[k0 // 128 + i] for i in range(n // 128)], [Rpw[w_]])
                            self.cp(S if (wcnt % 2) else V, KT[0:64, k0:k0 + n], pstp[w_][0:64, 0:n], [Rpw[w_]], [RKTn])
                        for kt0 in range(0, c.NKT, 8):
                            nt8 = min(8, c.NKT - kt0)
                            w_ = wcnt % 2
                            wcnt += 1
                            for i in range(nt8):
                                kt = kt0 + i
                                self.mm(pstp[w_][:, i * 64:(i + 1) * 64], ckvT[:, kt * 128:(kt + 1) * 128],
                                        wukv[:, h * 128 + 64:(h + 1) * 128], True, True, [Rwu, Rckv[kt]], [Rpw[w_]])
                            self.cp(V, Vh[:, kt0:kt0 + nt8, 0:64], pstp[w_][:, 0:nt8 * 64].rearrange("p (t d) -> p t d", d=64),
                                    [Rpw[w_]], [RVh])
                        for seg, ng, keys in qsegs:
                            for gidx in range(ng):
                                nt = GT if seg == "own" else c.NC
                                n = nt * 128
                                qc0 = (c.CTX + gidx * QG) if seg == "own" else 0
                                qt0 = (c.NC + gidx * GT) if seg == "own" else 0
                                Rq_ = [RQT[qt0 + i] for i in range(nt)]
                                pa, Rpa = pstp[0], Rpw[0]
                                pb2, Rpb2 = pstp[1], Rpw[1]
                                for (pq, Rpq, s_) in ((pa, Rpa, 0), (pb2, Rpb2, 1)):
                                    c0 = (h * 2 + s_) * 128
                                    for k in range(2):
                                        self.mm(pq[0:96, 0:n], wuq[:, k, c0:c0 + 96], cqT[:, k, qc0:qc0 + n], k == 0, k == 1,
                                                [Rwu] + [Rcq[qt0 + i] for i in range(nt)], [Rpq])
                                self.cp(S, QT[0:64, qc0:qc0 + n], pa[0:64, 0:n], [Rpa], Rq_)
                                self.tt(V, ta[64:96, 0:n], pa[64:96, 0:n], mrope[64:96, 0, qc0:qc0 + n], ALU.mult, [Rpa, Rmr], [Rta])
                                self.tt(V, tb[64:96, 0:n], pb2[64:96, 0:n], mrope[64:96, 1, qc0:qc0 + n], ALU.mult, [Rpb2, Rmr], [Rtb])
                                self.tt(V, QT[64:96, qc0:qc0 + n], ta[64:96, 0:n], tb[64:96, 0:n], ALU.add, [Rta, Rtb], Rq_)
                        for seg, ng, keys in qsegs:
                            for gidx in range(ng):
                                nt = GT if seg == "own" else c.NC
                                n = nt * 128
                                qc0 = (c.CTX + gidx * QG) if seg == "own" else 0
                                qt0 = (c.NC + gidx * GT) if seg == "own" else 0
                                Rq_ = [RQT[qt0 + i] for i in range(nt)]
                                pairs = [keys[i:i + 2] for i in range(0, len(keys), 2)]
                                base = cnt
                                cnt += len(pairs)

                                def st_pair(pi):
                                    sb = (base + pi) % 2
                                    for rep in range(1 + DUP_ST_MLA):
                                        for j, kt in enumerate(pairs[pi]):
                                            self.mm(pstp[sb][:, j * 512:j * 512 + n], KT[0:96, kt * 128:(kt + 1) * 128],
                                                    QT[0:96, qc0:qc0 + n], True, True, [RKTn, RKTr[kt]] + Rq_, [Rpst[sb]])
                                st_pair(0)
                                if len(pairs) > 1:
                                    st_pair(1)
                                for pi, pr in enumerate(pairs):
                                    sb = (base + pi) % 2
                                    pbi = (base + pi) % NP_
                                    np_ = len(pr)
                                    self.act(pb_[pbi][:, 0:np_, 0:n],
                                             pstp[sb][:, :].rearrange("p (j c) -> p j c", j=2)[:, 0:np_, 0:n], AF.Exp,
                                             [Rpst[sb]], [Rpb[pbi]], scale=scale)
                                    if pi + 2 < len(pairs):
                                        st_pair(pi + 2)
                                    for j, kt in enumerate(pr):
                                        first = (pi == 0 and j == 0)
                                        lastk = (pi == len(pairs) - 1 and j == np_ - 1)
                                        for i in range(nt):
                                            self.mm(po[i][:, 0:65], pb_[pbi][:, j, i * 128:(i + 1) * 128], Vh[:, kt, 0:65],
                                                    first, lastk, [Rpb[pbi], RVh], [Rpo[i]])
                                for i in range(nt):
                                    self.em.op(V, lambda e_, i=i: e_.reciprocal(out=rin[:, 0:1], in_=po[i][:, 64:65]), [Rpo[i]], [Rrin])
                                    self.ts(V, Otok[:, qt0 + i, h * 64:(h + 1) * 64], po[i][:, 0:64], rin[:, 0:1], None, ALU.mult,
                                            None, [Rpo[i], Rrin], [ROt[qt0 + i]])
                    em.flush()
                with contextlib.ExitStack() as sp:
                    ptb = [self.P(sp, [128, 4, 128], BF16, "ptr") for _ in range(2)]
                    Rptb = [Res(), Res()]
                    for ti in range(c.NT + (0 if last else c.NC)):
                        b = ti % 2
                        tq = (ti + c.NC) if ti < c.NT else (ti - c.NT)
                        for q in range(4):
                            self.tr(ptb[b][:, q, :], Otok[:, tq, q * 128:(q + 1) * 128], self.ident[:], [ROt[tq], self.Rc], [Rptb[b]])
                        if ti < c.NT:
                            self.cp(S if ti % 2 else V, self.mixT[:, 4:8, ti * 128:(ti + 1) * 128], ptb[b][:], [Rptb[b]],
                                    [self.Rmix[4 + q][ti] for q in range(4)])
                        else:
                            tc_ = ti - c.NT
                            self.cp(S if ti % 2 else V, self.mixTc[:, 4:8, tc_ * 128:(tc_ + 1) * 128], ptb[b][:], [Rptb[b]],
                                    [self.Rmixc[4 + q][tc_] for q in range(4)])
                    em.flush()

    def resid_update(self, sp, py, Rpy, xt, Rxt, gbc, Rgbc, st, Rst, junk, Rjunk, tmp, Rtmp):
        c = self.c
        hw = c.D // 2
        for j in range(2):
            self.act(junk[:, 0:hw], py[j][:, 0:hw], AF.Square, [Rpy[j]], [Rjunk, Rst], accum_out=st[:, j:j + 1])
        self.tt(V, st[:, 2:3], st[:, 0:1], st[:, 1:2], ALU.add, [Rst], [Rst])
        self.ts(V, st[:, 2:3], st[:, 2:3], 1.0 / c.D, EPS, ALU.mult, ALU.add, [Rst], [Rst])
        self.rsqrt_(st[:, 3:4], st[:, 2:3], 1, [Rst], [Rst])
        for j in range(2):
            self.stt(tmp[:, 0:hw], py[j][:, 0:hw], st[:, 3:4], gbc[:, j * hw:(j + 1) * hw], ALU.mult, ALU.mult,
                     [Rpy[j], Rst, Rgbc], [Rtmp])
            self.tt(V, xt[:, j * hw:(j + 1) * hw], xt[:, j * hw:(j + 1) * hw], tmp[:, 0:hw], ALU.add, [Rxt, Rtmp], [Rxt])

    def phase_out(self, last):
        c, em = self.c, self.em
        hw = c.D // 2
        with contextlib.ExitStack() as sp:
            wout = self.T(sp, [128, 8, c.D], BF16, "wout")
            Rw = Res()
            em.dma_group(G, [(wout[:, k, :], self.W["wout"][k]) for k in range(8)], writes=[Rw], semkey="wout")
            pbank = self.P(sp, [128, 512], F32, "pbk")
            Rpbk = Res()
            py = [[self.P(sp, [128, 512], F32, "py") for _ in range(2)] for _ in range(2)]
            Rpy = [[Res(), Res()], [Res(), Res()]]
            st = self.T(sp, [128, 4], F32, "st")
            Rst = Res()
            junk = self.T(sp, [128, 512], BF16, "junk")
            Rjunk = Res()
            tmp = self.T(sp, [128, 512], F32, "tmp")
            Rtmp = Res()
            segs = [("own", 0, c.NT)] + ([] if last else [("ctx", 1, c.NC)])
            ti = 0
            for seg, s_, ntile in segs:
                gbc, Rgbc = self.bcast_vec(sp, 2, s_, pbank, Rpbk)
                mixT, Rmix = (self.mixT, self.Rmix) if seg == "own" else (self.mixTc, self.Rmixc)
                for t in range(ntile):
                    b = ti % 2
                    ti += 1
                    for j in range(2):
                        for k in range(8):
                            self.mm(py[b][j][:, 0:hw], mixT[:, k, t * 128:(t + 1) * 128], wout[:, k, j * hw:(j + 1) * hw],
                                    k == 0, k == 7, [Rmix[k][t], Rw], [Rpy[b][j]])
                    xt, Rxt = (self.x[:, t, :], self.Rx[t]) if seg == "own" else (self.xc[:, t, :], self.Rxc[t])
                    self.resid_update(sp, py[b], Rpy[b], xt, Rxt, gbc, Rgbc, st, Rst, junk, Rjunk, tmp, Rtmp)
            em.flush()

    def phase_ffn(self, last, post_tile=None):
        c, em = self.c, self.em
        hw = c.D // 2
        QG = c.QG
        with contextlib.ExitStack() as sp:
            w2 = self.T(sp, [128, c.FC, c.D], BF16, "w2")
            Rw2 = Res()
            em.dma_group(G, [(w2[:, f, :], self.W["w2"][f]) for f in range(c.FC)], writes=[Rw2], semkey="w2")
            NW = 3
            w1 = [self.T(sp, [128, c.KC, 256], BF16, "w1") for _ in range(NW)]
            Rw1 = [Res() for _ in range(NW)]
            nctx = self.make_norm_ctx(sp)
            hT = self.T(sp, [128, c.KC, QG], BF16, "hT")
            RhT = Res()
            hid = self.T(sp, [128, c.FC, QG], BF16, "hid")
            Rhid = [Res() for _ in range(c.FC)]
            pu = [self.P(sp, [128, 512], F32, "pu") for _ in range(2)]
            Rpu = [Res(), Res()]
            pgs = [self.P(sp, [128, 512], F32, "pg") for _ in range(2)]
            Rpgs = [Res(), Res()]
            pg, Rpg = pgs[0], Rpgs[0]
            py = [self.P(sp, [128, 512], F32, "py") for _ in range(2)]
            Rpy = [Res(), Res()]
            sgt = [self.T(sp, [128, QG], BF16, "sg") for _ in range(2)]
            Rsg = [Res(), Res()]
            st = self.T(sp, [128, 4], F32, "st")
            Rst = Res()
            junk = self.T(sp, [128, 512], BF16, "junk")
            Rjunk = Res()
            tmp = self.T(sp, [128, 512], F32, "tmp")
            Rtmp = Res()
            segs = [("own", 0)] + ([] if last else [("ctx", 1)])
            wi = 0
            hT2 = self.T(sp, [128, c.KC, QG], BF16, "hT2")
            hTs, RhTs = [hT, hT2], [RhT, Res()]
            glist = []
            for seg, s_ in segs:
                for (t0, nt) in self.seg_groups(seg):
                    glist.append((seg, s_, t0, nt))

            def emit_norm(gi):
                seg, s_, t0, nt = glist[gi]
                self.norm_items(nctx, [(seg, t0 + i, hTs[gi % 2], i * 128, RhTs[gi % 2], 3, None) for i in range(nt)])
            emit_norm(0)
            gbc = Rgbc = None
            cur_seg = None
            for gi, (seg, s_, t0, nt) in enumerate(glist):
                if seg != cur_seg:
                    gbc, Rgbc = self.bcast_vec(sp, 5, s_, pg, Rpg)
                    cur_seg = seg
                hTg, RhTg = hTs[gi % 2], RhTs[gi % 2]
                n = nt * 128
                for f in range(c.FC):
                    wb = wi % NW
                    wi += 1
                    self.load_w(w1[wb][:], self.W["w1"][f].rearrange("p (k n) -> p k n", k=c.KC), Rw1[wb], f"w1_{wb}")
                    ub = f % 2
                    for k in range(c.KC):
                        self.mm(pu[ub][:, 0:n], w1[wb][:, k, 0:128], hTg[:, k, 0:n], k == 0, k == c.KC - 1, [Rw1[wb], RhTg],
                                [Rpu[ub]])
                    for k in range(c.KC):
                        self.mm(pgs[ub][:, 0:n], w1[wb][:, k, 128:256], hTg[:, k, 0:n], k == 0, k == c.KC - 1, [Rw1[wb], RhTg],
                                [Rpgs[ub]])
                    self.act(sgt[ub][:, 0:n], pgs[ub][:, 0:n], AF.Silu, [Rpgs[ub]], [Rsg[ub]])
                    self.tt(V, hid[:, f, 0:n], pu[ub][:, 0:n], sgt[ub][:, 0:n], ALU.mult, [Rpu[ub], Rsg[ub]], [Rhid[f]])
                if gi + 1 < len(glist):
                    emit_norm(gi + 1)
                for i in range(nt):
                    t = t0 + i
                    for j in range(2):
                        for f in range(c.FC):
                            self.mm(py[j][:, 0:hw], hid[:, f, i * 128:(i + 1) * 128], w2[:, f, j * hw:(j + 1) * hw],
                                    f == 0, f == c.FC - 1, [Rhid[f], Rw2], [Rpy[j]])
                    xt, Rxt = (self.x[:, t, :], self.Rx[t]) if seg == "own" else (self.xc[:, t, :], self.Rxc[t])
                    self.resid_update(sp, py, Rpy, xt, Rxt, gbc, Rgbc, st, Rst, junk, Rjunk, tmp, Rtmp)
                    if post_tile is not None and seg == "own":
                        post_tile(t)
            em.flush()


_PROGRAMS = {}


def _get_program(cfg, layers, out_ctx, fused=False):
    key = (cfg.D, cfg.TH, cfg.CTX, cfg.DFF, cfg.GW, cfg.QG, tuple(layers), out_ctx, fused)
    if key not in _PROGRAMS:
        _PROGRAMS[key] = Builder(cfg, layers, out_ctx, fused).build()
    return _PROGRAMS[key]


def _core_inputs(cfg, inp, b, h, x_full, xc_full, layers):
    c = cfg
    d = _core_tables(c, h)
    d["x_own"] = np.ascontiguousarray(x_full[b, h * c.TH:(h + 1) * c.TH])
    d["x_oth"] = np.ascontiguousarray(x_full[b, (1 - h) * c.TH:(2 - h) * c.TH])
    d["xc"] = np.ascontiguousarray(xc_full[b])
    cc = np.stack([inp["c"][b], inp["c_ctx"]], -1)
    d["cc"] = np.ascontiguousarray(cc.reshape(c.KC, 128, 2).transpose(1, 0, 2))
    return d


def run_layers(cfg, inp, layer_groups, runner=None, fused=False):
    c = cfg
    inp = {k: np.asarray(v, dtype=np.float32) for k, v in inp.items()}
    x = inp["x"]
    xc = inp["ctx"]
    for layers in layer_groups:
        out_ctx = 0 if layers[-1] == c.DEPTH - 1 else 1
        nc = _get_program(c, layers, out_ctx, fused)
        wl = {}
        for l in layers:
            wl.update(_layer_weights(c, inp, l))
        in_maps = []
        for core in range(2 * c.BATCH):
            b, h = core // 2, core % 2
            d = _core_inputs(c, inp, b, h, x, xc, layers)
            if fused:
                tb = _core_tables(c, 1 - h)
                d["flB"], d["ropetabB"], d["cxB"] = tb["fl"], tb["ropetab"], tb["cx"]
            d.update(wl)
            in_maps.append(d)
        if runner is None:
            res = run_bass_kernel_spmd(nc, in_maps, core_ids=list(range(2 * c.BATCH))).results
        else:
            res = runner(nc, in_maps)
        xn = np.zeros_like(x)
        xcn = xc.copy()
        for core in range(2 * c.BATCH):
            b, h = core // 2, core % 2
            xn[b, h * c.TH:(h + 1) * c.TH] = res[core]["x_out"]
            if out_ctx and h == 0:
                xcn[b] = res[core]["xc_out"]
        x, xc = xn, xcn
    return x


def kernel(**inputs):
    cfg = Cfg()
    out = run_layers(cfg, inputs, [list(range(cfg.DEPTH))], fused=True)
    return np.asarray(out, dtype=np.float32)
```

```python
import contextlib
import math
import numpy as np
import concourse.bass as bass
import concourse.mybir as mybir
from concourse.bass_utils import run_bass_kernel_spmd

F32 = mybir.dt.float32
BF16 = mybir.dt.bfloat16
AF = mybir.ActivationFunctionType
ALU = mybir.AluOpType
ENGINES = ("tensor", "vector", "scalar", "gpsimd", "sync")
PE, V, S, G, SP = ENGINES
EPS = 1e-6
ROPE_BASE = 10000.0
DUP_ST_MLA = 0
DUP_ST_RET = 0
FAST_RSQRT = 0
RET_SPLIT = 0
CONV_K = 31
HALO = 15
NWC = 21


class Cfg:
    def __init__(s, D=1024, TH=2048, CTX=256, DFF=2816, GW=64, BATCH=4, DEPTH=2, QG=512):
        s.D, s.TH, s.CTX, s.DFF, s.GW, s.BATCH, s.DEPTH = D, TH, CTX, DFF, GW, BATCH, DEPTH
        s.KC, s.NT, s.NC, s.FC = D // 128, TH // 128, CTX // 128, DFF // 128
        s.T = 2 * TH
        s.QG = min(QG, TH)
        s.NG = TH // s.QG
        s.GT = s.QG // 128
        s.NK = CTX + 2 * TH
        s.NKT = s.NK // 128
        s.MG = 6 * D // 512
        o = 0
        s.vo = {}
        for name, n in (("pre1", s.KC), ("post1", s.KC), ("pre2", s.KC), ("post2", s.KC), ("modb", 6 * s.KC),
                        ("convb", 2), ("lng", 2), ("lnb", 2), ("convw", 2 * CONV_K), ("qg", 2), ("kvg", 1)):
            s.vo[name] = o
            o += n
        s.NV = o


class Res:
    __slots__ = ("name", "w", "r")

    def __init__(self, name=""):
        self.name = name
        self.w = None
        self.r = []


class Emit:
    def __init__(self, nc, stack):
        self.nc = nc
        self.stack = stack
        self.q = {e: [] for e in ENGINES}
        self.sems = {}
        self.count = {}
        self.seen = {e: {} for e in ENGINES}
        for e in ENGINES:
            self._mksem("E_" + e)
        self.nblk = 0
        self.phase_keys = set()

    def _mksem(self, key):
        if key not in self.sems:
            self.sems[key] = self.stack.enter_context(self.nc.semaphore("s_" + key))
            self.count[key] = 0
        return key

    def _wait(self, eng, ev):
        if ev is None:
            return
        key, val = ev
        if eng == PE and key == "E_tensor":
            return
        if self.seen[eng].get(key, 0) >= val:
            return
        self.seen[eng][key] = val
        sem = self.sems[key]
        self.q[eng].append(lambda e, sem=sem, val=val: e.wait_ge(sem, val))

    def _deps(self, eng, reads, writes):
        for r in reads:
            self._wait(eng, r.w)
        for w in writes:
            self._wait(eng, w.w)
            for ev in w.r:
                self._wait(eng, ev)

    def _commit(self, ev, reads, writes):
        for r in reads:
            r.r.append(ev)
        for w in writes:
            w.w = ev
            w.r = []

    def op(self, eng, fn, reads=(), writes=()):
        self._deps(eng, reads, writes)
        key = "E_" + eng
        self.count[key] += 1
        val = self.count[key]
        sem = self.sems[key]
        self.q[eng].append(lambda e, fn=fn, sem=sem: fn(e).then_inc(sem, 1))
        self._commit((key, val), reads, writes)

    def dma(self, queue, out, in_, reads=(), writes=(), semkey="d", **kw):
        self._deps(queue, reads, writes)
        key = self._mksem("D_" + semkey)
        self.phase_keys.add(key)
        self.count[key] += 16
        val = self.count[key]
        sem = self.sems[key]
        self.q[queue].append(
            lambda e, out=out, in_=in_, sem=sem, kw=kw: e.dma_start(out=out, in_=in_, **kw).then_inc(sem, 16))
        ev = (key, val)
        self._commit(ev, reads, writes)
        return ev

    def dma_group(self, queue, pairs, reads=(), writes=(), semkey="d", **kw):
        self._deps(queue, reads, writes)
        key = self._mksem("D_" + semkey)
        self.phase_keys.add(key)
        sem = self.sems[key]
        for (out, in_) in pairs:
            self.count[key] += 16
            self.q[queue].append(
                lambda e, out=out, in_=in_, sem=sem, kw=kw: e.dma_start(out=out, in_=in_, **kw).then_inc(sem, 16))
        ev = (key, self.count[key])
        self._commit(ev, reads, writes)
        return ev

    def wait_event(self, eng, ev):
        self._wait(eng, ev)

    def flush(self):
        nc = self.nc
        if not any(self.q[e] for e in ENGINES):
            return
        for key in sorted(self.phase_keys):
            self._wait(SP, (key, self.count[key]))
        self.phase_keys = set()
        with nc.Block() as block:
            for e in ENGINES:
                lst = self.q[e]
                if not lst:
                    continue

                def body(eng, lst=lst):
                    for f in lst:
                        f(eng)
                getattr(block, e)(body)
        self.q = {e: [] for e in ENGINES}
        self.nblk += 1


def _perm(n_head_dim):
    half = n_head_dim // 2
    q = half // 2
    p = np.zeros(n_head_dim, np.int64)
    for i in range(n_head_dim):
        j = i % half
        base = i - j
        p[i] = base + (j + q if j < q else j - q)
    return p


def _rope_tables(cfg, pos_list, hd):
    half = hd // 2
    f = half // 2
    n = len(pos_list)
    cos = np.ones((hd, n), np.float64)
    sin = np.zeros((hd, n), np.float64)
    pos = np.asarray(pos_list)
    lat = pos >= 0
    row = (pos // cfg.GW).astype(np.float64)
    col = (pos % cfg.GW).astype(np.float64)
    for i in range(hd):
        j = i % half
        part = i // half
        fi = j % f
        sign = -1.0 if j < f else 1.0
        inv = np.float32(ROPE_BASE) ** (-np.float32(fi) / np.float32(f))
        ang = (row if part == 0 else col).astype(np.float32) * np.float32(inv)
        cos[i, lat] = np.cos(ang[lat])
        sin[i, lat] = sign * np.sin(ang[lat])
    return cos.astype(np.float32), sin.astype(np.float32)


def _kt_lists(cfg):
    c = cfg
    lat = []
    nF = nB = nO = 0
    for g in range(c.NG):
        lst = []
        for j in range(c.NC):
            lst.append((j, "F", nF)); nF += 1
        for j in range(c.NC):
            lst.append((j, "B", nB)); nB += 1
        for i in range(c.NT):
            lst.append((c.NC + i, "O", nO)); nO += 1
        for i in range(c.NT):
            kt = c.NC + c.NT + i
            if i < g * c.GT:
                lst.append((kt, "F", nF)); nF += 1
            elif i >= (g + 1) * c.GT:
                lst.append((kt, "B", nB)); nB += 1
            else:
                lst.append((kt, "D", i - g * c.GT))
        lat.append(lst)
    ctxl = [(j, "D", j) for j in range(c.NC)]
    return lat, ctxl, nF, nB, nO


def _core_tables(cfg, h):
    c = cfg
    P0 = h * c.TH
    Q0 = (1 - h) * c.TH
    pos = [-1] * c.CTX + list(range(Q0, Q0 + c.TH)) + list(range(P0, P0 + c.TH))
    rc, rs = _rope_tables(c, pos, 64)
    mc, ms = _rope_tables(c, pos, 32)
    rope = np.zeros((128, 4, c.NK), np.float32)
    rope[:, 0] = np.concatenate([rc, rc], 0)
    rope[:, 1] = np.concatenate([rs, rs], 0)
    rope[:, 2] = 1.0
    rope[64:96, 2] = mc
    rope[64:96, 3] = ms
    s = np.arange(128)[:, None, None]
    j = np.arange(c.GT)[None, :, None]
    t = np.arange(c.QG)[None, None, :]
    dtab = (t - 128 * j - s).astype(np.float32)
    lat, ctxl, nF, nB, nO = _kt_lists(c)
    cx = np.zeros(nF + nB + nO, np.float32)
    for g in range(c.NG):
        gb = P0 + g * c.QG
        for (kt, kind, e) in lat[g]:
            if kt < c.NC:
                kbF = -c.CTX + 128 * kt
                kbB = c.T + 128 * kt
            elif kt < c.NC + c.NT:
                kbF = kbB = Q0 + 128 * (kt - c.NC)
            else:
                kbF = kbB = P0 + 128 * (kt - c.NC - c.NT)
            if kind == "F":
                cx[e] = gb - kbF
            elif kind == "B":
                cx[nF + e] = kbB - gb
            elif kind == "O":
                cx[nF + nB + e] = abs(gb - kbF)
    cxr = np.broadcast_to(cx[None, :], (128, cx.size)).copy()
    fl = np.zeros((128, 2), np.float32)
    fl[:, 0] = 1.0 if h == 1 else 0.0
    fl[:, 1] = 1.0 if h == 0 else 0.0
    return dict(ropetab=rope, dtab=dtab, cx=cxr, fl=fl)


def _fm(v):
    return np.ascontiguousarray(v.reshape(-1, 128).T)


def _layer_weights(cfg, inp, l):
    c = cfg
    w_in = inp["w_in"][l]
    p64, p32 = _perm(64), _perm(32)
    a = w_in[:, 0:512]
    q = w_in[:, 512:768]
    k = w_in[:, 768:1024]
    v = w_in[:, 1024:1280]
    g = w_in[:, 1280:1536]
    cq = w_in[:, 1536:1792]
    ckv = w_in[:, 1792:1920]
    kr = w_in[:, 1920:1952]
    hp = np.concatenate([h * 64 + p64 for h in range(4)])
    krc = np.concatenate([ckv[:, 0:64], kr, kr], 1)
    krs = np.concatenate([ckv[:, 0:64], kr[:, p32], kr[:, p32]], 1)
    cat = np.concatenate([a, q, q[:, hp], k, k[:, hp], v, g, cq, ckv, krc, krs], 1)
    assert cat.shape[1] == NWC * 128
    win = np.ascontiguousarray(cat.reshape(c.KC, 128, NWC, 128).transpose(2, 1, 0, 3)).reshape(NWC, 128, c.KC * 128)
    uq = inp["mla_w_uq"][l].reshape(256, 8, 96)
    uqs = uq.copy()
    uqs[:, :, 64:96] = uq[:, :, 64:96][:, :, p32]
    pad = np.zeros((256, 8, 32), np.float32) + uq[:, :, 0:32]
    uqc = np.stack([np.concatenate([uq, pad], 2), np.concatenate([uqs, pad], 2)], 2)
    wuq = np.ascontiguousarray(uqc.reshape(2, 128, 16 * 128))
    wukv = np.ascontiguousarray(inp["mla_w_ukv"][l])
    wout = np.ascontiguousarray(inp["w_out"][l].reshape(8, 128, c.D))
    f1 = inp["ffn_w_in"][l]
    u1, g1 = f1[:, :c.DFF], f1[:, c.DFF:]
    w1 = np.stack([u1.reshape(c.KC, 128, c.FC, 128), g1.reshape(c.KC, 128, c.FC, 128)], 3)
    w1 = np.ascontiguousarray(w1.transpose(2, 1, 0, 3, 4)).reshape(c.FC, 128, c.KC * 256)
    w2 = np.ascontiguousarray(inp["ffn_w_out"][l].reshape(c.FC, 128, c.D))
    mw = inp["mod_w"][l]
    modw = np.ascontiguousarray(mw.reshape(c.KC, 128, c.MG, 512).transpose(2, 1, 0, 3)).reshape(c.MG, 128, c.KC * 512)
    vecs = np.zeros((128, c.NV), np.float32)
    vo = c.vo
    vecs[:, vo["pre1"]:vo["pre1"] + c.KC] = _fm(inp["pre1_g"][l])
    vecs[:, vo["post1"]:vo["post1"] + c.KC] = _fm(inp["post1_g"][l])
    vecs[:, vo["pre2"]:vo["pre2"] + c.KC] = _fm(inp["pre2_g"][l])
    vecs[:, vo["post2"]:vo["post2"] + c.KC] = _fm(inp["post2_g"][l])
    vecs[:, vo["modb"]:vo["modb"] + 6 * c.KC] = _fm(inp["mod_b"][l])
    vecs[:, vo["convb"]:vo["convb"] + 2] = _fm(inp["conv_b"][l])
    vecs[:, vo["lng"]:vo["lng"] + 2] = _fm(inp["conv_ln_g"][l])
    vecs[:, vo["lnb"]:vo["lnb"] + 2] = _fm(inp["conv_ln_b"][l])
    cw = inp["conv_w"][l]
    vecs[:, vo["convw"]:vo["convw"] + 2 * CONV_K] = np.ascontiguousarray(
        cw.T.reshape(2, 128, CONV_K).transpose(1, 0, 2)).reshape(128, 2 * CONV_K)
    vecs[:, vo["qg"]:vo["qg"] + 2] = _fm(inp["mla_q_norm_g"][l])
    vecs[:, vo["kvg"]:vo["kvg"] + 1] = _fm(inp["mla_kv_norm_g"][l])
    rows = np.broadcast_to(inp["ret_gn_g"][l][None, :], (128, 256)).copy()
    lgv = np.broadcast_to(inp["ret_log_decay"][l].reshape(1, 8), (128, 8)).copy()
    d = dict(win=win, wuq=wuq, wukv=wukv, wout=wout, w1=w1, w2=w2, modw=modw, vecs=vecs, rows=rows, lg=lgv)
    return {f"{k}{l}": np.ascontiguousarray(v_, dtype=np.float32) for k, v_ in d.items()}


class Builder:
    def __init__(self, cfg, layers, n_out_ctx, fused=False):
        self.c = cfg
        self.layers = layers
        self.n_out_ctx = n_out_ctx
        self.fused = fused
        self.nc = bass.Bass("TRN2", target_bir_lowering=False)
        self.uid = 0

    def T(self, st, shape, dt, name=None):
        self.uid += 1
        return st.enter_context(self.nc.sbuf_tensor(f"{name or 't'}_{self.uid}", shape, dt))

    def P(self, st, shape, dt, name=None):
        self.uid += 1
        return st.enter_context(self.nc.psum_tensor(f"{name or 'p'}_{self.uid}", shape, dt))

    def dram_in(self, name, shape, dt=F32):
        return self.nc.dram_tensor(name, list(shape), dt, kind="ExternalInput").ap()

    def mm(self, out, lhsT, rhs, start, stop, reads, writes):
        self.em.op(PE, lambda e: e.matmul(out, lhsT=lhsT, rhs=rhs, start=start, stop=stop), reads, writes)

    def tr(self, out, in_, ident, reads, writes):
        self.em.op(PE, lambda e: e.transpose(out=out, in_=in_, identity=ident), reads, writes)

    def act(self, out, in_, func, reads, writes, **kw):
        self.em.op(S, lambda e: e.activation(out=out, in_=in_, func=func, **kw), reads, writes)

    def tt(self, eng, out, in0, in1, op, reads, writes):
        self.em.op(eng, lambda e: e.tensor_tensor(out=out, in0=in0, in1=in1, op=op), reads, writes)

    def ts(self, eng, out, in0, s1, s2, op0, op1, reads, writes):
        if op1 is None:
            self.em.op(eng, lambda e: e.tensor_scalar(out=out, in0=in0, scalar1=s1, scalar2=None, op0=op0), reads, writes)
        else:
            self.em.op(eng, lambda e: e.tensor_scalar(out=out, in0=in0, scalar1=s1, scalar2=s2, op0=op0, op1=op1),
                       reads, writes)

    def stt(self, out, in0, scalar, in1, op0, op1, reads, writes):
        self.em.op(V, lambda e: e.scalar_tensor_tensor(out=out, in0=in0, scalar=scalar, in1=in1, op0=op0, op1=op1),
                   reads, writes)

    def cp(self, eng, out, in_, reads, writes):
        if eng == S:
            self.em.op(S, lambda e: e.copy(out=out, in_=in_), reads, writes)
        else:
            self.em.op(eng, lambda e: e.tensor_copy(out=out, in_=in_), reads, writes)

    def rsqrt_(self, out, in_, width, reads, writes):
        self.em.op(S, lambda e: e.activation(out=out, in_=in_, func=AF.Sqrt), list(reads), writes)
        self.em.op(V, lambda e: e.reciprocal(out=out, in_=out), list(writes), writes)

    def build(self):
        c, nc = self.c, self.nc
        with contextlib.ExitStack() as top:
            self.em = Emit(nc, top)
            em = self.em
            self.d_xown = self.dram_in("x_own", [c.TH, c.D])
            self.d_xoth = self.dram_in("x_oth", [c.TH, c.D])
            self.d_xc = self.dram_in("xc", [c.CTX, c.D])
            self.d_cc = self.dram_in("cc", [128, c.KC, 2])
            self.d_fl = self.dram_in("fl", [128, 2])
            self.d_rope = self.dram_in("ropetab", [128, 4, c.NK])
            self.d_dtab = self.dram_in("dtab", [128, c.GT, c.QG])
            lat, ctxl, nF, nB, nO = _kt_lists(c)
            self.ktl_lat, self.ktl_ctx, self.nF, self.nB, self.nO = lat, ctxl, nF, nB, nO
            self.d_cx = self.dram_in("cx", [128, nF + nB + nO])
            if self.fused:
                self.d_flB = self.dram_in("flB", [128, 2])
                self.d_ropeB = self.dram_in("ropetabB", [128, 4, c.NK])
                self.d_cxB = self.dram_in("cxB", [128, nF + nB + nO])
            self.dw = {}
            for l in self.layers:
                self.dw[l] = dict(
                    win=self.dram_in(f"win{l}", [NWC, 128, c.KC * 128]),
                    wuq=self.dram_in(f"wuq{l}", [2, 128, 2048]),
                    wukv=self.dram_in(f"wukv{l}", [128, 1024]),
                    wout=self.dram_in(f"wout{l}", [8, 128, c.D]),
                    w1=self.dram_in(f"w1{l}", [c.FC, 128, c.KC * 256]),
                    w2=self.dram_in(f"w2{l}", [c.FC, 128, c.D]),
                    modw=self.dram_in(f"modw{l}", [c.MG, 128, c.KC * 512]),
                    vecs=self.dram_in(f"vecs{l}", [128, c.NV]),
                    rows=self.dram_in(f"rows{l}", [128, 256]),
                    lg=self.dram_in(f"lg{l}", [128, 8]),
                )
            self.d_out = nc.dram_tensor("x_out", [c.TH, c.D], F32, kind="ExternalOutput").ap()
            if self.n_out_ctx:
                self.d_outc = nc.dram_tensor("xc_out", [c.CTX, c.D], F32, kind="ExternalOutput").ap()

            self.hc_ng = 1 + 2 * c.NG
            self.hcache = nc.dram_tensor("hcache", [self.hc_ng, 128, c.KC * c.QG], BF16, kind="Internal").ap()
            self.Rhc = [Res() for _ in range(self.hc_ng)]
            self.x = self.T(top, [128, c.NT, c.D], F32, "x")
            self.xc = self.T(top, [128, c.NC, c.D], F32, "xc")
            self.Rx = [Res(f"x{i}") for i in range(c.NT)]
            self.Rxc = [Res(f"xc{i}") for i in range(c.NC)]
            self.ident = self.T(top, [128, 128], BF16, "ident")
            self.identf = self.T(top, [128, 128], F32, "identf")
            self.onesb = self.T(top, [128, 128], BF16, "onesb")
            self.onesf = self.T(top, [128, 128], F32, "onesf")
            self.epsc = self.T(top, [128, 1], F32, "epsc")
            self.fl = self.T(top, [128, 2], F32, "fl")
            self.flB = self.T(top, [128, 2], F32, "flB")
            self.cc = self.T(top, [128, c.KC, 2], F32, "cc")
            self.Rc = Res("const")
            self.modT = self.T(top, [128, 6 * c.KC, 2], F32, "modT")
            self.AB = self.T(top, [128, 6, c.KC, 2], F32, "AB")
            self.Rmod = Res("mod")
            self.vecs = self.T(top, [128, c.NV], F32, "vecs")
            self.lg = self.T(top, [128, 8], F32, "lg")
            self.lgx = self.T(top, [128, 16], F32, "lgx")
            self.Rvec = Res("vecs")

            self.emit_consts()
            em.flush()
            self.Roth = [Res() for _ in range(c.NT)]
            tabA = (self.d_rope, self.d_cx, self.fl)
            if not self.fused:
                self.load_x(self.d_xown)
                self.d_oth_cur = self.d_xoth
                self.cur_rope, self.cur_cx, self.cur_fl = tabA
                for li, l in enumerate(self.layers):
                    self.emit_layer(l, l == c.DEPTH - 1, first=(li == 0))
            else:
                assert c.DEPTH == 2
                tabB = (self.d_ropeB, self.d_cxB, self.flB)
                xoth1 = nc.dram_tensor("xoth1", [c.TH, c.D], F32, kind="Internal").ap()
                self.load_x(self.d_xoth)
                self.d_oth_cur = self.d_xown
                self.cur_rope, self.cur_cx, self.cur_fl = tabB

                def handover(t):
                    em.dma(SP, xoth1[t * 128:(t + 1) * 128, :], self.x[:, t, :], reads=[self.Rx[t]], writes=[self.Roth[t]],
                           semkey=f"xst{t % 4}")
                    em.dma(SP, self.x[:, t, :], self.d_xown[t * 128:(t + 1) * 128, :], writes=[self.Rx[t]], semkey=f"x{t}")
                self.emit_layer(0, True, first=True, post_tile=handover)
                self.d_oth_cur = self.d_xoth
                self.cur_rope, self.cur_cx, self.cur_fl = tabA
                self.hc_swap = True
                self.emit_layer(0, False, first=False)
                self.hc_swap = False
                self.d_oth_cur = xoth1
                self.out_evs = []

                def emit_out(t):
                    self.out_evs.append(em.dma(SP, self.d_out[t * 128:(t + 1) * 128, :], self.x[:, t, :], reads=[self.Rx[t]],
                                               semkey="out"))
                self.emit_layer(1, True, first=False, post_tile=emit_out)
            evs = list(getattr(self, "out_evs", []))
            if not evs:
                for t in range(c.NT):
                    evs.append(em.dma(SP, self.d_out[t * 128:(t + 1) * 128, :], self.x[:, t, :], reads=[self.Rx[t]], semkey="out"))
            if self.n_out_ctx:
                for t in range(c.NC):
                    evs.append(em.dma(SP, self.d_outc[t * 128:(t + 1) * 128, :], self.xc[:, t, :], reads=[self.Rxc[t]],
                                      semkey="out"))
            em.wait_event(SP, evs[-1])
            em.flush()
        return nc

    def emit_consts(self):
        c, em = self.c, self.em
        Rc = self.Rc
        identf, ident, onesb, onesf = self.identf, self.ident, self.onesb, self.onesf
        em.op(G, lambda e: e.memset(identf[:], 0.0), writes=[Rc])
        em.op(G, lambda e: e.affine_select(out=identf[:], in_=identf[:], pattern=[[-1, 128]], compare_op=ALU.not_equal,
                                           fill=1.0, base=0, channel_multiplier=1), reads=[Rc], writes=[Rc])
        em.op(V, lambda e: e.tensor_copy(out=ident[:], in_=identf[:]), reads=[Rc], writes=[Rc])
        em.op(G, lambda e: e.memset(onesf[:], 1.0), writes=[Rc])
        em.op(V, lambda e: e.tensor_copy(out=onesb[:], in_=onesf[:]), reads=[Rc], writes=[Rc])
        epsc = self.epsc
        em.op(G, lambda e: e.memset(epsc[:], EPS), writes=[Rc])
        em.dma(SP, self.fl[:], self.d_fl, writes=[Rc], semkey="c0")
        em.dma(SP, self.cc[:], self.d_cc, writes=[Rc], semkey="c1")
        if self.fused:
            em.dma(SP, self.flB[:], self.d_flB, writes=[Rc], semkey="c2")
        for t in range(c.NC):
            em.dma(SP, self.xc[:, t, :], self.d_xc[t * 128:(t + 1) * 128, :], writes=[self.Rxc[t]], semkey=f"xc{t}")

    def load_x(self, src):
        c, em = self.c, self.em
        for t in range(c.NT):
            em.dma(SP, self.x[:, t, :], src[t * 128:(t + 1) * 128, :], writes=[self.Rx[t]], semkey=f"x{t}")

    def emit_layer(self, l, last, first, post_tile=None):
        c, em = self.c, self.em
        self.l = l
        self.W = self.dw[l]
        done = getattr(self, "_mod_done", set())
        self.phase_mod(full=(l not in done))
        done.add(l)
        self._mod_done = done
        if not getattr(self, "hc_swap", False):
            self.phase_norm1()
        with contextlib.ExitStack() as ms:
            self.mixT = self.T(ms, [128, 8, c.TH], BF16, "mixT")
            self.mixTc = self.T(ms, [128, 8, c.CTX], BF16, "mixTc")
            self.Rmix = [[Res() for _ in range(c.NT)] for _ in range(8)]
            self.Rmixc = [[Res() for _ in range(c.NC)] for _ in range(8)]
            self.phase_conv(last)
            for hp in range(2):
                self.phase_ret(hp, last)
            self.phase_mla(last)
            self.phase_out(last)
        self.phase_ffn(last, post_tile)

    def seg_tiles(self, seg):
        c = self.c
        return {"ctx": c.NC, "oth": c.NT, "own": c.NT}[seg]

    def seg_kbase(self, seg):
        c = self.c
        return {"ctx": 0, "oth": c.CTX, "own": c.CTX + c.TH}[seg]

    def seg_groups(self, seg):
        c = self.c
        n = self.seg_tiles(seg)
        g = min(n, c.GT)
        return [(i, g) for i in range(0, n, g)]

    def phase_mod(self, full=True):
        c, em, W = self.c, self.em, self.W
        KC = c.KC
        with contextlib.ExitStack() as ph:
            Rvec, Rmod, Rc = self.Rvec, self.Rmod, self.Rc
            if full:
                em.dma(SP, self.vecs[:], W["vecs"], writes=[Rvec], semkey="vec")
                em.dma(SP, self.lg[:], W["lg"], writes=[Rvec], semkey="vec")
                self.mod_vectors(ph)
            self.decay_scalars()
            em.flush()

    def mod_vectors(self, ph):
        c, em, W = self.c, self.em, self.W
        KC = c.KC
        if True:
            Rvec, Rmod, Rc = self.Rvec, self.Rmod, self.Rc
            scT = self.T(ph, [128, KC, 2], BF16, "scT")
            Rs = Res()
            self.act(scT[:], self.cc[:], AF.Silu, [Rc], [Rs])
            wb = [self.T(ph, [128, KC, 512], BF16, "modw") for _ in range(2)]
            Rwb = [Res(), Res()]
            prow = [self.P(ph, [128, 512], F32, "prow") for _ in range(2)]
            Rprow = [Res(), Res()]
            rowt = [self.T(ph, [2, 512], F32, "rowt") for _ in range(2)]
            Rrow = [Res(), Res()]
            pT = self.P(ph, [128, c.MG * 4, 2], F32, "pT")
            RpT = Res()
            for j in range(c.MG):
                b = j % 2
                for k in range(0, KC, 4):
                    k2 = min(KC, k + 4)
                    em.dma(G, wb[b][:, k:k2, :], W["modw"][j].rearrange("p (k n) -> p k n", k=KC)[:, k:k2, :], writes=[Rwb[b]],
                           semkey=f"mw{b}")
                for k in range(KC):
                    self.mm(prow[b][0:2, :], scT[:, k, :], wb[b][:, k, :], k == 0, k == KC - 1, [Rs, Rwb[b]], [Rprow[b]])
                self.cp(V, rowt[b][:], prow[b][0:2, :], [Rprow[b]], [Rrow[b]])
                for q in range(4):
                    self.mm(pT[:, j * 4 + q, :], rowt[b][0:2, q * 128:(q + 1) * 128], self.identf[0:2, 0:2], True, True,
                            [Rrow[b], Rc], [RpT])
            modT = self.modT
            vo = c.vo
            for s_ in range(2):
                self.tt(V, modT[:, :, s_], pT[:, :, s_], self.vecs[:, vo["modb"]:vo["modb"] + 6 * KC], ALU.add,
                        [RpT, Rvec], [Rmod])
            AB = self.AB
            for s_ in range(2):
                for (dst, vsc, vsh, vg, gpre, gpost) in ((0, 1, 0, 2, "pre1", "post1"), (3, 4, 3, 5, "pre2", "post2")):
                    self.stt(AB[:, dst, :, s_], modT[:, vsc * KC:(vsc + 1) * KC, s_], 1.0,
                             self.vecs[:, vo[gpre]:vo[gpre] + KC], ALU.add, ALU.mult, [Rmod, Rvec], [Rmod])
                    self.cp(V, AB[:, dst + 1, :, s_], modT[:, vsh * KC:(vsh + 1) * KC, s_], [Rmod], [Rmod])
                    self.tt(V, AB[:, dst + 2, :, s_], modT[:, vg * KC:(vg + 1) * KC, s_],
                            self.vecs[:, vo[gpost]:vo[gpost] + KC], ALU.mult, [Rmod, Rvec], [Rmod])

    def decay_scalars(self):
        if True:
            Rvec, Rc = self.Rvec, self.Rc
            lg, lgx, fl = self.lg, self.lgx, self.cur_fl
            self.ts(V, lgx[:, 0:4], lg[:, 4:8], -1.0, None, ALU.mult, None, [Rvec], [Rvec])
            self.ts(V, lgx[:, 4:8], lg[:, 0:4], fl[:, 0:1], None, ALU.mult, None, [Rvec, Rc], [Rvec])
            self.ts(V, lgx[:, 12:16], lg[:, 4:8], fl[:, 1:2], None, ALU.mult, None, [Rvec, Rc], [Rvec])
            self.tt(V, lgx[:, 8:12], lgx[:, 4:8], lgx[:, 12:16], ALU.subtract, [Rvec], [Rvec])
            self.tt(V, lgx[:, 4:8], lgx[:, 4:8], lgx[:, 12:16], ALU.add, [Rvec], [Rvec])

    def bcast_vec(self, ph, vec_idx, s_, pbank, Rp):
        c = self.c
        out = self.T(ph, [128, c.D], F32, "bc")
        Ro = Res()
        dg = [self.T(ph, [128, 128], F32, "dg") for _ in range(2)]
        Rdg = [Res(), Res()]
        for k in range(c.KC):
            b = k % 2
            self.ts(V, dg[b][:], self.identf[:], self.AB[:, vec_idx, k, s_:s_ + 1], None, ALU.mult, None,
                    [self.Rc, self.Rmod], [Rdg[b]])
            self.mm(pbank[:, 0:128], self.onesf[:], dg[b][:], True, True, [Rdg[b], self.Rc], [Rp])
            self.cp(V, out[:, k * 128:(k + 1) * 128], pbank[:, 0:128], [Rp], [Ro])
        return out, Ro

    def make_norm_ctx(self, ph):
        c = self.c
        d = dict(
            junk=self.T(ph, [128, c.D], BF16, "junk"), Rjunk=Res(),
            xn=[self.T(ph, [128, c.D], BF16, "xn") for _ in range(2)], Rxn=[Res(), Res()],
            st=[self.T(ph, [128, 4], F32, "nst") for _ in range(2)], Rst=[Res(), Res()],
            pT=[self.P(ph, [128, c.KC, 128], BF16, "pTn") for _ in range(2)], RpT=[Res(), Res()],
            xo=[self.T(ph, [128, c.D], F32, "xo") for _ in range(2)], Rxo=[Res(), Res()],
            i=0,
        )
        return d

    def norm_A(self, nctx, seg, tile):
        c, em = self.c, self.em
        i = nctx["i"]
        nctx["i"] += 1
        b = i % 2
        if seg == "own":
            xin, Rxin = self.x[:, tile, :], self.Rx[tile]
        elif seg == "ctx":
            xin, Rxin = self.xc[:, tile, :], self.Rxc[tile]
        else:
            xo, Rxo = nctx["xo"][b], nctx["Rxo"][b]
            em.dma(SP, xo[:], self.d_oth_cur[tile * 128:(tile + 1) * 128, :], reads=[self.Roth[tile]], writes=[Rxo],
                   semkey=f"xo{b}")
            xin, Rxin = xo[:], Rxo
        st, Rst = nctx["st"][b], nctx["Rst"][b]
        xn, Rxn = nctx["xn"][b], nctx["Rxn"][b]
        self.act(nctx["junk"][:], xin, AF.Square, [Rxin], [nctx["Rjunk"], Rst], accum_out=st[:, 0:1])
        if FAST_RSQRT:
            self.act(st[:, 2:3], st[:, 0:1], AF.Abs_reciprocal_sqrt, [Rst, self.Rc], [Rst], scale=1.0 / c.D, bias=self.epsc[:, 0:1])
        else:
            self.ts(V, st[:, 1:2], st[:, 0:1], 1.0 / c.D, EPS, ALU.mult, ALU.add, [Rst], [Rst])
            self.rsqrt_(st[:, 2:3], st[:, 1:2], 1, [Rst], [Rst])
        self.ts(V, xn[:], xin, st[:, 2:3], None, ALU.mult, None, [Rxin, Rst], [Rxn])
        return b

    def norm_B(self, nctx, b, seg, hT, col0, RhT, vA, s_=None):
        c = self.c
        if s_ is None:
            s_ = 1 if seg == "ctx" else 0
        xn, Rxn = nctx["xn"][b], nctx["Rxn"][b]
        pT, RpT = nctx["pT"][b], nctx["RpT"][b]
        for k in range(c.KC):
            self.tr(pT[:, k, :], xn[:, k * 128:(k + 1) * 128], self.ident[:], [Rxn, self.Rc], [RpT])
        for k in range(c.KC):
            A = self.AB[:, vA, k, s_:s_ + 1]
            B = self.AB[:, vA + 1, k, s_:s_ + 1]
            if k % 2 == 0:
                self.ts(V, hT[:, k, col0:col0 + 128], pT[:, k, :], A, B, ALU.mult, ALU.add, [RpT, self.Rmod], [RhT])
            else:
                self.act(hT[:, k, col0:col0 + 128], pT[:, k, :], AF.Identity, [RpT, self.Rmod], [RhT], scale=A, bias=B)

    def norm_items(self, nctx, items):
        prev = None
        for it in items:
            b = self.norm_A(nctx, it[0], it[1])
            if prev is not None:
                pb, pit = prev
                self.norm_B(nctx, pb, pit[0], pit[2], pit[3], pit[4], pit[5])
                if pit[6] is not None:
                    pit[6]()
            prev = (b, it)
        if prev is not None:
            pb, pit = prev
            self.norm_B(nctx, pb, pit[0], pit[2], pit[3], pit[4], pit[5])
            if pit[6] is not None:
                pit[6]()

    def hc_index(self, seg, t0):
        c = self.c
        if getattr(self, "hc_swap", False) and seg != "ctx":
            seg = "own" if seg == "oth" else "oth"
        return {"ctx": 0, "oth": 1, "own": 1 + c.NG}[seg] + (0 if seg == "ctx" else t0 // c.GT)

    def phase_norm1(self):
        c, em = self.c, self.em
        with contextlib.ExitStack() as sp:
            nctx = self.make_norm_ctx(sp)
            hT = [self.T(sp, [128, c.KC, c.QG], BF16, "hT") for _ in range(2)]
            RhT = [Res(), Res()]
            gi = 0
            items = []
            for seg in ("ctx", "oth", "own"):
                for (t0, nt) in self.seg_groups(seg):
                    b = gi % 2
                    gi += 1
                    g = self.hc_index(seg, t0)

                    def store(b=b, g=g, nt=nt):
                        dst = self.hcache[g].rearrange("p (k n) -> p k n", k=c.KC)
                        em.dma(SP, dst[:, :, 0:nt * 128], hT[b][:, :, 0:nt * 128], reads=[RhT[b]], writes=[self.Rhc[g]],
                               semkey=f"hcst{b}")
                    for i in range(nt):
                        items.append((seg, t0 + i, hT[b], i * 128, RhT[b], 0, store if i == nt - 1 else None))
            self.norm_items(nctx, items)
            em.flush()

    def load_h(self, hT, RhT, seg, t0, nt, key):
        c = self.c
        g = self.hc_index(seg, t0)
        off = (t0 % c.GT) * 128 if seg != "ctx" else t0 * 128
        src = self.hcache[g].rearrange("p (k n) -> p k n", k=c.KC)
        self.em.dma(SP, hT[:, :, 0:nt * 128], src[:, :, off:off + nt * 128], reads=[self.Rhc[g]], writes=[RhT], semkey=key)

    def load_w(self, dst, src, Rd, key):
        return self.em.dma(G, dst, src, writes=[Rd], semkey=key)

    def load_win(self, ph, chunks):
        c = self.c
        out = {}
        for i, ch in enumerate(chunks):
            t = self.T(ph, [128, c.KC, 128], BF16, f"win{ch}")
            R = Res()
            self.load_w(t[:], self.W["win"][ch].rearrange("p (k n) -> p k n", k=c.KC), R, f"win{i}")
            out[ch] = (t, R)
        return out

    def proj_fm(self, ps, Rps, wt, hT, RhT, cols, ncols, M=128):
        c = self.c
        w, Rw = wt
        for k in range(c.KC):
            self.mm(ps[0:M, 0:ncols], w[:, k, 0:M], hT[:, k, cols:cols + ncols], k == 0, k == c.KC - 1, [Rw, RhT], [Rps])

    def proj_tm(self, ps, Rps, wt, hT, RhT, col0, o0=0):
        c = self.c
        w, Rw = wt
        for k in range(c.KC):
            self.mm(ps[:, o0:o0 + 128], hT[:, k, col0:col0 + 128], w[:, k, :], k == 0, k == c.KC - 1, [Rw, RhT], [Rps])

    def phase_conv(self, last):
        c, em = self.c, self.em
        vo = c.vo
        with contextlib.ExitStack() as ph:
            wts = self.load_win(ph, [0, 1, 2, 3])
            QG = c.QG
            hT = [self.T(ph, [128, c.KC, QG], BF16, "hT") for _ in range(2)]
            RhT = [Res(), Res()]
            pu = [self.P(ph, [128, 512], F32, "pu") for _ in range(2)]
            pg = [self.P(ph, [128, 512], F32, "pg") for _ in range(2)]
            Rpu, Rpg = [Res(), Res()], [Res(), Res()]
            sg = [self.T(ph, [128, 512], F32, "sg") for _ in range(2)]
            Rsg = [Res(), Res()]
            segs = [("own", c.TH)] + ([] if last else [("ctx", c.CTX)])
            for seg, ntok in segs:
                with contextlib.ExitStack() as sp:
                    ypad = self.T(sp, [128, 2, ntok + 2 * HALO], F32, "ypad")
                    Rypad = [Res(), Res()]
                    acc = self.T(sp, [128, 2, ntok], F32, "acc")
                    Racc = [Res(), Res()]
                    for ch in range(2):
                        em.op(G, lambda e, ch=ch: e.memset(ypad[:, ch, 0:HALO], 0.0), writes=[Rypad[ch]])
                        em.op(G, lambda e, ch=ch, ntok=ntok: e.memset(ypad[:, ch, HALO + ntok:2 * HALO + ntok], 0.0),
                              writes=[Rypad[ch]])
                    gi = 0

                    def glu_group(sseg, t0, nt, dst_fn):
                        nonlocal gi
                        b = gi % 2
                        gi += 1
                        self.load_h(hT[b], RhT[b], sseg, t0, nt, f"hld{b}")
                        n = nt * 128
                        for ch in range(2):
                            bb = ch
                            self.proj_fm(pu[bb], Rpu[bb], wts[ch], hT[b], RhT[b], 0, n)
                            self.proj_fm(pg[bb], Rpg[bb], wts[2 + ch], hT[b], RhT[b], 0, n)
                            self.act(sg[bb][:, 0:n], pg[bb][:, 0:n], AF.Sigmoid, [Rpg[bb]], [Rsg[bb]])
                            dst_fn(ch, pu[bb], Rpu[bb], sg[bb], Rsg[bb], n)

                    for (t0, nt) in self.seg_groups(seg):
                        def dst(ch, pu_, Rpu_, sg_, Rsg_, n, t0=t0):
                            self.tt(V, ypad[:, ch, HALO + t0 * 128:HALO + t0 * 128 + n], pu_[:, 0:n], sg_[:, 0:n], ALU.mult,
                                    [Rpu_, Rsg_], [Rypad[ch]])
                        glu_group(seg, t0, nt, dst)
                    if seg == "own":
                        tmp = self.T(sp, [128, 128], F32, "halo")
                        Rtmp = Res()

                        def dst_l(ch, pu_, Rpu_, sg_, Rsg_, n):
                            self.tt(V, tmp[:], pu_[:, 0:128], sg_[:, 0:128], ALU.mult, [Rpu_, Rsg_], [Rtmp])
                            self.ts(V, ypad[:, ch, 0:HALO], tmp[:, 128 - HALO:128], self.cur_fl[:, 0:1], None, ALU.mult, None,
                                    [Rtmp, self.Rc], [Rypad[ch]])

                        def dst_r(ch, pu_, Rpu_, sg_, Rsg_, n, ntok=ntok):
                            self.tt(V, tmp[:], pu_[:, 0:128], sg_[:, 0:128], ALU.mult, [Rpu_, Rsg_], [Rtmp])
                            self.ts(V, ypad[:, ch, HALO + ntok:2 * HALO + ntok], tmp[:, 0:HALO], self.cur_fl[:, 1:2], None,
                                    ALU.mult, None, [Rtmp, self.Rc], [Rypad[ch]])
                        glu_group("oth", c.NT - 1, 1, dst_l)
                        glu_group("oth", 0, 1, dst_r)
                    cw0 = vo["convw"]
                    for ch in range(2):
                        for k in range(CONV_K):
                            wk = self.vecs[:, cw0 + ch * CONV_K + k:cw0 + ch * CONV_K + k + 1]
                            if k == 0:
                                self.ts(V, acc[:, ch, :], ypad[:, ch, 0:ntok], wk, None, ALU.mult, None,
                                        [Rypad[ch], self.Rvec], [Racc[ch]])
                            else:
                                self.stt(acc[:, ch, :], ypad[:, ch, k:k + ntok], wk, acc[:, ch, :], ALU.mult, ALU.add,
                                         [Rypad[ch], self.Rvec, Racc[ch]], [Racc[ch]])
                        self.ts(V, acc[:, ch, :], acc[:, ch, :], self.vecs[:, vo["convb"] + ch:vo["convb"] + ch + 1], None,
                                ALU.add, None, [Racc[ch], self.Rvec], [Racc[ch]])
                    sq = self.T(sp, [128, 2, 512], F32, "sq")
                    Rsq = Res()
                    mean = self.T(sp, [128, 512], F32, "mean")
                    var = self.T(sp, [128, 512], F32, "var")
                    Rmv = Res()
                    zt = self.T(sp, [128, 512], F32, "zt")
                    Rzt = Res()
                    mixT, Rmix = (self.mixT, self.Rmix) if seg == "own" else (self.mixTc, self.Rmixc)
                    for (t0, nt) in self.seg_groups(seg):
                        n = nt * 128
                        c0 = t0 * 128
                        p1, Rp1, p2, Rp2 = pu[0], Rpu[0], pg[0], Rpg[0]
                        for ch in range(2):
                            self.act(sq[:, ch, 0:n], acc[:, ch, c0:c0 + n], AF.Square, [Racc[ch]], [Rsq])
                        for ch in range(2):
                            self.mm(p1[:, 0:n], self.onesf[:], acc[:, ch, c0:c0 + n], ch == 0, ch == 1, [Racc[ch], self.Rc], [Rp1])
                        for ch in range(2):
                            self.mm(p2[:, 0:n], self.onesf[:], sq[:, ch, 0:n], ch == 0, ch == 1, [Rsq, self.Rc], [Rp2])
                        self.ts(V, mean[:, 0:n], p1[:, 0:n], 1.0 / 256, None, ALU.mult, None, [Rp1], [Rmv])
                        self.tt(V, var[:, 0:n], mean[:, 0:n], mean[:, 0:n], ALU.mult, [Rmv], [Rmv])
                        self.stt(var[:, 0:n], p2[:, 0:n], 1.0 / 256, var[:, 0:n], ALU.mult, ALU.subtract, [Rp2, Rmv], [Rmv])
                        self.ts(V, var[:, 0:n], var[:, 0:n], EPS, None, ALU.add, None, [Rmv], [Rmv])
                        self.rsqrt_(var[:, 0:n], var[:, 0:n], n, [Rmv], [Rmv])
                        for ch in range(2):
                            self.tt(V, zt[:, 0:n], acc[:, ch, c0:c0 + n], mean[:, 0:n], ALU.subtract, [Racc[ch], Rmv], [Rzt])
                            self.tt(V, zt[:, 0:n], zt[:, 0:n], var[:, 0:n], ALU.mult, [Rzt, Rmv], [Rzt])
                            self.ts(V, zt[:, 0:n], zt[:, 0:n], self.vecs[:, vo["lng"] + ch:vo["lng"] + ch + 1],
                                    self.vecs[:, vo["lnb"] + ch:vo["lnb"] + ch + 1], ALU.mult, ALU.add, [Rzt, self.Rvec], [Rzt])
                            self.act(mixT[:, ch, c0:c0 + n], zt[:, 0:n], AF.Silu, [Rzt], [Rmix[ch][t] for t in range(t0, t0 + nt)])
                    em.flush()

    def phase_ret(self, hp, last):
        c, em = self.c, self.em
        QG, GT = c.QG, c.GT
        with contextlib.ExitStack() as ph:
            NQ = c.TH + c.CTX
            kT = self.T(ph, [128, c.NK], BF16, "kT")
            qT = self.T(ph, [128, NQ], BF16, "qT")
            vv = self.T(ph, [128, c.NKT, 128], BF16, "vv")
            sgt = self.T(ph, [128, c.NT + c.NC, 128], BF16, "sgt")
            RkT = [Res() for _ in range(c.NKT)]
            RqT = [Res() for _ in range(c.NT + c.NC)]
            Rvv = [Res() for _ in range(c.NKT)]
            Rsgt = [Res() for _ in range(c.NT + c.NC)]
            Ef = self.T(ph, [128, 2, QG], BF16, "Ef")
            Eb = self.T(ph, [128, 2, QG], BF16, "Eb")
            Eo = self.T(ph, [128, 2, QG], BF16, "Eo")
            Ed = self.T(ph, [128, GT, 2, QG], BF16, "Ed")
            nF, nB, nO = self.nF, self.nB, self.nO
            Ct = self.T(ph, [128, 2, nF + nB + nO], F32, "Ct")
            RE = Res()
            with contextlib.ExitStack() as sp:
                dt_ = self.T(sp, [128, GT, QG], F32, "dtab")
                cx = self.T(sp, [128, nF + nB + nO], F32, "cx")
                Rt = Res()
                em.dma(SP, dt_[:], self.d_dtab, writes=[Rt], semkey="tab")
                em.dma(SP, cx[:], self.cur_cx, writes=[Rt], semkey="tab")
                dp = self.T(sp, [128, QG], F32, "dp")
                dn = self.T(sp, [128, QG], F32, "dn")
                ind = self.T(sp, [128, QG], F32, "ind")
                t1 = self.T(sp, [128, QG], F32, "t1")
                t2 = self.T(sp, [128, QG], F32, "t2")
                Rd = Res()
                lg, lgx = self.lg, self.lgx
                for hh in range(2):
                    h = 2 * hp + hh
                    self.act(Ef[:, hh, :], dt_[:, 0, :], AF.Exp, [Rt, self.Rvec], [RE], scale=lg[:, h:h + 1])
                    self.act(Eb[:, hh, :], dt_[:, 0, :], AF.Exp, [Rt, self.Rvec], [RE], scale=lgx[:, h:h + 1])
                    self.act(Eo[:, hh, :], dt_[:, 0, :], AF.Exp, [Rt, self.Rvec], [RE], scale=lgx[:, 8 + h:9 + h])
                    self.act(Ct[:, hh, 0:nF], cx[:, 0:nF], AF.Exp, [Rt, self.Rvec], [RE], scale=lg[:, h:h + 1])
                    self.act(Ct[:, hh, nF:nF + nB], cx[:, nF:nF + nB], AF.Exp, [Rt, self.Rvec], [RE], scale=lg[:, 4 + h:5 + h])
                    self.act(Ct[:, hh, nF + nB:], cx[:, nF + nB:], AF.Exp, [Rt, self.Rvec], [RE], scale=lgx[:, 4 + h:5 + h])
                for j in range(GT):
                    self.ts(V, dp[:], dt_[:, j, :], 0.0, None, ALU.max, None, [Rt], [Rd])
                    self.ts(V, dn[:], dt_[:, j, :], 0.0, None, ALU.min, None, [Rt], [Rd])
                    self.ts(V, ind[:], dt_[:, j, :], 0.0, None, ALU.is_ge, None, [Rt], [Rd])
                    for hh in range(2):
                        h = 2 * hp + hh
                        self.act(t1[:], dp[:], AF.Exp, [Rd, self.Rvec], [Rd], scale=lg[:, h:h + 1])
                        self.act(t2[:], dn[:], AF.Exp, [Rd, self.Rvec], [Rd], scale=lgx[:, h:h + 1])
                        self.tt(V, t1[:], t1[:], t2[:], ALU.subtract, [Rd], [Rd])
                        self.tt(V, t1[:], t1[:], ind[:], ALU.mult, [Rd], [Rd])
                        self.tt(V, Ed[:, j, hh, :], t1[:], t2[:], ALU.add, [Rd], [RE])
                em.flush()
            with contextlib.ExitStack() as sp:
                wts = self.load_win(sp, [4 + hp, 6 + hp, 8 + hp, 10 + hp, 12 + hp, 14 + hp])
                hT = [self.T(sp, [128, c.KC, QG], BF16, "hT") for _ in range(2)]
                RhT = [Res(), Res()]
                rope = [self.T(sp, [128, 2, QG], F32, "rope") for _ in range(2)]
                Rrope = [Res(), Res()]
                pp = [self.P(sp, [128, 512], F32, "pp") for _ in range(4)]
                Rpp = [Res() for _ in range(4)]
                ptm = [self.P(sp, [128, 512], F32, "ptm") for _ in range(2)]
                Rptm = [Res(), Res()]
                ta = self.T(sp, [128, QG], F32, "ta")
                tb = self.T(sp, [128, QG], F32, "tb")
                Rta, Rtb = Res(), Res()
                gi = 0
                for seg in ("ctx", "oth", "own"):
                    kb = self.seg_kbase(seg)
                    for (t0, nt) in self.seg_groups(seg):
                        b = gi % 2
                        gi += 1
                        n = nt * 128
                        k0 = kb + t0 * 128
                        em.dma(SP, rope[b][:, :, 0:n], self.cur_rope[:, 0:2, k0:k0 + n], writes=[Rrope[b]], semkey=f"rope{b}")
                        self.load_h(hT[b], RhT[b], seg, t0, nt, f"hld{b}")
                        need_q = (seg == "own") or (seg == "ctx" and not last)
                        jobs = [(2, 3, 0.125, kT, k0, [RkT[kb // 128 + t0 + i] for i in range(nt)])]
                        if need_q:
                            qc0 = (0 if seg == "ctx" else c.CTX) + t0 * 128
                            qt0 = (0 if seg == "ctx" else c.NC) + t0
                            jobs.append((0, 1, 1.0, qT, qc0, [RqT[qt0 + i] for i in range(nt)]))
                        for (wa, wb_, scl, dstT, d0, Rd_) in jobs:
                            pa, Rpa = pp[(wa) % 4], Rpp[(wa) % 4]
                            pb, Rpb = pp[(wb_) % 4], Rpp[(wb_) % 4]
                            self.proj_fm(pa, Rpa, wts[[4, 6, 8, 10][wa] + hp], hT[b], RhT[b], 0, n)
                            self.proj_fm(pb, Rpb, wts[[4, 6, 8, 10][wb_] + hp], hT[b], RhT[b], 0, n)
                            self.stt(ta[:, 0:n], pa[:, 0:n], scl, rope[b][:, 0, 0:n], ALU.mult, ALU.mult, [Rpa, Rrope[b]], [Rta])
                            self.stt(tb[:, 0:n], pb[:, 0:n], scl, rope[b][:, 1, 0:n], ALU.mult, ALU.mult, [Rpb, Rrope[b]], [Rtb])
                            self.tt(V, dstT[:, d0:d0 + n], ta[:, 0:n], tb[:, 0:n], ALU.add, [Rta, Rtb], Rd_)
                        pv, Rpv = ptm[0], Rptm[0]
                        for i in range(nt):
                            self.proj_tm(pv, Rpv, wts[12 + hp], hT[b], RhT[b], i * 128, o0=i * 128)
                        kt0 = kb // 128 + t0
                        self.cp(V, vv[:, kt0:kt0 + nt, :], pv[:, 0:n].rearrange("p (t d) -> p t d", d=128), [Rpv],
                                [Rvv[kt0 + i] for i in range(nt)])
                        if need_q:
                            pg_, Rpg_ = ptm[1], Rptm[1]
                            for i in range(nt):
                                self.proj_tm(pg_, Rpg_, wts[14 + hp], hT[b], RhT[b], i * 128, o0=i * 128)
                            self.act(sgt[:, qt0:qt0 + nt, :], pg_[:, 0:n].rearrange("p (t d) -> p t d", d=128), AF.Silu,
                                     [Rpg_], [Rsgt[qt0 + i] for i in range(nt)])
                em.flush()
            with contextlib.ExitStack() as sp:
                NS_ = 3
                pst = [self.P(sp, [128, 512], F32, "pst") for _ in range(NS_)]
                Rpst = [Res() for _ in range(NS_)]
                po = [self.P(sp, [128, 512], F32, "po") for _ in range(GT)]
                Rpo = [Res() for _ in range(GT)]
                ptr = self.P(sp, [128, 128], BF16, "ptr")
                Rptr = Res()
                NP_ = 4
                pb_ = [self.T(sp, [128, QG], BF16, "pbuf") for _ in range(NP_)]
                Rpb = [Res() for _ in range(NP_)]
                tsc = [self.T(sp, [128, QG], BF16, "tsc") for _ in range(2)]
                Rtsc = [Res(), Res()]
                rows = self.T(sp, [128, 256], F32, "rows")
                Rrows = Res()
                em.dma(SP, rows[:], self.W["rows"], writes=[Rrows], semkey="rows")
                bst = self.T(sp, [128, 8], F32, "bst")
                bag = self.T(sp, [128, 4], F32, "bag")
                Rb = Res()
                yt = self.T(sp, [128, 64], F32, "yt")
                Ryt = Res()
                yret = [self.T(sp, [128, 128], BF16, "yret") for _ in range(max(GT, c.NC))]
                Ryret = [Res() for _ in range(max(GT, c.NC))]
                cnt = 0
                qsegs = [("own", self.ktl_lat)] + ([] if last else [("ctx", [self.ktl_ctx])])
                ti = 0
                for seg, ktls in qsegs:
                    for gidx, ktl in enumerate(ktls):
                        nt = GT if seg == "own" else c.NC
                        n = nt * 128
                        qc0 = (c.CTX + gidx * QG) if seg == "own" else 0
                        qt0 = (c.NC + gidx * GT) if seg == "own" else 0
                        for hh in range(2):
                            h = 2 * hp + hh
                            r0 = hh * 64
                            base = cnt
                            cnt += len(ktl)

                            def st_mm(idx):
                                kt = ktl[idx][0]
                                sb = (base + idx) % NS_
                                for rep in range(1 + DUP_ST_RET):
                                    self.mm(pst[sb][:, 0:n], kT[r0:r0 + 64, kt * 128:(kt + 1) * 128], qT[r0:r0 + 64, qc0:qc0 + n],
                                            True, True, [RkT[kt]] + [RqT[qt0 + i] for i in range(nt)], [Rpst[sb]])
                            st_mm(0)
                            if len(ktl) > 1:
                                st_mm(1)
                            for idx, (kt, kind, e) in enumerate(ktl):
                                sb = (base + idx) % NS_
                                pbi = (base + idx) % NP_
                                if idx + 2 < len(ktl):
                                    st_mm(idx + 2)
                                alt = RET_SPLIT and (idx % 2 == 1)
                                if kind == "D":
                                    Eap, cs = Ed[:, e, hh, 0:n], None
                                else:
                                    Eap = {"F": Ef, "B": Eb, "O": Eo}[kind][:, hh, 0:n]
                                    eo = {"F": 0, "B": nF, "O": nF + nB}[kind] + e
                                    cs = Ct[:, hh, eo:eo + 1]
                                if alt:
                                    tq_ = (base + idx) // 2 % 2
                                    if cs is None:
                                        self.act(tsc[tq_][:, 0:n], pst[sb][:, 0:n], AF.Copy, [Rpst[sb]], [Rtsc[tq_]])
                                    else:
                                        self.act(tsc[tq_][:, 0:n], pst[sb][:, 0:n], AF.Copy, [Rpst[sb], RE], [Rtsc[tq_]], scale=cs)
                                    self.tt(G, pb_[pbi][:, 0:n], tsc[tq_][:, 0:n], Eap, ALU.mult, [Rtsc[tq_], RE], [Rpb[pbi]])
                                elif cs is None:
                                    self.tt(V, pb_[pbi][:, 0:n], pst[sb][:, 0:n], Eap, ALU.mult, [Rpst[sb], RE], [Rpb[pbi]])
                                else:
                                    self.stt(pb_[pbi][:, 0:n], pst[sb][:, 0:n], cs, Eap, ALU.mult, ALU.mult, [Rpst[sb], RE], [Rpb[pbi]])
                                for i in range(nt):
                                    self.mm(po[i][:, 0:64], pb_[pbi][:, i * 128:(i + 1) * 128], vv[:, kt, r0:r0 + 64],
                                            idx == 0, idx == len(ktl) - 1, [Rpb[pbi], Rvv[kt]], [Rpo[i]])
                            for i in range(nt):
                                yb = i
                                self.em.op(V, lambda e_, i=i: e_.bn_stats(out=bst[:, 0:6], in_=po[i][:, 0:64]), [Rpo[i]], [Rb])
                                self.em.op(V, lambda e_: e_.bn_aggr(out=bag[:, 0:2], in_=bst[:, 0:6]), [Rb], [Rb])
                                self.ts(V, bag[:, 2:3], bag[:, 1:2], EPS, None, ALU.add, None, [Rb], [Rb])
                                self.rsqrt_(bag[:, 3:4], bag[:, 2:3], 1, [Rb], [Rb])
                                self.ts(V, yt[:], po[i][:, 0:64], bag[:, 0:1], bag[:, 3:4], ALU.subtract, ALU.mult,
                                        [Rpo[i], Rb], [Ryt])
                                self.tt(V, yt[:], yt[:], rows[:, h * 64:(h + 1) * 64], ALU.mult, [Ryt, Rrows], [Ryt])
                                self.tt(V, yret[yb][:, r0:r0 + 64], yt[:], sgt[:, qt0 + i, r0:r0 + 64], ALU.mult,
                                        [Ryt, Rsgt[qt0 + i]], [Ryret[yb]])
                                if hh == 1:
                                    self.tr(ptr[:], yret[yb][:], self.ident[:], [Ryret[yb], self.Rc], [Rptr])
                                    if seg == "own":
                                        tcol = (gidx * GT + i) * 128
                                        self.cp(S, self.mixT[:, 2 + hp, tcol:tcol + 128], ptr[:], [Rptr],
                                                [self.Rmix[2 + hp][gidx * GT + i]])
                                    else:
                                        self.cp(S, self.mixTc[:, 2 + hp, i * 128:(i + 1) * 128], ptr[:], [Rptr],
                                                [self.Rmixc[2 + hp][i]])
                em.flush()

    def phase_mla(self, last):
        c, em = self.c, self.em
        QG, GT = c.QG, c.GT
        vo = c.vo
        with contextlib.ExitStack() as ph:
            NQ = c.TH + c.CTX
            ckvT = self.T(ph, [128, c.NK], BF16, "ckvT")
            Rckv = [Res() for _ in range(c.NKT)]
            KT = self.T(ph, [128, c.NK], BF16, "KT")
            RKTn = Res()
            RKTr = [Res() for _ in range(c.NKT)]
            cqT = self.T(ph, [128, 2, NQ], BF16, "cqT")
            Rcq = [Res() for _ in range(c.NT + c.NC)]
            with contextlib.ExitStack() as sp:
                wts = self.load_win(sp, [16, 17, 18, 19, 20])
                hTs = [self.T(sp, [128, c.KC, QG], BF16, "hT") for _ in range(2)]
                RhTs = [Res(), Res()]
                ropes = [self.T(sp, [128, 2, QG], F32, "rope") for _ in range(2)]
                Rropes = [Res(), Res()]
                gi = 0
                pp = [self.P(sp, [128, 512], F32, "pp") for _ in range(4)]
                Rpp = [Res() for _ in range(4)]
                pss = [self.P(sp, [128, 512], F32, "pss") for _ in range(2)]
                Rpss = [Res(), Res()]
                sq = self.T(sp, [128, 2, QG], BF16, "sq")
                Rsq = Res()
                rs = self.T(sp, [128, QG], F32, "rs")
                Rrs = Res()
                ta = self.T(sp, [128, QG], F32, "ta")
                tb = self.T(sp, [128, QG], F32, "tb")
                Rta, Rtb = Res(), Res()
                for seg in ("ctx", "oth", "own"):
                    kb = self.seg_kbase(seg)
                    for (t0, nt) in self.seg_groups(seg):
                        n = nt * 128
                        k0 = kb + t0 * 128
                        kt0 = k0 // 128
                        b = gi % 2
                        gi += 1
                        hT, RhT, rope, Rrope = hTs[b], RhTs[b], ropes[b], Rropes[b]
                        em.dma(SP, rope[64:96, :, 0:n], self.cur_rope[64:96, 2:4, k0:k0 + n], writes=[Rrope], semkey=f"rope{b}")
                        self.load_h(hT, RhT, seg, t0, nt, f"hld{b}")
                        pa, Rpa = pp[0], Rpp[0]
                        self.proj_fm(pa, Rpa, wts[18], hT, RhT, 0, n)
                        self.act(sq[:, 0, 0:n], pa[:, 0:n], AF.Square, [Rpa], [Rsq])
                        self.mm(pss[0][:, 0:n], self.onesb[:], sq[:, 0, 0:n], True, True, [Rsq, self.Rc], [Rpss[0]])
                        self.ts(V, rs[:, 0:n], pss[0][:, 0:n], 1.0 / 128, EPS, ALU.mult, ALU.add, [Rpss[0]], [Rrs])
                        self.rsqrt_(rs[:, 0:n], rs[:, 0:n], n, [Rrs], [Rrs])
                        self.stt(ckvT[:, k0:k0 + n], pa[:, 0:n], self.vecs[:, vo["kvg"]:vo["kvg"] + 1], rs[:, 0:n], ALU.mult,
                                 ALU.mult, [Rpa, self.Rvec, Rrs], [Rckv[kt0 + i] for i in range(nt)])
                        pa, Rpa = pp[1], Rpp[1]
                        pb, Rpb = pp[2], Rpp[2]
                        self.proj_fm(pa, Rpa, wts[19], hT, RhT, 0, n, M=96)
                        self.proj_fm(pb, Rpb, wts[20], hT, RhT, 0, n, M=96)
                        self.tt(V, ta[64:96, 0:n], pa[64:96, 0:n], rope[64:96, 0, 0:n], ALU.mult, [Rpa, Rrope], [Rta])
                        self.tt(V, tb[64:96, 0:n], pb[64:96, 0:n], rope[64:96, 1, 0:n], ALU.mult, [Rpb, Rrope], [Rtb])
                        self.tt(V, KT[64:96, k0:k0 + n], ta[64:96, 0:n], tb[64:96, 0:n], ALU.add, [Rta, Rtb],
                                [RKTr[kt0 + i] for i in range(nt)])
                        need_q = (seg == "own") or (seg == "ctx" and not last)
                        if need_q:
                            qc0 = (0 if seg == "ctx" else c.CTX) + t0 * 128
                            qt0 = (0 if seg == "ctx" else c.NC) + t0
                            pq = [pp[3], pp[0]]
                            Rpq = [Rpp[3], Rpp[0]]
                            for ch in range(2):
                                self.proj_fm(pq[ch], Rpq[ch], wts[16 + ch], hT, RhT, 0, n)
                                self.act(sq[:, ch, 0:n], pq[ch][:, 0:n], AF.Square, [Rpq[ch]], [Rsq])
                            for ch in range(2):
                                self.mm(pss[1][:, 0:n], self.onesb[:], sq[:, ch, 0:n], ch == 0, ch == 1, [Rsq, self.Rc], [Rpss[1]])
                            self.ts(V, rs[:, 0:n], pss[1][:, 0:n], 1.0 / 256, EPS, ALU.mult, ALU.add, [Rpss[1]], [Rrs])
                            self.rsqrt_(rs[:, 0:n], rs[:, 0:n], n, [Rrs], [Rrs])
                            for ch in range(2):
                                self.stt(cqT[:, ch, qc0:qc0 + n], pq[ch][:, 0:n], self.vecs[:, vo["qg"] + ch:vo["qg"] + ch + 1],
                                         rs[:, 0:n], ALU.mult, ALU.mult, [Rpq[ch], self.Rvec, Rrs],
                                         [Rcq[qt0 + i] for i in range(nt)])
                em.flush()
            with contextlib.ExitStack() as at:
                QT = self.T(at, [128, NQ], BF16, "QT")
                RQT = [Res() for _ in range(c.NT + c.NC)]
                Vh = self.T(at, [128, c.NKT, 65], BF16, "Vh")
                RVh = Res()
                mrope = self.T(at, [128, 2, NQ], F32, "mrope")
                Rmr = Res()
                em.dma_group(SP, [(mrope[64:96, :, 0:c.CTX], self.cur_rope[64:96, 2:4, 0:c.CTX]),
                                  (mrope[64:96, :, c.CTX:NQ], self.cur_rope[64:96, 2:4, c.CTX + c.TH:c.NK])],
                             writes=[Rmr], semkey="mrope")
                Otok = self.T(at, [128, c.NT + c.NC, 512], BF16, "Otok")
                ROt = [Res() for _ in range(c.NT + c.NC)]
                wuq = self.T(at, [128, 2, 2048], BF16, "wuq")
                wukv = self.T(at, [128, 1024], BF16, "wukv")
                Rwu = Res()
                em.dma_group(G, [(wuq[:, k, :], self.W["wuq"][k]) for k in range(2)] + [(wukv[:], self.W["wukv"])],
                             writes=[Rwu], semkey="wuq")
                em.op(G, lambda e: e.memset(Vh[:, :, 64:65], 1.0), writes=[RVh])
                with contextlib.ExitStack() as sp:
                    pstp = [self.P(sp, [128, 1024], F32, "pstp") for _ in range(2)]
                    Rpst = [Res(), Res()]
                    po = [self.P(sp, [128, 512], F32, "po") for _ in range(GT)]
                    Rpo = [Res() for _ in range(GT)]
                    Rpw = Rpst
                    NP_ = 3
                    pb_ = [self.T(sp, [128, 2, 512], BF16, "pbuf") for _ in range(NP_)]
                    Rpb = [Res() for _ in range(NP_)]
                    ta = self.T(sp, [128, QG], F32, "ta")
                    tb = self.T(sp, [128, QG], F32, "tb")
                    Rta, Rtb = Res(), Res()
                    rin = self.T(sp, [128, 2], F32, "rin")
                    Rrin = Res()
                    scale = 96.0 ** -0.5
                    cnt = 0
                    wcnt = 0
                    qsegs = [("own", c.NG, list(range(c.NKT)))] + ([] if last else [("ctx", 1, list(range(c.NC)))])
                    for h in range(8):
                        for k0 in range(0, c.NK, 512):
                            n = min(512, c.NK - k0)
                            w_ = wcnt % 2
                            wcnt += 1
                            self.mm(pstp[w_][0:64, 0:n], wukv[:, h * 128:h * 128 + 64], ckvT[:, k0:k0 + n], True, True,
                                    [Rwu] + [Rckv[k0 // 128 + i] for i in range(n // 128)], [Rpw[w_]])
                            self.cp(S if (wcnt % 2) else V, KT[0:64, k0:k0 + n], pstp[w_][0:64, 0:n], [Rpw[w_]], [RKTn])
                        for kt0 in range(0, c.NKT, 8):
                            nt8 = min(8, c.NKT - kt0)
                            w_ = wcnt % 2
                            wcnt += 1
                            for i in range(nt8):
                                kt = kt0 + i
                                self.mm(pstp[w_][:, i * 64:(i + 1) * 64], ckvT[:, kt * 128:(kt + 1) * 128],
                                        wukv[:, h * 128 + 64:(h + 1) * 128], True, True, [Rwu, Rckv[kt]], [Rpw[w_]])
                            self.cp(V, Vh[:, kt0:kt0 + nt8, 0:64], pstp[w_][:, 0:nt8 * 64].rearrange("p (t d) -> p t d", d=64),
                                    [Rpw[w_]], [RVh])
                        for seg, ng, keys in qsegs:
                            for gidx in range(ng):
                                nt = GT if seg == "own" else c.NC
                                n = nt * 128
                                qc0 = (c.CTX + gidx * QG) if seg == "own" else 0
                                qt0 = (c.NC + gidx * GT) if seg == "own" else 0
                                Rq_ = [RQT[qt0 + i] for i in range(nt)]
                                pa, Rpa = pstp[0], Rpw[0]
                                pb2, Rpb2 = pstp[1], Rpw[1]
                                for (pq, Rpq, s_) in ((pa, Rpa, 0), (pb2, Rpb2, 1)):
                                    c0 = (h * 2 + s_) * 128
                                    for k in range(2):
                                        self.mm(pq[0:96, 0:n], wuq[:, k, c0:c0 + 96], cqT[:, k, qc0:qc0 + n], k == 0, k == 1,
                                                [Rwu] + [Rcq[qt0 + i] for i in range(nt)], [Rpq])
                                self.cp(S, QT[0:64, qc0:qc0 + n], pa[0:64, 0:n], [Rpa], Rq_)
                                self.tt(V, ta[64:96, 0:n], pa[64:96, 0:n], mrope[64:96, 0, qc0:qc0 + n], ALU.mult, [Rpa, Rmr], [Rta])
                                self.tt(V, tb[64:96, 0:n], pb2[64:96, 0:n], mrope[64:96, 1, qc0:qc0 + n], ALU.mult, [Rpb2, Rmr], [Rtb])
                                self.tt(V, QT[64:96, qc0:qc0 + n], ta[64:96, 0:n], tb[64:96, 0:n], ALU.add, [Rta, Rtb], Rq_)
                        for seg, ng, keys in qsegs:
                            for gidx in range(ng):
                                nt = GT if seg == "own" else c.NC
                                n = nt * 128
                                qc0 = (c.CTX + gidx * QG) if seg == "own" else 0
                                qt0 = (c.NC + gidx * GT) if seg == "own" else 0
                                Rq_ = [RQT[qt0 + i] for i in range(nt)]
                                pairs = [keys[i:i + 2] for i in range(0, len(keys), 2)]
                                base = cnt
                                cnt += len(pairs)

                                def st_pair(pi):
                                    sb = (base + pi) % 2
                                    for rep in range(1 + DUP_ST_MLA):
                                        for j, kt in enumerate(pairs[pi]):
                                            self.mm(pstp[sb][:, j * 512:j * 512 + n], KT[0:96, kt * 128:(kt + 1) * 128],
                                                    QT[0:96, qc0:qc0 + n], True, True, [RKTn, RKTr[kt]] + Rq_, [Rpst[sb]])
                                st_pair(0)
                                if len(pairs) > 1:
                                    st_pair(1)
                                for pi, pr in enumerate(pairs):
                                    sb = (base + pi) % 2
                                    pbi = (base + pi) % NP_
                                    np_ = len(pr)
                                    self.act(pb_[pbi][:, 0:np_, 0:n],
                                             pstp[sb][:, :].rearrange("p (j c) -> p j c", j=2)[:, 0:np_, 0:n], AF.Exp,
                                             [Rpst[sb]], [Rpb[pbi]], scale=scale)
                                    if pi + 2 < len(pairs):
                                        st_pair(pi + 2)
                                    for j, kt in enumerate(pr):
                                        first = (pi == 0 and j == 0)
                                        lastk = (pi == len(pairs) - 1 and j == np_ - 1)
                                        for i in range(nt):
                                            self.mm(po[i][:, 0:65], pb_[pbi][:, j, i * 128:(i + 1) * 128], Vh[:, kt, 0:65],
                                                    first, lastk, [Rpb[pbi], RVh], [Rpo[i]])
                                for i in range(nt):
                                    self.em.op(V, lambda e_, i=i: e_.reciprocal(out=rin[:, 0:1], in_=po[i][:, 64:65]), [Rpo[i]], [Rrin])
                                    self.ts(V, Otok[:, qt0 + i, h * 64:(h + 1) * 64], po[i][:, 0:64], rin[:, 0:1], None, ALU.mult,
                                            None, [Rpo[i], Rrin], [ROt[qt0 + i]])
                    em.flush()
                with contextlib.ExitStack() as sp:
                    ptb = [self.P(sp, [128, 4, 128], BF16, "ptr") for _ in range(2)]
                    Rptb = [Res(), Res()]
                    for ti in range(c.NT + (0 if last else c.NC)):
                        b = ti % 2
                        tq = (ti + c.NC) if ti < c.NT else (ti - c.NT)
                        for q in range(4):
                            self.tr(ptb[b][:, q, :], Otok[:, tq, q * 128:(q + 1) * 128], self.ident[:], [ROt[tq], self.Rc], [Rptb[b]])
                        if ti < c.NT:
                            self.cp(S if ti % 2 else V, self.mixT[:, 4:8, ti * 128:(ti + 1) * 128], ptb[b][:], [Rptb[b]],
                                    [self.Rmix[4 + q][ti] for q in range(4)])
                        else:
                            tc_ = ti - c.NT
                            self.cp(S if ti % 2 else V, self.mixTc[:, 4:8, tc_ * 128:(tc_ + 1) * 128], ptb[b][:], [Rptb[b]],
                                    [self.Rmixc[4 + q][tc_] for q in range(4)])
                    em.flush()

    def resid_update(self, sp, py, Rpy, xt, Rxt, gbc, Rgbc, st, Rst, junk, Rjunk, tmp, Rtmp):
        c = self.c
        hw = c.D // 2
        for j in range(2):
            self.act(junk[:, 0:hw], py[j][:, 0:hw], AF.Square, [Rpy[j]], [Rjunk, Rst], accum_out=st[:, j:j + 1])
        self.tt(V, st[:, 2:3], st[:, 0:1], st[:, 1:2], ALU.add, [Rst], [Rst])
        self.ts(V, st[:, 2:3], st[:, 2:3], 1.0 / c.D, EPS, ALU.mult, ALU.add, [Rst], [Rst])
        self.rsqrt_(st[:, 3:4], st[:, 2:3], 1, [Rst], [Rst])
        for j in range(2):
            self.stt(tmp[:, 0:hw], py[j][:, 0:hw], st[:, 3:4], gbc[:, j * hw:(j + 1) * hw], ALU.mult, ALU.mult,
                     [Rpy[j], Rst, Rgbc], [Rtmp])
            self.tt(V, xt[:, j * hw:(j + 1) * hw], xt[:, j * hw:(j + 1) * hw], tmp[:, 0:hw], ALU.add, [Rxt, Rtmp], [Rxt])

    def phase_out(self, last):
        c, em = self.c, self.em
        hw = c.D // 2
        with contextlib.ExitStack() as sp:
            wout = self.T(sp, [128, 8, c.D], BF16, "wout")
            Rw = Res()
            em.dma_group(G, [(wout[:, k, :], self.W["wout"][k]) for k in range(8)], writes=[Rw], semkey="wout")
            pbank = self.P(sp, [128, 512], F32, "pbk")
            Rpbk = Res()
            py = [[self.P(sp, [128, 512], F32, "py") for _ in range(2)] for _ in range(2)]
            Rpy = [[Res(), Res()], [Res(), Res()]]
            st = self.T(sp, [128, 4], F32, "st")
            Rst = Res()
            junk = self.T(sp, [128, 512], BF16, "junk")
            Rjunk = Res()
            tmp = self.T(sp, [128, 512], F32, "tmp")
            Rtmp = Res()
            segs = [("own", 0, c.NT)] + ([] if last else [("ctx", 1, c.NC)])
            ti = 0
            for seg, s_, ntile in segs:
                gbc, Rgbc = self.bcast_vec(sp, 2, s_, pbank, Rpbk)
                mixT, Rmix = (self.mixT, self.Rmix) if seg == "own" else (self.mixTc, self.Rmixc)
                for t in range(ntile):
                    b = ti % 2
                    ti += 1
                    for j in range(2):
                        for k in range(8):
                            self.mm(py[b][j][:, 0:hw], mixT[:, k, t * 128:(t + 1) * 128], wout[:, k, j * hw:(j + 1) * hw],
                                    k == 0, k == 7, [Rmix[k][t], Rw], [Rpy[b][j]])
                    xt, Rxt = (self.x[:, t, :], self.Rx[t]) if seg == "own" else (self.xc[:, t, :], self.Rxc[t])
                    self.resid_update(sp, py[b], Rpy[b], xt, Rxt, gbc, Rgbc, st, Rst, junk, Rjunk, tmp, Rtmp)
            em.flush()

    def phase_ffn(self, last, post_tile=None):
        c, em = self.c, self.em
        hw = c.D // 2
        QG = c.QG
        with contextlib.ExitStack() as sp:
            w2 = self.T(sp, [128, c.FC, c.D], BF16, "w2")
            Rw2 = Res()
            em.dma_group(G, [(w2[:, f, :], self.W["w2"][f]) for f in range(c.FC)], writes=[Rw2], semkey="w2")
            NW = 3
            w1 = [self.T(sp, [128, c.KC, 256], BF16, "w1") for _ in range(NW)]
            Rw1 = [Res() for _ in range(NW)]
            nctx = self.make_norm_ctx(sp)
            hT = self.T(sp, [128, c.KC, QG], BF16, "hT")
            RhT = Res()
            hid = self.T(sp, [128, c.FC, QG], BF16, "hid")
            Rhid = [Res() for _ in range(c.FC)]
            pu = [self.P(sp, [128, 512], F32, "pu") for _ in range(2)]
            Rpu = [Res(), Res()]
            pgs = [self.P(sp, [128, 512], F32, "pg") for _ in range(2)]
            Rpgs = [Res(), Res()]
            pg, Rpg = pgs[0], Rpgs[0]
            py = [self.P(sp, [128, 512], F32, "py") for _ in range(2)]
            Rpy = [Res(), Res()]
            sgt = [self.T(sp, [128, QG], BF16, "sg") for _ in range(2)]
            Rsg = [Res(), Res()]
            st = self.T(sp, [128, 4], F32, "st")
            Rst = Res()
            junk = self.T(sp, [128, 512], BF16, "junk")
            Rjunk = Res()
            tmp = self.T(sp, [128, 512], F32, "tmp")
            Rtmp = Res()
            segs = [("own", 0)] + ([] if last else [("ctx", 1)])
            wi = 0
            hT2 = self.T(sp, [128, c.KC, QG], BF16, "hT2")
            hTs, RhTs = [hT, hT2], [RhT, Res()]
            glist = []
            for seg, s_ in segs:
                for (t0, nt) in self.seg_groups(seg):
                    glist.append((seg, s_, t0, nt))

            def emit_norm(gi):
                seg, s_, t0, nt = glist[gi]
                self.norm_items(nctx, [(seg, t0 + i, hTs[gi % 2], i * 128, RhTs[gi % 2], 3, None) for i in range(nt)])
            emit_norm(0)
            gbc = Rgbc = None
            cur_seg = None
            for gi, (seg, s_, t0, nt) in enumerate(glist):
                if seg != cur_seg:
                    gbc, Rgbc = self.bcast_vec(sp, 5, s_, pg, Rpg)
                    cur_seg = seg
                hTg, RhTg = hTs[gi % 2], RhTs[gi % 2]
                n = nt * 128
                for f in range(c.FC):
                    wb = wi % NW
                    wi += 1
                    self.load_w(w1[wb][:], self.W["w1"][f].rearrange("p (k n) -> p k n", k=c.KC), Rw1[wb], f"w1_{wb}")
                    ub = f % 2
                    for k in range(c.KC):
                        self.mm(pu[ub][:, 0:n], w1[wb][:, k, 0:128], hTg[:, k, 0:n], k == 0, k == c.KC - 1, [Rw1[wb], RhTg],
                                [Rpu[ub]])
                    for k in range(c.KC):
                        self.mm(pgs[ub][:, 0:n], w1[wb][:, k, 128:256], hTg[:, k, 0:n], k == 0, k == c.KC - 1, [Rw1[wb], RhTg],
                                [Rpgs[ub]])
                    self.act(sgt[ub][:, 0:n], pgs[ub][:, 0:n], AF.Silu, [Rpgs[ub]], [Rsg[ub]])
                    self.tt(V, hid[:, f, 0:n], pu[ub][:, 0:n], sgt[ub][:, 0:n], ALU.mult, [Rpu[ub], Rsg[ub]], [Rhid[f]])
                if gi + 1 < len(glist):
                    emit_norm(gi + 1)
                for i in range(nt):
                    t = t0 + i
                    for j in range(2):
                        for f in range(c.FC):
                            self.mm(py[j][:, 0:hw], hid[:, f, i * 128:(i + 1) * 128], w2[:, f, j * hw:(j + 1) * hw],
                                    f == 0, f == c.FC - 1, [Rhid[f], Rw2], [Rpy[j]])
                    xt, Rxt = (self.x[:, t, :], self.Rx[t]) if seg == "own" else (self.xc[:, t, :], self.Rxc[t])
                    self.resid_update(sp, py, Rpy, xt, Rxt, gbc, Rgbc, st, Rst, junk, Rjunk, tmp, Rtmp)
                    if post_tile is not None and seg == "own":
                        post_tile(t)
            em.flush()


_PROGRAMS = {}


def _get_program(cfg, layers, out_ctx, fused=False):
    key = (cfg.D, cfg.TH, cfg.CTX, cfg.DFF, cfg.GW, cfg.QG, tuple(layers), out_ctx, fused)
    if key not in _PROGRAMS:
        _PROGRAMS[key] = Builder(cfg, layers, out_ctx, fused).build()
    return _PROGRAMS[key]


def _core_inputs(cfg, inp, b, h, x_full, xc_full, layers):
    c = cfg
    d = _core_tables(c, h)
    d["x_own"] = np.ascontiguousarray(x_full[b, h * c.TH:(h + 1) * c.TH])
    d["x_oth"] = np.ascontiguousarray(x_full[b, (1 - h) * c.TH:(2 - h) * c.TH])
    d["xc"] = np.ascontiguousarray(xc_full[b])
    cc = np.stack([inp["c"][b], inp["c_ctx"]], -1)
    d["cc"] = np.ascontiguousarray(cc.reshape(c.KC, 128, 2).transpose(1, 0, 2))
    return d


def run_layers(cfg, inp, layer_groups, runner=None, fused=False):
    c = cfg
    inp = {k: np.asarray(v, dtype=np.float32) for k, v in inp.items()}
    x = inp["x"]
    xc = inp["ctx"]
    for layers in layer_groups:
        out_ctx = 0 if layers[-1] == c.DEPTH - 1 else 1
        nc = _get_program(c, layers, out_ctx, fused)
        wl = {}
        for l in layers:
            wl.update(_layer_weights(c, inp, l))
        in_maps = []
        for core in range(2 * c.BATCH):
            b, h = core // 2, core % 2
            d = _core_inputs(c, inp, b, h, x, xc, layers)
            if fused:
                tb = _core_tables(c, 1 - h)
                d["flB"], d["ropetabB"], d["cxB"] = tb["fl"], tb["ropetab"], tb["cx"]
            d.update(wl)
            in_maps.append(d)
        if runner is None:
            res = run_bass_kernel_spmd(nc, in_maps, core_ids=list(range(2 * c.BATCH))).results
        else:
            res = runner(nc, in_maps)
        xn = np.zeros_like(x)
        xcn = xc.copy()
        for core in range(2 * c.BATCH):
            b, h = core // 2, core % 2
            xn[b, h * c.TH:(h + 1) * c.TH] = res[core]["x_out"]
            if out_ctx and h == 0:
                xcn[b] = res[core]["xc_out"]
        x, xc = xn, xcn
    return x


def kernel(**inputs):
    cfg = Cfg()
    out = run_layers(cfg, inputs, [list(range(cfg.DEPTH))], fused=True)
    return np.asarray(out, dtype=np.float32)
```

```python
import contextlib
import math
import numpy as np
import concourse.bass as bass
import concourse.mybir as mybir
from concourse.bass_utils import run_bass_kernel_spmd

F32 = mybir.dt.float32
BF16 = mybir.dt.bfloat16
AF = mybir.ActivationFunctionType
ALU = mybir.AluOpType
ENGINES = ("tensor", "vector", "scalar", "gpsimd", "sync")
PE, V, S, G, SP = ENGINES
EPS = 1e-6
ROPE_BASE = 10000.0
DUP_ST_MLA = 0
DUP_ST_RET = 0
FAST_RSQRT = 0
RET_SPLIT = 0
CONV_K = 31
HALO = 15
NWC = 21


class Cfg:
    def __init__(s, D=1024, TH=2048, CTX=256, DFF=2816, GW=64, BATCH=4, DEPTH=2, QG=512):
        s.D, s.TH, s.CTX, s.DFF, s.GW, s.BATCH, s.DEPTH = D, TH, CTX, DFF, GW, BATCH, DEPTH
        s.KC, s.NT, s.NC, s.FC = D // 128, TH // 128, CTX // 128, DFF // 128
        s.T = 2 * TH
        s.QG = min(QG, TH)
        s.NG = TH // s.QG
        s.GT = s.QG // 128
        s.NK = CTX + 2 * TH
        s.NKT = s.NK // 128
        s.MG = 6 * D // 512
        o = 0
        s.vo = {}
        for name, n in (("pre1", s.KC), ("post1", s.KC), ("pre2", s.KC), ("post2", s.KC), ("modb", 6 * s.KC),
                        ("convb", 2), ("lng", 2), ("lnb", 2), ("convw", 2 * CONV_K), ("qg", 2), ("kvg", 1)):
            s.vo[name] = o
            o += n
        s.NV = o


class Res:
    __slots__ = ("name", "w", "r")

    def __init__(self, name=""):
        self.name = name
        self.w = None
        self.r = []


class Emit:
    def __init__(self, nc, stack):
        self.nc = nc
        self.stack = stack
        self.q = {e: [] for e in ENGINES}
        self.sems = {}
        self.count = {}
        self.seen = {e: {} for e in ENGINES}
        for e in ENGINES:
            self._mksem("E_" + e)
        self.nblk = 0
        self.phase_keys = set()

    def _mksem(self, key):
        if key not in self.sems:
            self.sems[key] = self.stack.enter_context(self.nc.semaphore("s_" + key))
            self.count[key] = 0
        return key

    def _wait(self, eng, ev):
        if ev is None:
            return
        key, val = ev
        if eng == PE and key == "E_tensor":
            return
        if self.seen[eng].get(key, 0) >= val:
            return
        self.seen[eng][key] = val
        sem = self.sems[key]
        self.q[eng].append(lambda e, sem=sem, val=val: e.wait_ge(sem, val))

    def _deps(self, eng, reads, writes):
        for r in reads:
            self._wait(eng, r.w)
        for w in writes:
            self._wait(eng, w.w)
            for ev in w.r:
                self._wait(eng, ev)

    def _commit(self, ev, reads, writes):
        for r in reads:
            r.r.append(ev)
        for w in writes:
            w.w = ev
            w.r = []

    def op(self, eng, fn, reads=(), writes=()):
        self._deps(eng, reads, writes)
        key = "E_" + eng
        self.count[key] += 1
        val = self.count[key]
        sem = self.sems[key]
        self.q[eng].append(lambda e, fn=fn, sem=sem: fn(e).then_inc(sem, 1))
        self._commit((key, val), reads, writes)

    def dma(self, queue, out, in_, reads=(), writes=(), semkey="d", **kw):
        self._deps(queue, reads, writes)
        key = self._mksem("D_" + semkey)
        self.phase_keys.add(key)
        self.count[key] += 16
        val = self.count[key]
        sem = self.sems[key]
        self.q[queue].append(
            lambda e, out=out, in_=in_, sem=sem, kw=kw: e.dma_start(out=out, in_=in_, **kw).then_inc(sem, 16))
        ev = (key, val)
        self._commit(ev, reads, writes)
        return ev

    def dma_group(self, queue, pairs, reads=(), writes=(), semkey="d", **kw):
        self._deps(queue, reads, writes)
        key = self._mksem("D_" + semkey)
        self.phase_keys.add(key)
        sem = self.sems[key]
        for (out, in_) in pairs:
            self.count[key] += 16
            self.q[queue].append(
                lambda e, out=out, in_=in_, sem=sem, kw=kw: e.dma_start(out=out, in_=in_, **kw).then_inc(sem, 16))
        ev = (key, self.count[key])
        self._commit(ev, reads, writes)
        return ev

    def wait_event(self, eng, ev):
        self._wait(eng, ev)

    def flush(self):
        nc = self.nc
        if not any(self.q[e] for e in ENGINES):
            return
        for key in sorted(self.phase_keys):
            self._wait(SP, (key, self.count[key]))
        self.phase_keys = set()
        with nc.Block() as block:
            for e in ENGINES:
                lst = self.q[e]
                if not lst:
                    continue

                def body(eng, lst=lst):
                    for f in lst:
                        f(eng)
                getattr(block, e)(body)
        self.q = {e: [] for e in ENGINES}
        self.nblk += 1


def _perm(n_head_dim):
    half = n_head_dim // 2
    q = half // 2
    p = np.zeros(n_head_dim, np.int64)
    for i in range(n_head_dim):
        j = i % half
        base = i - j
        p[i] = base + (j + q if j < q else j - q)
    return p


def _rope_tables(cfg, pos_list, hd):
    half = hd // 2
    f = half // 2
    n = len(pos_list)
    cos = np.ones((hd, n), np.float64)
    sin = np.zeros((hd, n), np.float64)
    pos = np.asarray(pos_list)
    lat = pos >= 0
    row = (pos // cfg.GW).astype(np.float64)
    col = (pos % cfg.GW).astype(np.float64)
    for i in range(hd):
        j = i % half
        part = i // half
        fi = j % f
        sign = -1.0 if j < f else 1.0
        inv = np.float32(ROPE_BASE) ** (-np.float32(fi) / np.float32(f))
        ang = (row if part == 0 else col).astype(np.float32) * np.float32(inv)
        cos[i, lat] = np.cos(ang[lat])
        sin[i, lat] = sign * np.sin(ang[lat])
    return cos.astype(np.float32), sin.astype(np.float32)


def _kt_lists(cfg):
    c = cfg
    lat = []
    nF = nB = nO = 0
    for g in range(c.NG):
        lst = []
        for j in range(c.NC):
            lst.append((j, "F", nF)); nF += 1
        for j in range(c.NC):
            lst.append((j, "B", nB)); nB += 1
        for i in range(c.NT):
            lst.append((c.NC + i, "O", nO)); nO += 1
        for i in range(c.NT):
            kt = c.NC + c.NT + i
            if i < g * c.GT:
                lst.append((kt, "F", nF)); nF += 1
            elif i >= (g + 1) * c.GT:
                lst.append((kt, "B", nB)); nB += 1
            else:
                lst.append((kt, "D", i - g * c.GT))
        lat.append(lst)
    ctxl = [(j, "D", j) for j in range(c.NC)]
    return lat, ctxl, nF, nB, nO


def _core_tables(cfg, h):
    c = cfg
    P0 = h * c.TH
    Q0 = (1 - h) * c.TH
    pos = [-1] * c.CTX + list(range(Q0, Q0 + c.TH)) + list(range(P0, P0 + c.TH))
    rc, rs = _rope_tables(c, pos, 64)
    mc, ms = _rope_tables(c, pos, 32)
    rope = np.zeros((128, 4, c.NK), np.float32)
    rope[:, 0] = np.concatenate([rc, rc], 0)
    rope[:, 1] = np.concatenate([rs, rs], 0)
    rope[:, 2] = 1.0
    rope[64:96, 2] = mc
    rope[64:96, 3] = ms
    s = np.arange(128)[:, None, None]
    j = np.arange(c.GT)[None, :, None]
    t = np.arange(c.QG)[None, None, :]
    dtab = (t - 128 * j - s).astype(np.float32)
    lat, ctxl, nF, nB, nO = _kt_lists(c)
    cx = np.zeros(nF + nB + nO, np.float32)
    for g in range(c.NG):
        gb = P0 + g * c.QG
        for (kt, kind, e) in lat[g]:
            if kt < c.NC:
                kbF = -c.CTX + 128 * kt
                kbB = c.T + 128 * kt
            elif kt < c.NC + c.NT:
                kbF = kbB = Q0 + 128 * (kt - c.NC)
            else:
                kbF = kbB = P0 + 128 * (kt - c.NC - c.NT)
            if kind == "F":
                cx[e] = gb - kbF
            elif kind == "B":
                cx[nF + e] = kbB - gb
            elif kind == "O":
                cx[nF + nB + e] = abs(gb - kbF)
    cxr = np.broadcast_to(cx[None, :], (128, cx.size)).copy()
    fl = np.zeros((128, 2), np.float32)
    fl[:, 0] = 1.0 if h == 1 else 0.0
    fl[:, 1] = 1.0 if h == 0 else 0.0
    return dict(ropetab=rope, dtab=dtab, cx=cxr, fl=fl)


def _fm(v):
    return np.ascontiguousarray(v.reshape(-1, 128).T)


def _layer_weights(cfg, inp, l):
    c = cfg
    w_in = inp["w_in"][l]
    p64, p32 = _perm(64), _perm(32)
    a = w_in[:, 0:512]
    q = w_in[:, 512:768]
    k = w_in[:, 768:1024]
    v = w_in[:, 1024:1280]
    g = w_in[:, 1280:1536]
    cq = w_in[:, 1536:1792]
    ckv = w_in[:, 1792:1920]
    kr = w_in[:, 1920:1952]
    hp = np.concatenate([h * 64 + p64 for h in range(4)])
    krc = np.concatenate([ckv[:, 0:64], kr, kr], 1)
    krs = np.concatenate([ckv[:, 0:64], kr[:, p32], kr[:, p32]], 1)
    cat = np.concatenate([a, q, q[:, hp], k, k[:, hp], v, g, cq, ckv, krc, krs], 1)
    assert cat.shape[1] == NWC * 128
    win = np.ascontiguousarray(cat.reshape(c.KC, 128, NWC, 128).transpose(2, 1, 0, 3)).reshape(NWC, 128, c.KC * 128)
    uq = inp["mla_w_uq"][l].reshape(256, 8, 96)
    uqs = uq.copy()
    uqs[:, :, 64:96] = uq[:, :, 64:96][:, :, p32]
    pad = np.zeros((256, 8, 32), np.float32) + uq[:, :, 0:32]
    uqc = np.stack([np.concatenate([uq, pad], 2), np.concatenate([uqs, pad], 2)], 2)
    wuq = np.ascontiguousarray(uqc.reshape(2, 128, 16 * 128))
    wukv = np.ascontiguousarray(inp["mla_w_ukv"][l])
    wout = np.ascontiguousarray(inp["w_out"][l].reshape(8, 128, c.D))
    f1 = inp["ffn_w_in"][l]
    u1, g1 = f1[:, :c.DFF], f1[:, c.DFF:]
    w1 = np.stack([u1.reshape(c.KC, 128, c.FC, 128), g1.reshape(c.KC, 128, c.FC, 128)], 3)
    w1 = np.ascontiguousarray(w1.transpose(2, 1, 0, 3, 4)).reshape(c.FC, 128, c.KC * 256)
    w2 = np.ascontiguousarray(inp["ffn_w_out"][l].reshape(c.FC, 128, c.D))
    mw = inp["mod_w"][l]
    modw = np.ascontiguousarray(mw.reshape(c.KC, 128, c.MG, 512).transpose(2, 1, 0, 3)).reshape(c.MG, 128, c.KC * 512)
    vecs = np.zeros((128, c.NV), np.float32)
    vo = c.vo
    vecs[:, vo["pre1"]:vo["pre1"] + c.KC] = _fm(inp["pre1_g"][l])
    vecs[:, vo["post1"]:vo["post1"] + c.KC] = _fm(inp["post1_g"][l])
    vecs[:, vo["pre2"]:vo["pre2"] + c.KC] = _fm(inp["pre2_g"][l])
    vecs[:, vo["post2"]:vo["post2"] + c.KC] = _fm(inp["post2_g"][l])
    vecs[:, vo["modb"]:vo["modb"] + 6 * c.KC] = _fm(inp["mod_b"][l])
    vecs[:, vo["convb"]:vo["convb"] + 2] = _fm(inp["conv_b"][l])
    vecs[:, vo["lng"]:vo["lng"] + 2] = _fm(inp["conv_ln_g"][l])
    vecs[:, vo["lnb"]:vo["lnb"] + 2] = _fm(inp["conv_ln_b"][l])
    cw = inp["conv_w"][l]
    vecs[:, vo["convw"]:vo["convw"] + 2 * CONV_K] = np.ascontiguousarray(
        cw.T.reshape(2, 128, CONV_K).transpose(1, 0, 2)).reshape(128, 2 * CONV_K)
    vecs[:, vo["qg"]:vo["qg"] + 2] = _fm(inp["mla_q_norm_g"][l])
    vecs[:, vo["kvg"]:vo["kvg"] + 1] = _fm(inp["mla_kv_norm_g"][l])
    rows = np.broadcast_to(inp["ret_gn_g"][l][None, :], (128, 256)).copy()
    lgv = np.broadcast_to(inp["ret_log_decay"][l].reshape(1, 8), (128, 8)).copy()
    d = dict(win=win, wuq=wuq, wukv=wukv, wout=wout, w1=w1, w2=w2, modw=modw, vecs=vecs, rows=rows, lg=lgv)
    return {f"{k}{l}": np.ascontiguousarray(v_, dtype=np.float32) for k, v_ in d.items()}


class Builder:
    def __init__(self, cfg, layers, n_out_ctx, fused=False):
        self.c = cfg
        self.layers = layers
        self.n_out_ctx = n_out_ctx
        self.fused = fused
        self.nc = bass.Bass("TRN2", target_bir_lowering=False)
        self.uid = 0

    def T(self, st, shape, dt, name=None):
        self.uid += 1
        return st.enter_context(self.nc.sbuf_tensor(f"{name or 't'}_{self.uid}", shape, dt))

    def P(self, st, shape, dt, name=None):
        self.uid += 1
        return st.enter_context(self.nc.psum_tensor(f"{name or 'p'}_{self.uid}", shape, dt))

    def dram_in(self, name, shape, dt=F32):
        return self.nc.dram_tensor(name, list(shape), dt, kind="ExternalInput").ap()

    def mm(self, out, lhsT, rhs, start, stop, reads, writes):
        self.em.op(PE, lambda e: e.matmul(out, lhsT=lhsT, rhs=rhs, start=start, stop=stop), reads, writes)

    def tr(self, out, in_, ident, reads, writes):
        self.em.op(PE, lambda e: e.transpose(out=out, in_=in_, identity=ident), reads, writes)

    def act(self, out, in_, func, reads, writes, **kw):
        self.em.op(S, lambda e: e.activation(out=out, in_=in_, func=func, **kw), reads, writes)

    def tt(self, eng, out, in0, in1, op, reads, writes):
        self.em.op(eng, lambda e: e.tensor_tensor(out=out, in0=in0, in1=in1, op=op), reads, writes)

    def ts(self, eng, out, in0, s1, s2, op0, op1, reads, writes):
        if op1 is None:
            self.em.op(eng, lambda e: e.tensor_scalar(out=out, in0=in0, scalar1=s1, scalar2=None, op0=op0), reads, writes)
        else:
            self.em.op(eng, lambda e: e.tensor_scalar(out=out, in0=in0, scalar1=s1, scalar2=s2, op0=op0, op1=op1),
                       reads, writes)

    def stt(self, out, in0, scalar, in1, op0, op1, reads, writes):
        self.em.op(V, lambda e: e.scalar_tensor_tensor(out=out, in0=in0, scalar=scalar, in1=in1, op0=op0, op1=op1),
                   reads, writes)

    def cp(self, eng, out, in_, reads, writes):
        if eng == S:
            self.em.op(S, lambda e: e.copy(out=out, in_=in_), reads, writes)
        else:
            self.em.op(eng, lambda e: e.tensor_copy(out=out, in_=in_), reads, writes)

    def rsqrt_(self, out, in_, width, reads, writes):
        self.em.op(S, lambda e: e.activation(out=out, in_=in_, func=AF.Sqrt), list(reads), writes)
        self.em.op(V, lambda e: e.reciprocal(out=out, in_=out), list(writes), writes)

    def build(self):
        c, nc = self.c, self.nc
        with contextlib.ExitStack() as top:
            self.em = Emit(nc, top)
            em = self.em
            self.d_xown = self.dram_in("x_own", [c.TH, c.D])
            self.d_xoth = self.dram_in("x_oth", [c.TH, c.D])
            self.d_xc = self.dram_in("xc", [c.CTX, c.D])
            self.d_cc = self.dram_in("cc", [128, c.KC, 2])
            self.d_fl = self.dram_in("fl", [128, 2])
            self.d_rope = self.dram_in("ropetab", [128, 4, c.NK])
            self.d_dtab = self.dram_in("dtab", [128, c.GT, c.QG])
            lat, ctxl, nF, nB, nO = _kt_lists(c)
            self.ktl_lat, self.ktl_ctx, self.nF, self.nB, self.nO = lat, ctxl, nF, nB, nO
            self.d_cx = self.dram_in("cx", [128, nF + nB + nO])
            if self.fused:
                self.d_flB = self.dram_in("flB", [128, 2])
                self.d_ropeB = self.dram_in("ropetabB", [128, 4, c.NK])
                self.d_cxB = self.dram_in("cxB", [128, nF + nB + nO])
            self.dw = {}
            for l in self.layers:
                self.dw[l] = dict(
                    win=self.dram_in(f"win{l}", [NWC, 128, c.KC * 128]),
                    wuq=self.dram_in(f"wuq{l}", [2, 128, 2048]),
                    wukv=self.dram_in(f"wukv{l}", [128, 1024]),
                    wout=self.dram_in(f"wout{l}", [8, 128, c.D]),
                    w1=self.dram_in(f"w1{l}", [c.FC, 128, c.KC * 256]),
                    w2=self.dram_in(f"w2{l}", [c.FC, 128, c.D]),
                    modw=self.dram_in(f"modw{l}", [c.MG, 128, c.KC * 512]),
                    vecs=self.dram_in(f"vecs{l}", [128, c.NV]),
                    rows=self.dram_in(f"rows{l}", [128, 256]),
                    lg=self.dram_in(f"lg{l}", [128, 8]),
                )
            self.d_out = nc.dram_tensor("x_out", [c.TH, c.D], F32, kind="ExternalOutput").ap()
            if self.n_out_ctx:
                self.d_outc = nc.dram_tensor("xc_out", [c.CTX, c.D], F32, kind="ExternalOutput").ap()

            self.hc_ng = 1 + 2 * c.NG
            self.hcache = nc.dram_tensor("hcache", [self.hc_ng, 128, c.KC * c.QG], BF16, kind="Internal").ap()
            self.Rhc = [Res() for _ in range(self.hc_ng)]
            self.kcache = [nc.dram_tensor(f"kcache{i}", [128, c.NK], BF16, kind="Internal").ap() for i in range(2)]
            self.vcache = [nc.dram_tensor(f"vcache{i}", [128, c.NKT * 128], BF16, kind="Internal").ap() for i in range(2)]
            self.Rkvc = [Res(), Res()]
            self.ckvcache = nc.dram_tensor("ckvcache", [128, c.NK], BF16, kind="Internal").ap()
            self.krcache = nc.dram_tensor("krcache", [32, c.NK], BF16, kind="Internal").ap()
            self.Rmlac = Res()
            self.kv_mode = None
            self.x = self.T(top, [128, c.NT, c.D], F32, "x")
            self.xc = self.T(top, [128, c.NC, c.D], F32, "xc")
            self.Rx = [Res(f"x{i}") for i in range(c.NT)]
            self.Rxc = [Res(f"xc{i}") for i in range(c.NC)]
            self.ident = self.T(top, [128, 128], BF16, "ident")
            self.identf = self.T(top, [128, 128], F32, "identf")
            self.onesb = self.T(top, [128, 128], BF16, "onesb")
            self.onesf = self.T(top, [128, 128], F32, "onesf")
            self.epsc = self.T(top, [128, 1], F32, "epsc")
            self.fl = self.T(top, [128, 2], F32, "fl")
            self.flB = self.T(top, [128, 2], F32, "flB")
            self.cc = self.T(top, [128, c.KC, 2], F32, "cc")
            self.Rc = Res("const")
            self.modT = self.T(top, [128, 6 * c.KC, 2], F32, "modT")
            self.AB = self.T(top, [128, 6, c.KC, 2], F32, "AB")
            self.Rmod = Res("mod")
            self.vecs = self.T(top, [128, c.NV], F32, "vecs")
            self.lg = self.T(top, [128, 8], F32, "lg")
            self.lgx = self.T(top, [128, 16], F32, "lgx")
            self.Rvec = Res("vecs")

            self.emit_consts()
            em.flush()
            self.Roth = [Res() for _ in range(c.NT)]
            tabA = (self.d_rope, self.d_cx, self.fl)
            if not self.fused:
                self.load_x(self.d_xown)
                self.d_oth_cur = self.d_xoth
                self.cur_rope, self.cur_cx, self.cur_fl = tabA
                for li, l in enumerate(self.layers):
                    self.emit_layer(l, l == c.DEPTH - 1, first=(li == 0))
            else:
                assert c.DEPTH == 2
                tabB = (self.d_ropeB, self.d_cxB, self.flB)
                xoth1 = nc.dram_tensor("xoth1", [c.TH, c.D], F32, kind="Internal").ap()
                self.load_x(self.d_xoth)
                self.d_oth_cur = self.d_xown
                self.cur_rope, self.cur_cx, self.cur_fl = tabB

                def handover(t):
                    em.dma(SP, xoth1[t * 128:(t + 1) * 128, :], self.x[:, t, :], reads=[self.Rx[t]], writes=[self.Roth[t]],
                           semkey=f"xst{t % 4}")
                    em.dma(SP, self.x[:, t, :], self.d_xown[t * 128:(t + 1) * 128, :], writes=[self.Rx[t]], semkey=f"x{t}")
                self.kv_mode = "write"
                self.emit_layer(0, True, first=True, post_tile=handover)
                self.kv_mode = "read"
                self.d_oth_cur = self.d_xoth
                self.cur_rope, self.cur_cx, self.cur_fl = tabA
                self.hc_swap = True
                self.emit_layer(0, False, first=False)
                self.hc_swap = False
                self.kv_mode = None
                self.d_oth_cur = xoth1
                self.out_evs = []

                def emit_out(t):
                    self.out_evs.append(em.dma(SP, self.d_out[t * 128:(t + 1) * 128, :], self.x[:, t, :], reads=[self.Rx[t]],
                                               semkey="out"))
                self.emit_layer(1, True, first=False, post_tile=emit_out)
            evs = list(getattr(self, "out_evs", []))
            if not evs:
                for t in range(c.NT):
                    evs.append(em.dma(SP, self.d_out[t * 128:(t + 1) * 128, :], self.x[:, t, :], reads=[self.Rx[t]], semkey="out"))
            if self.n_out_ctx:
                for t in range(c.NC):
                    evs.append(em.dma(SP, self.d_outc[t * 128:(t + 1) * 128, :], self.xc[:, t, :], reads=[self.Rxc[t]],
                                      semkey="out"))
            em.wait_event(SP, evs[-1])
            em.flush()
        return nc

    def emit_consts(self):
        c, em = self.c, self.em
        Rc = self.Rc
        identf, ident, onesb, onesf = self.identf, self.ident, self.onesb, self.onesf
        em.op(G, lambda e: e.memset(identf[:], 0.0), writes=[Rc])
        em.op(G, lambda e: e.affine_select(out=identf[:], in_=identf[:], pattern=[[-1, 128]], compare_op=ALU.not_equal,
                                           fill=1.0, base=0, channel_multiplier=1), reads=[Rc], writes=[Rc])
        em.op(V, lambda e: e.tensor_copy(out=ident[:], in_=identf[:]), reads=[Rc], writes=[Rc])
        em.op(G, lambda e: e.memset(onesf[:], 1.0), writes=[Rc])
        em.op(V, lambda e: e.tensor_copy(out=onesb[:], in_=onesf[:]), reads=[Rc], writes=[Rc])
        epsc = self.epsc
        em.op(G, lambda e: e.memset(epsc[:], EPS), writes=[Rc])
        em.dma(SP, self.fl[:], self.d_fl, writes=[Rc], semkey="c0")
        em.dma(SP, self.cc[:], self.d_cc, writes=[Rc], semkey="c1")
        if self.fused:
            em.dma(SP, self.flB[:], self.d_flB, writes=[Rc], semkey="c2")
        for t in range(c.NC):
            em.dma(SP, self.xc[:, t, :], self.d_xc[t * 128:(t + 1) * 128, :], writes=[self.Rxc[t]], semkey=f"xc{t}")

    def load_x(self, src):
        c, em = self.c, self.em
        for t in range(c.NT):
            em.dma(SP, self.x[:, t, :], src[t * 128:(t + 1) * 128, :], writes=[self.Rx[t]], semkey=f"x{t}")

    def emit_layer(self, l, last, first, post_tile=None):
        c, em = self.c, self.em
        self.l = l
        self.W = self.dw[l]
        done = getattr(self, "_mod_done", set())
        self.phase_mod(full=(l not in done))
        done.add(l)
        self._mod_done = done
        if not getattr(self, "hc_swap", False):
            self.phase_norm1()
        with contextlib.ExitStack() as ms:
            self.mixT = self.T(ms, [128, 8, c.TH], BF16, "mixT")
            self.mixTc = self.T(ms, [128, 8, c.CTX], BF16, "mixTc")
            self.Rmix = [[Res() for _ in range(c.NT)] for _ in range(8)]
            self.Rmixc = [[Res() for _ in range(c.NC)] for _ in range(8)]
            self.phase_conv(last)
            for hp in range(2):
                self.phase_ret(hp, last)
            self.phase_mla(last)
            self.phase_out(last)
        self.phase_ffn(last, post_tile)

    def seg_tiles(self, seg):
        c = self.c
        return {"ctx": c.NC, "oth": c.NT, "own": c.NT}[seg]

    def seg_kbase(self, seg):
        c = self.c
        return {"ctx": 0, "oth": c.CTX, "own": c.CTX + c.TH}[seg]

    def seg_groups(self, seg):
        c = self.c
        n = self.seg_tiles(seg)
        g = min(n, c.GT)
        return [(i, g) for i in range(0, n, g)]

    def phase_mod(self, full=True):
        c, em, W = self.c, self.em, self.W
        KC = c.KC
        with contextlib.ExitStack() as ph:
            Rvec, Rmod, Rc = self.Rvec, self.Rmod, self.Rc
            if full:
                em.dma(SP, self.vecs[:], W["vecs"], writes=[Rvec], semkey="vec")
                em.dma(SP, self.lg[:], W["lg"], writes=[Rvec], semkey="vec")
                self.mod_vectors(ph)
            self.decay_scalars()
            em.flush()

    def mod_vectors(self, ph):
        c, em, W = self.c, self.em, self.W
        KC = c.KC
        if True:
            Rvec, Rmod, Rc = self.Rvec, self.Rmod, self.Rc
            scT = self.T(ph, [128, KC, 2], BF16, "scT")
            Rs = Res()
            self.act(scT[:], self.cc[:], AF.Silu, [Rc], [Rs])
            wb = [self.T(ph, [128, KC, 512], BF16, "modw") for _ in range(2)]
            Rwb = [Res(), Res()]
            prow = [self.P(ph, [128, 512], F32, "prow") for _ in range(2)]
            Rprow = [Res(), Res()]
            rowt = [self.T(ph, [2, 512], F32, "rowt") for _ in range(2)]
            Rrow = [Res(), Res()]
            pT = self.P(ph, [128, c.MG * 4, 2], F32, "pT")
            RpT = Res()
            for j in range(c.MG):
                b = j % 2
                for k in range(0, KC, 4):
                    k2 = min(KC, k + 4)
                    em.dma(G, wb[b][:, k:k2, :], W["modw"][j].rearrange("p (k n) -> p k n", k=KC)[:, k:k2, :], writes=[Rwb[b]],
                           semkey=f"mw{b}")
                for k in range(KC):
                    self.mm(prow[b][0:2, :], scT[:, k, :], wb[b][:, k, :], k == 0, k == KC - 1, [Rs, Rwb[b]], [Rprow[b]])
                self.cp(V, rowt[b][:], prow[b][0:2, :], [Rprow[b]], [Rrow[b]])
                for q in range(4):
                    self.mm(pT[:, j * 4 + q, :], rowt[b][0:2, q * 128:(q + 1) * 128], self.identf[0:2, 0:2], True, True,
                            [Rrow[b], Rc], [RpT])
            modT = self.modT
            vo = c.vo
            for s_ in range(2):
                self.tt(V, modT[:, :, s_], pT[:, :, s_], self.vecs[:, vo["modb"]:vo["modb"] + 6 * KC], ALU.add,
                        [RpT, Rvec], [Rmod])
            AB = self.AB
            for s_ in range(2):
                for (dst, vsc, vsh, vg, gpre, gpost) in ((0, 1, 0, 2, "pre1", "post1"), (3, 4, 3, 5, "pre2", "post2")):
                    self.stt(AB[:, dst, :, s_], modT[:, vsc * KC:(vsc + 1) * KC, s_], 1.0,
                             self.vecs[:, vo[gpre]:vo[gpre] + KC], ALU.add, ALU.mult, [Rmod, Rvec], [Rmod])
                    self.cp(V, AB[:, dst + 1, :, s_], modT[:, vsh * KC:(vsh + 1) * KC, s_], [Rmod], [Rmod])
                    self.tt(V, AB[:, dst + 2, :, s_], modT[:, vg * KC:(vg + 1) * KC, s_],
                            self.vecs[:, vo[gpost]:vo[gpost] + KC], ALU.mult, [Rmod, Rvec], [Rmod])

    def decay_scalars(self):
        if True:
            Rvec, Rc = self.Rvec, self.Rc
            lg, lgx, fl = self.lg, self.lgx, self.cur_fl
            self.ts(V, lgx[:, 0:4], lg[:, 4:8], -1.0, None, ALU.mult, None, [Rvec], [Rvec])
            self.ts(V, lgx[:, 4:8], lg[:, 0:4], fl[:, 0:1], None, ALU.mult, None, [Rvec, Rc], [Rvec])
            self.ts(V, lgx[:, 12:16], lg[:, 4:8], fl[:, 1:2], None, ALU.mult, None, [Rvec, Rc], [Rvec])
            self.tt(V, lgx[:, 8:12], lgx[:, 4:8], lgx[:, 12:16], ALU.subtract, [Rvec], [Rvec])
            self.tt(V, lgx[:, 4:8], lgx[:, 4:8], lgx[:, 12:16], ALU.add, [Rvec], [Rvec])

    def bcast_vec(self, ph, vec_idx, s_, pbank, Rp):
        c = self.c
        out = self.T(ph, [128, c.D], F32, "bc")
        Ro = Res()
        dg = [self.T(ph, [128, 128], F32, "dg") for _ in range(2)]
        Rdg = [Res(), Res()]
        for k in range(c.KC):
            b = k % 2
            self.ts(V, dg[b][:], self.identf[:], self.AB[:, vec_idx, k, s_:s_ + 1], None, ALU.mult, None,
                    [self.Rc, self.Rmod], [Rdg[b]])
            self.mm(pbank[:, 0:128], self.onesf[:], dg[b][:], True, True, [Rdg[b], self.Rc], [Rp])
            self.cp(V, out[:, k * 128:(k + 1) * 128], pbank[:, 0:128], [Rp], [Ro])
        return out, Ro

    def make_norm_ctx(self, ph):
        c = self.c
        d = dict(
            junk=self.T(ph, [128, c.D], BF16, "junk"), Rjunk=Res(),
            xn=[self.T(ph, [128, c.D], BF16, "xn") for _ in range(2)], Rxn=[Res(), Res()],
            st=[self.T(ph, [128, 4], F32, "nst") for _ in range(2)], Rst=[Res(), Res()],
            pT=[self.P(ph, [128, c.KC, 128], BF16, "pTn") for _ in range(2)], RpT=[Res(), Res()],
            xo=[self.T(ph, [128, c.D], F32, "xo") for _ in range(2)], Rxo=[Res(), Res()],
            i=0,
        )
        return d

    def norm_A(self, nctx, seg, tile):
        c, em = self.c, self.em
        i = nctx["i"]
        nctx["i"] += 1
        b = i % 2
        if seg == "own":
            xin, Rxin = self.x[:, tile, :], self.Rx[tile]
        elif seg == "ctx":
            xin, Rxin = self.xc[:, tile, :], self.Rxc[tile]
        else:
            xo, Rxo = nctx["xo"][b], nctx["Rxo"][b]
            em.dma(SP, xo[:], self.d_oth_cur[tile * 128:(tile + 1) * 128, :], reads=[self.Roth[tile]], writes=[Rxo],
                   semkey=f"xo{b}")
            xin, Rxin = xo[:], Rxo
        st, Rst = nctx["st"][b], nctx["Rst"][b]
        xn, Rxn = nctx["xn"][b], nctx["Rxn"][b]
        self.act(nctx["junk"][:], xin, AF.Square, [Rxin], [nctx["Rjunk"], Rst], accum_out=st[:, 0:1])
        if FAST_RSQRT:
            self.act(st[:, 2:3], st[:, 0:1], AF.Abs_reciprocal_sqrt, [Rst, self.Rc], [Rst], scale=1.0 / c.D, bias=self.epsc[:, 0:1])
        else:
            self.ts(V, st[:, 1:2], st[:, 0:1], 1.0 / c.D, EPS, ALU.mult, ALU.add, [Rst], [Rst])
            self.rsqrt_(st[:, 2:3], st[:, 1:2], 1, [Rst], [Rst])
        self.ts(V, xn[:], xin, st[:, 2:3], None, ALU.mult, None, [Rxin, Rst], [Rxn])
        return b

    def norm_B(self, nctx, b, seg, hT, col0, RhT, vA, s_=None):
        c = self.c
        if s_ is None:
            s_ = 1 if seg == "ctx" else 0
        xn, Rxn = nctx["xn"][b], nctx["Rxn"][b]
        pT, RpT = nctx["pT"][b], nctx["RpT"][b]
        for k in range(c.KC):
            self.tr(pT[:, k, :], xn[:, k * 128:(k + 1) * 128], self.ident[:], [Rxn, self.Rc], [RpT])
        for k in range(c.KC):
            A = self.AB[:, vA, k, s_:s_ + 1]
            B = self.AB[:, vA + 1, k, s_:s_ + 1]
            if k % 2 == 0:
                self.ts(V, hT[:, k, col0:col0 + 128], pT[:, k, :], A, B, ALU.mult, ALU.add, [RpT, self.Rmod], [RhT])
            else:
                self.act(hT[:, k, col0:col0 + 128], pT[:, k, :], AF.Identity, [RpT, self.Rmod], [RhT], scale=A, bias=B)

    def norm_items(self, nctx, items):
        prev = None
        for it in items:
            b = self.norm_A(nctx, it[0], it[1])
            if prev is not None:
                pb, pit = prev
                self.norm_B(nctx, pb, pit[0], pit[2], pit[3], pit[4], pit[5])
                if pit[6] is not None:
                    pit[6]()
            prev = (b, it)
        if prev is not None:
            pb, pit = prev
            self.norm_B(nctx, pb, pit[0], pit[2], pit[3], pit[4], pit[5])
            if pit[6] is not None:
                pit[6]()

    def hc_index(self, seg, t0):
        c = self.c
        if getattr(self, "hc_swap", False) and seg != "ctx":
            seg = "own" if seg == "oth" else "oth"
        return {"ctx": 0, "oth": 1, "own": 1 + c.NG}[seg] + (0 if seg == "ctx" else t0 // c.GT)

    def phase_norm1(self):
        c, em = self.c, self.em
        with contextlib.ExitStack() as sp:
            nctx = self.make_norm_ctx(sp)
            hT = [self.T(sp, [128, c.KC, c.QG], BF16, "hT") for _ in range(2)]
            RhT = [Res(), Res()]
            gi = 0
            items = []
            for seg in ("ctx", "oth", "own"):
                for (t0, nt) in self.seg_groups(seg):
                    b = gi % 2
                    gi += 1
                    g = self.hc_index(seg, t0)

                    def store(b=b, g=g, nt=nt):
                        dst = self.hcache[g].rearrange("p (k n) -> p k n", k=c.KC)
                        em.dma(SP, dst[:, :, 0:nt * 128], hT[b][:, :, 0:nt * 128], reads=[RhT[b]], writes=[self.Rhc[g]],
                               semkey=f"hcst{b}")
                    for i in range(nt):
                        items.append((seg, t0 + i, hT[b], i * 128, RhT[b], 0, store if i == nt - 1 else None))
            self.norm_items(nctx, items)
            em.flush()

    def load_h(self, hT, RhT, seg, t0, nt, key):
        c = self.c
        g = self.hc_index(seg, t0)
        off = (t0 % c.GT) * 128 if seg != "ctx" else t0 * 128
        src = self.hcache[g].rearrange("p (k n) -> p k n", k=c.KC)
        self.em.dma(SP, hT[:, :, 0:nt * 128], src[:, :, off:off + nt * 128], reads=[self.Rhc[g]], writes=[RhT], semkey=key)

    def load_w(self, dst, src, Rd, key):
        return self.em.dma(G, dst, src, writes=[Rd], semkey=key)

    def load_win(self, ph, chunks):
        c = self.c
        out = {}
        for i, ch in enumerate(chunks):
            t = self.T(ph, [128, c.KC, 128], BF16, f"win{ch}")
            R = Res()
            self.load_w(t[:], self.W["win"][ch].rearrange("p (k n) -> p k n", k=c.KC), R, f"win{i}")
            out[ch] = (t, R)
        return out

    def proj_fm(self, ps, Rps, wt, hT, RhT, cols, ncols, M=128):
        c = self.c
        w, Rw = wt
        for k in range(c.KC):
            self.mm(ps[0:M, 0:ncols], w[:, k, 0:M], hT[:, k, cols:cols + ncols], k == 0, k == c.KC - 1, [Rw, RhT], [Rps])

    def proj_tm(self, ps, Rps, wt, hT, RhT, col0, o0=0):
        c = self.c
        w, Rw = wt
        for k in range(c.KC):
            self.mm(ps[:, o0:o0 + 128], hT[:, k, col0:col0 + 128], w[:, k, :], k == 0, k == c.KC - 1, [Rw, RhT], [Rps])

    def phase_conv(self, last):
        c, em = self.c, self.em
        vo = c.vo
        with contextlib.ExitStack() as ph:
            wts = self.load_win(ph, [0, 1, 2, 3])
            QG = c.QG
            hT = [self.T(ph, [128, c.KC, QG], BF16, "hT") for _ in range(2)]
            RhT = [Res(), Res()]
            pu = [self.P(ph, [128, 512], F32, "pu") for _ in range(2)]
            pg = [self.P(ph, [128, 512], F32, "pg") for _ in range(2)]
            Rpu, Rpg = [Res(), Res()], [Res(), Res()]
            sg = [self.T(ph, [128, 512], F32, "sg") for _ in range(2)]
            Rsg = [Res(), Res()]
            segs = [("own", c.TH)] + ([] if last else [("ctx", c.CTX)])
            for seg, ntok in segs:
                with contextlib.ExitStack() as sp:
                    ypad = self.T(sp, [128, 2, ntok + 2 * HALO], F32, "ypad")
                    Rypad = [Res(), Res()]
                    acc = self.T(sp, [128, 2, ntok], F32, "acc")
                    Racc = [Res(), Res()]
                    for ch in range(2):
                        em.op(G, lambda e, ch=ch: e.memset(ypad[:, ch, 0:HALO], 0.0), writes=[Rypad[ch]])
                        em.op(G, lambda e, ch=ch, ntok=ntok: e.memset(ypad[:, ch, HALO + ntok:2 * HALO + ntok], 0.0),
                              writes=[Rypad[ch]])
                    gi = 0

                    def glu_group(sseg, t0, nt, dst_fn):
                        nonlocal gi
                        b = gi % 2
                        gi += 1
                        self.load_h(hT[b], RhT[b], sseg, t0, nt, f"hld{b}")
                        n = nt * 128
                        for ch in range(2):
                            bb = ch
                            self.proj_fm(pu[bb], Rpu[bb], wts[ch], hT[b], RhT[b], 0, n)
                            self.proj_fm(pg[bb], Rpg[bb], wts[2 + ch], hT[b], RhT[b], 0, n)
                            self.act(sg[bb][:, 0:n], pg[bb][:, 0:n], AF.Sigmoid, [Rpg[bb]], [Rsg[bb]])
                            dst_fn(ch, pu[bb], Rpu[bb], sg[bb], Rsg[bb], n)

                    for (t0, nt) in self.seg_groups(seg):
                        def dst(ch, pu_, Rpu_, sg_, Rsg_, n, t0=t0):
                            self.tt(V, ypad[:, ch, HALO + t0 * 128:HALO + t0 * 128 + n], pu_[:, 0:n], sg_[:, 0:n], ALU.mult,
                                    [Rpu_, Rsg_], [Rypad[ch]])
                        glu_group(seg, t0, nt, dst)
                    if seg == "own":
                        tmp = self.T(sp, [128, 128], F32, "halo")
                        Rtmp = Res()

                        def dst_l(ch, pu_, Rpu_, sg_, Rsg_, n):
                            self.tt(V, tmp[:], pu_[:, 0:128], sg_[:, 0:128], ALU.mult, [Rpu_, Rsg_], [Rtmp])
                            self.ts(V, ypad[:, ch, 0:HALO], tmp[:, 128 - HALO:128], self.cur_fl[:, 0:1], None, ALU.mult, None,
                                    [Rtmp, self.Rc], [Rypad[ch]])

                        def dst_r(ch, pu_, Rpu_, sg_, Rsg_, n, ntok=ntok):
                            self.tt(V, tmp[:], pu_[:, 0:128], sg_[:, 0:128], ALU.mult, [Rpu_, Rsg_], [Rtmp])
                            self.ts(V, ypad[:, ch, HALO + ntok:2 * HALO + ntok], tmp[:, 0:HALO], self.cur_fl[:, 1:2], None,
                                    ALU.mult, None, [Rtmp, self.Rc], [Rypad[ch]])
                        glu_group("oth", c.NT - 1, 1, dst_l)
                        glu_group("oth", 0, 1, dst_r)
                    cw0 = vo["convw"]
                    for ch in range(2):
                        for k in range(CONV_K):
                            wk = self.vecs[:, cw0 + ch * CONV_K + k:cw0 + ch * CONV_K + k + 1]
                            if k == 0:
                                self.ts(V, acc[:, ch, :], ypad[:, ch, 0:ntok], wk, None, ALU.mult, None,
                                        [Rypad[ch], self.Rvec], [Racc[ch]])
                            else:
                                self.stt(acc[:, ch, :], ypad[:, ch, k:k + ntok], wk, acc[:, ch, :], ALU.mult, ALU.add,
                                         [Rypad[ch], self.Rvec, Racc[ch]], [Racc[ch]])
                        self.ts(V, acc[:, ch, :], acc[:, ch, :], self.vecs[:, vo["convb"] + ch:vo["convb"] + ch + 1], None,
                                ALU.add, None, [Racc[ch], self.Rvec], [Racc[ch]])
                    sq = self.T(sp, [128, 2, 512], F32, "sq")
                    Rsq = Res()
                    mean = self.T(sp, [128, 512], F32, "mean")
                    var = self.T(sp, [128, 512], F32, "var")
                    Rmv = Res()
                    zt = self.T(sp, [128, 512], F32, "zt")
                    Rzt = Res()
                    mixT, Rmix = (self.mixT, self.Rmix) if seg == "own" else (self.mixTc, self.Rmixc)
                    for (t0, nt) in self.seg_groups(seg):
                        n = nt * 128
                        c0 = t0 * 128
                        p1, Rp1, p2, Rp2 = pu[0], Rpu[0], pg[0], Rpg[0]
                        for ch in range(2):
                            self.act(sq[:, ch, 0:n], acc[:, ch, c0:c0 + n], AF.Square, [Racc[ch]], [Rsq])
                        for ch in range(2):
                            self.mm(p1[:, 0:n], self.onesf[:], acc[:, ch, c0:c0 + n], ch == 0, ch == 1, [Racc[ch], self.Rc], [Rp1])
                        for ch in range(2):
                            self.mm(p2[:, 0:n], self.onesf[:], sq[:, ch, 0:n], ch == 0, ch == 1, [Rsq, self.Rc], [Rp2])
                        self.ts(V, mean[:, 0:n], p1[:, 0:n], 1.0 / 256, None, ALU.mult, None, [Rp1], [Rmv])
                        self.tt(V, var[:, 0:n], mean[:, 0:n], mean[:, 0:n], ALU.mult, [Rmv], [Rmv])
                        self.stt(var[:, 0:n], p2[:, 0:n], 1.0 / 256, var[:, 0:n], ALU.mult, ALU.subtract, [Rp2, Rmv], [Rmv])
                        self.ts(V, var[:, 0:n], var[:, 0:n], EPS, None, ALU.add, None, [Rmv], [Rmv])
                        self.rsqrt_(var[:, 0:n], var[:, 0:n], n, [Rmv], [Rmv])
                        for ch in range(2):
                            self.tt(V, zt[:, 0:n], acc[:, ch, c0:c0 + n], mean[:, 0:n], ALU.subtract, [Racc[ch], Rmv], [Rzt])
                            self.tt(V, zt[:, 0:n], zt[:, 0:n], var[:, 0:n], ALU.mult, [Rzt, Rmv], [Rzt])
                            self.ts(V, zt[:, 0:n], zt[:, 0:n], self.vecs[:, vo["lng"] + ch:vo["lng"] + ch + 1],
                                    self.vecs[:, vo["lnb"] + ch:vo["lnb"] + ch + 1], ALU.mult, ALU.add, [Rzt, self.Rvec], [Rzt])
                            self.act(mixT[:, ch, c0:c0 + n], zt[:, 0:n], AF.Silu, [Rzt], [Rmix[ch][t] for t in range(t0, t0 + nt)])
                    em.flush()

    def phase_ret(self, hp, last):
        c, em = self.c, self.em
        QG, GT = c.QG, c.GT
        with contextlib.ExitStack() as ph:
            NQ = c.TH + c.CTX
            kT = self.T(ph, [128, c.NK], BF16, "kT")
            qT = self.T(ph, [128, NQ], BF16, "qT")
            vv = self.T(ph, [128, c.NKT, 128], BF16, "vv")
            sgt = self.T(ph, [128, c.NT + c.NC, 128], BF16, "sgt")
            RkT = [Res() for _ in range(c.NKT)]
            RqT = [Res() for _ in range(c.NT + c.NC)]
            Rvv = [Res() for _ in range(c.NKT)]
            Rsgt = [Res() for _ in range(c.NT + c.NC)]
            Ef = self.T(ph, [128, 2, QG], BF16, "Ef")
            Eb = self.T(ph, [128, 2, QG], BF16, "Eb")
            Eo = self.T(ph, [128, 2, QG], BF16, "Eo")
            Ed = self.T(ph, [128, GT, 2, QG], BF16, "Ed")
            nF, nB, nO = self.nF, self.nB, self.nO
            Ct = self.T(ph, [128, 2, nF + nB + nO], F32, "Ct")
            RE = Res()
            with contextlib.ExitStack() as sp:
                dt_ = self.T(sp, [128, GT, QG], F32, "dtab")
                cx = self.T(sp, [128, nF + nB + nO], F32, "cx")
                Rt = Res()
                em.dma(SP, dt_[:], self.d_dtab, writes=[Rt], semkey="tab")
                em.dma(SP, cx[:], self.cur_cx, writes=[Rt], semkey="tab")
                dp = self.T(sp, [128, QG], F32, "dp")
                dn = self.T(sp, [128, QG], F32, "dn")
                ind = self.T(sp, [128, QG], F32, "ind")
                t1 = self.T(sp, [128, QG], F32, "t1")
                t2 = self.T(sp, [128, QG], F32, "t2")
                Rd = Res()
                lg, lgx = self.lg, self.lgx
                for hh in range(2):
                    h = 2 * hp + hh
                    self.act(Ef[:, hh, :], dt_[:, 0, :], AF.Exp, [Rt, self.Rvec], [RE], scale=lg[:, h:h + 1])
                    self.act(Eb[:, hh, :], dt_[:, 0, :], AF.Exp, [Rt, self.Rvec], [RE], scale=lgx[:, h:h + 1])
                    self.act(Eo[:, hh, :], dt_[:, 0, :], AF.Exp, [Rt, self.Rvec], [RE], scale=lgx[:, 8 + h:9 + h])
                    self.act(Ct[:, hh, 0:nF], cx[:, 0:nF], AF.Exp, [Rt, self.Rvec], [RE], scale=lg[:, h:h + 1])
                    self.act(Ct[:, hh, nF:nF + nB], cx[:, nF:nF + nB], AF.Exp, [Rt, self.Rvec], [RE], scale=lg[:, 4 + h:5 + h])
                    self.act(Ct[:, hh, nF + nB:], cx[:, nF + nB:], AF.Exp, [Rt, self.Rvec], [RE], scale=lgx[:, 4 + h:5 + h])
                for j in range(GT):
                    self.ts(V, dp[:], dt_[:, j, :], 0.0, None, ALU.max, None, [Rt], [Rd])
                    self.ts(V, dn[:], dt_[:, j, :], 0.0, None, ALU.min, None, [Rt], [Rd])
                    self.ts(V, ind[:], dt_[:, j, :], 0.0, None, ALU.is_ge, None, [Rt], [Rd])
                    for hh in range(2):
                        h = 2 * hp + hh
                        self.act(t1[:], dp[:], AF.Exp, [Rd, self.Rvec], [Rd], scale=lg[:, h:h + 1])
                        self.act(t2[:], dn[:], AF.Exp, [Rd, self.Rvec], [Rd], scale=lgx[:, h:h + 1])
                        self.tt(V, t1[:], t1[:], t2[:], ALU.subtract, [Rd], [Rd])
                        self.tt(V, t1[:], t1[:], ind[:], ALU.mult, [Rd], [Rd])
                        self.tt(V, Ed[:, j, hh, :], t1[:], t2[:], ALU.add, [Rd], [RE])
                em.flush()
            with contextlib.ExitStack() as sp:
                wts = self.load_win(sp, [4 + hp, 6 + hp, 8 + hp, 10 + hp, 12 + hp, 14 + hp])
                hT = [self.T(sp, [128, c.KC, QG], BF16, "hT") for _ in range(2)]
                RhT = [Res(), Res()]
                rope = [self.T(sp, [128, 2, QG], F32, "rope") for _ in range(2)]
                Rrope = [Res(), Res()]
                pp = [self.P(sp, [128, 512], F32, "pp") for _ in range(4)]
                Rpp = [Res() for _ in range(4)]
                ptm = [self.P(sp, [128, 512], F32, "ptm") for _ in range(2)]
                Rptm = [Res(), Res()]
                ta = self.T(sp, [128, QG], F32, "ta")
                tb = self.T(sp, [128, QG], F32, "tb")
                Rta, Rtb = Res(), Res()
                gi = 0
                reading = (self.kv_mode == "read")
                if reading:
                    kc, vc = self.kcache[hp], self.vcache[hp].rearrange("p (t d) -> p t d", d=128)
                    C_, T_ = c.CTX, c.TH
                    em.dma_group(SP, [(kT[:, 0:C_], kc[:, 0:C_]), (kT[:, C_:C_ + T_], kc[:, C_ + T_:c.NK]),
                                      (kT[:, C_ + T_:c.NK], kc[:, C_:C_ + T_])], reads=[self.Rkvc[hp]], writes=RkT, semkey="kvld")
                    em.dma_group(SP, [(vv[:, 0:c.NC, :], vc[:, 0:c.NC, :]), (vv[:, c.NC:c.NC + c.NT, :], vc[:, c.NC + c.NT:c.NKT, :]),
                                      (vv[:, c.NC + c.NT:c.NKT, :], vc[:, c.NC:c.NC + c.NT, :])], reads=[self.Rkvc[hp]], writes=Rvv,
                                 semkey="kvld2")
                for seg in ("ctx", "oth", "own"):
                    kb = self.seg_kbase(seg)
                    for (t0, nt) in self.seg_groups(seg):
                        need_q = (seg == "own") or (seg == "ctx" and not last)
                        if reading and not need_q:
                            continue
                        b = gi % 2
                        gi += 1
                        n = nt * 128
                        k0 = kb + t0 * 128
                        em.dma(SP, rope[b][:, :, 0:n], self.cur_rope[:, 0:2, k0:k0 + n], writes=[Rrope[b]], semkey=f"rope{b}")
                        self.load_h(hT[b], RhT[b], seg, t0, nt, f"hld{b}")
                        jobs = [] if reading else [(2, 3, 0.125, kT, k0, [RkT[kb // 128 + t0 + i] for i in range(nt)])]
                        if need_q:
                            qc0 = (0 if seg == "ctx" else c.CTX) + t0 * 128
                            qt0 = (0 if seg == "ctx" else c.NC) + t0
                            jobs.append((0, 1, 1.0, qT, qc0, [RqT[qt0 + i] for i in range(nt)]))
                        for (wa, wb_, scl, dstT, d0, Rd_) in jobs:
                            pa, Rpa = pp[(wa) % 4], Rpp[(wa) % 4]
                            pb, Rpb = pp[(wb_) % 4], Rpp[(wb_) % 4]
                            self.proj_fm(pa, Rpa, wts[[4, 6, 8, 10][wa] + hp], hT[b], RhT[b], 0, n)
                            self.proj_fm(pb, Rpb, wts[[4, 6, 8, 10][wb_] + hp], hT[b], RhT[b], 0, n)
                            self.stt(ta[:, 0:n], pa[:, 0:n], scl, rope[b][:, 0, 0:n], ALU.mult, ALU.mult, [Rpa, Rrope[b]], [Rta])
                            self.stt(tb[:, 0:n], pb[:, 0:n], scl, rope[b][:, 1, 0:n], ALU.mult, ALU.mult, [Rpb, Rrope[b]], [Rtb])
                            self.tt(V, dstT[:, d0:d0 + n], ta[:, 0:n], tb[:, 0:n], ALU.add, [Rta, Rtb], Rd_)
                        pv, Rpv = ptm[0], Rptm[0]
                        kt0 = kb // 128 + t0
                        if not reading:
                            for i in range(nt):
                                self.proj_tm(pv, Rpv, wts[12 + hp], hT[b], RhT[b], i * 128, o0=i * 128)
                            self.cp(V, vv[:, kt0:kt0 + nt, :], pv[:, 0:n].rearrange("p (t d) -> p t d", d=128), [Rpv],
                                    [Rvv[kt0 + i] for i in range(nt)])
                        if need_q:
                            pg_, Rpg_ = ptm[1], Rptm[1]
                            for i in range(nt):
                                self.proj_tm(pg_, Rpg_, wts[14 + hp], hT[b], RhT[b], i * 128, o0=i * 128)
                            self.act(sgt[:, qt0:qt0 + nt, :], pg_[:, 0:n].rearrange("p (t d) -> p t d", d=128), AF.Silu,
                                     [Rpg_], [Rsgt[qt0 + i] for i in range(nt)])
                if self.kv_mode == "write":
                    em.dma_group(SP, [(self.kcache[hp], kT[:]), (self.vcache[hp].rearrange("p (t d) -> p t d", d=128), vv[:])],
                                 reads=list(RkT) + list(Rvv), writes=[self.Rkvc[hp]], semkey="kvst")
                em.flush()
            with contextlib.ExitStack() as sp:
                NS_ = 3
                pst = [self.P(sp, [128, 512], F32, "pst") for _ in range(NS_)]
                Rpst = [Res() for _ in range(NS_)]
                po = [self.P(sp, [128, 512], F32, "po") for _ in range(GT)]
                Rpo = [Res() for _ in range(GT)]
                ptr = self.P(sp, [128, 128], BF16, "ptr")
                Rptr = Res()
                NP_ = 4
                pb_ = [self.T(sp, [128, QG], BF16, "pbuf") for _ in range(NP_)]
                Rpb = [Res() for _ in range(NP_)]
                tsc = [self.T(sp, [128, QG], BF16, "tsc") for _ in range(2)]
                Rtsc = [Res(), Res()]
                rows = self.T(sp, [128, 256], F32, "rows")
                Rrows = Res()
                em.dma(SP, rows[:], self.W["rows"], writes=[Rrows], semkey="rows")
                bst = self.T(sp, [128, 8], F32, "bst")
                bag = self.T(sp, [128, 4], F32, "bag")
                Rb = Res()
                yt = self.T(sp, [128, 64], F32, "yt")
                Ryt = Res()
                yret = [self.T(sp, [128, 128], BF16, "yret") for _ in range(max(GT, c.NC))]
                Ryret = [Res() for _ in range(max(GT, c.NC))]
                cnt = 0
                qsegs = [("own", self.ktl_lat)] + ([] if last else [("ctx", [self.ktl_ctx])])
                ti = 0
                for seg, ktls in qsegs:
                    for gidx, ktl in enumerate(ktls):
                        nt = GT if seg == "own" else c.NC
                        n = nt * 128
                        qc0 = (c.CTX + gidx * QG) if seg == "own" else 0
                        qt0 = (c.NC + gidx * GT) if seg == "own" else 0
                        for hh in range(2):
                            h = 2 * hp + hh
                            r0 = hh * 64
                            base = cnt
                            cnt += len(ktl)

                            def st_mm(idx):
                                kt = ktl[idx][0]
                                sb = (base + idx) % NS_
                                for rep in range(1 + DUP_ST_RET):
                                    self.mm(pst[sb][:, 0:n], kT[r0:r0 + 64, kt * 128:(kt + 1) * 128], qT[r0:r0 + 64, qc0:qc0 + n],
                                            True, True, [RkT[kt]] + [RqT[qt0 + i] for i in range(nt)], [Rpst[sb]])
                            st_mm(0)
                            if len(ktl) > 1:
                                st_mm(1)
                            for idx, (kt, kind, e) in enumerate(ktl):
                                sb = (base + idx) % NS_
                                pbi = (base + idx) % NP_
                                if idx + 2 < len(ktl):
                                    st_mm(idx + 2)
                                alt = RET_SPLIT and (idx % 2 == 1)
                                if kind == "D":
                                    Eap, cs = Ed[:, e, hh, 0:n], None
                                else:
                                    Eap = {"F": Ef, "B": Eb, "O": Eo}[kind][:, hh, 0:n]
                                    eo = {"F": 0, "B": nF, "O": nF + nB}[kind] + e
                                    cs = Ct[:, hh, eo:eo + 1]
                                if alt:
                                    tq_ = (base + idx) // 2 % 2
                                    if cs is None:
                                        self.act(tsc[tq_][:, 0:n], pst[sb][:, 0:n], AF.Copy, [Rpst[sb]], [Rtsc[tq_]])
                                    else:
                                        self.act(tsc[tq_][:, 0:n], pst[sb][:, 0:n], AF.Copy, [Rpst[sb], RE], [Rtsc[tq_]], scale=cs)
                                    self.tt(G, pb_[pbi][:, 0:n], tsc[tq_][:, 0:n], Eap, ALU.mult, [Rtsc[tq_], RE], [Rpb[pbi]])
                                elif cs is None:
                                    self.tt(V, pb_[pbi][:, 0:n], pst[sb][:, 0:n], Eap, ALU.mult, [Rpst[sb], RE], [Rpb[pbi]])
                                else:
                                    self.stt(pb_[pbi][:, 0:n], pst[sb][:, 0:n], cs, Eap, ALU.mult, ALU.mult, [Rpst[sb], RE], [Rpb[pbi]])
                                for i in range(nt):
                                    self.mm(po[i][:, 0:64], pb_[pbi][:, i * 128:(i + 1) * 128], vv[:, kt, r0:r0 + 64],
                                            idx == 0, idx == len(ktl) - 1, [Rpb[pbi], Rvv[kt]], [Rpo[i]])
                            for i in range(nt):
                                yb = i
                                self.em.op(V, lambda e_, i=i: e_.bn_stats(out=bst[:, 0:6], in_=po[i][:, 0:64]), [Rpo[i]], [Rb])
                                self.em.op(V, lambda e_: e_.bn_aggr(out=bag[:, 0:2], in_=bst[:, 0:6]), [Rb], [Rb])
                                self.ts(V, bag[:, 2:3], bag[:, 1:2], EPS, None, ALU.add, None, [Rb], [Rb])
                                self.rsqrt_(bag[:, 3:4], bag[:, 2:3], 1, [Rb], [Rb])
                                self.ts(V, yt[:], po[i][:, 0:64], bag[:, 0:1], bag[:, 3:4], ALU.subtract, ALU.mult,
                                        [Rpo[i], Rb], [Ryt])
                                self.tt(V, yt[:], yt[:], rows[:, h * 64:(h + 1) * 64], ALU.mult, [Ryt, Rrows], [Ryt])
                                self.tt(V, yret[yb][:, r0:r0 + 64], yt[:], sgt[:, qt0 + i, r0:r0 + 64], ALU.mult,
                                        [Ryt, Rsgt[qt0 + i]], [Ryret[yb]])
                                if hh == 1:
                                    self.tr(ptr[:], yret[yb][:], self.ident[:], [Ryret[yb], self.Rc], [Rptr])
                                    if seg == "own":
                                        tcol = (gidx * GT + i) * 128
                                        self.cp(S, self.mixT[:, 2 + hp, tcol:tcol + 128], ptr[:], [Rptr],
                                                [self.Rmix[2 + hp][gidx * GT + i]])
                                    else:
                                        self.cp(S, self.mixTc[:, 2 + hp, i * 128:(i + 1) * 128], ptr[:], [Rptr],
                                                [self.Rmixc[2 + hp][i]])
                em.flush()

    def phase_mla(self, last):
        c, em = self.c, self.em
        QG, GT = c.QG, c.GT
        vo = c.vo
        with contextlib.ExitStack() as ph:
            NQ = c.TH + c.CTX
            ckvT = self.T(ph, [128, c.NK], BF16, "ckvT")
            Rckv = [Res() for _ in range(c.NKT)]
            KT = self.T(ph, [128, c.NK], BF16, "KT")
            RKTn = Res()
            RKTr = [Res() for _ in range(c.NKT)]
            cqT = self.T(ph, [128, 2, NQ], BF16, "cqT")
            Rcq = [Res() for _ in range(c.NT + c.NC)]
            with contextlib.ExitStack() as sp:
                wts = self.load_win(sp, [16, 17, 18, 19, 20])
                hTs = [self.T(sp, [128, c.KC, QG], BF16, "hT") for _ in range(2)]
                RhTs = [Res(), Res()]
                ropes = [self.T(sp, [128, 2, QG], F32, "rope") for _ in range(2)]
                Rropes = [Res(), Res()]
                gi = 0
                pp = [self.P(sp, [128, 512], F32, "pp") for _ in range(4)]
                Rpp = [Res() for _ in range(4)]
                pss = [self.P(sp, [128, 512], F32, "pss") for _ in range(2)]
                Rpss = [Res(), Res()]
                sq = self.T(sp, [128, 2, QG], BF16, "sq")
                Rsq = Res()
                rs = self.T(sp, [128, QG], F32, "rs")
                Rrs = Res()
                ta = self.T(sp, [128, QG], F32, "ta")
                tb = self.T(sp, [128, QG], F32, "tb")
                Rta, Rtb = Res(), Res()
                reading = (self.kv_mode == "read")
                if reading:
                    C_, T_ = c.CTX, c.TH
                    kc, rc = self.ckvcache, self.krcache
                    em.dma_group(SP, [(ckvT[:, 0:C_], kc[:, 0:C_]), (ckvT[:, C_:C_ + T_], kc[:, C_ + T_:c.NK]),
                                      (ckvT[:, C_ + T_:c.NK], kc[:, C_:C_ + T_])], reads=[self.Rmlac], writes=Rckv, semkey="kvld")
                    em.dma_group(SP, [(KT[64:96, 0:C_], rc[:, 0:C_]), (KT[64:96, C_:C_ + T_], rc[:, C_ + T_:c.NK]),
                                      (KT[64:96, C_ + T_:c.NK], rc[:, C_:C_ + T_])], reads=[self.Rmlac], writes=RKTr, semkey="kvld2")
                for seg in ("ctx", "oth", "own"):
                    kb = self.seg_kbase(seg)
                    for (t0, nt) in self.seg_groups(seg):
                        if reading and not ((seg == "own") or (seg == "ctx" and not last)):
                            continue
                        n = nt * 128
                        k0 = kb + t0 * 128
                        kt0 = k0 // 128
                        b = gi % 2
                        gi += 1
                        hT, RhT, rope, Rrope = hTs[b], RhTs[b], ropes[b], Rropes[b]
                        em.dma(SP, rope[64:96, :, 0:n], self.cur_rope[64:96, 2:4, k0:k0 + n], writes=[Rrope], semkey=f"rope{b}")
                        self.load_h(hT, RhT, seg, t0, nt, f"hld{b}")
                        if not reading:
                            pa, Rpa = pp[0], Rpp[0]
                            self.proj_fm(pa, Rpa, wts[18], hT, RhT, 0, n)
                            self.act(sq[:, 0, 0:n], pa[:, 0:n], AF.Square, [Rpa], [Rsq])
                            self.mm(pss[0][:, 0:n], self.onesb[:], sq[:, 0, 0:n], True, True, [Rsq, self.Rc], [Rpss[0]])
                            self.ts(V, rs[:, 0:n], pss[0][:, 0:n], 1.0 / 128, EPS, ALU.mult, ALU.add, [Rpss[0]], [Rrs])
                            self.rsqrt_(rs[:, 0:n], rs[:, 0:n], n, [Rrs], [Rrs])
                            self.stt(ckvT[:, k0:k0 + n], pa[:, 0:n], self.vecs[:, vo["kvg"]:vo["kvg"] + 1], rs[:, 0:n], ALU.mult,
                                     ALU.mult, [Rpa, self.Rvec, Rrs], [Rckv[kt0 + i] for i in range(nt)])
                            pa, Rpa = pp[1], Rpp[1]
                            pb, Rpb = pp[2], Rpp[2]
                            self.proj_fm(pa, Rpa, wts[19], hT, RhT, 0, n, M=96)
                            self.proj_fm(pb, Rpb, wts[20], hT, RhT, 0, n, M=96)
                            self.tt(V, ta[64:96, 0:n], pa[64:96, 0:n], rope[64:96, 0, 0:n], ALU.mult, [Rpa, Rrope], [Rta])
                            self.tt(V, tb[64:96, 0:n], pb[64:96, 0:n], rope[64:96, 1, 0:n], ALU.mult, [Rpb, Rrope], [Rtb])
                            self.tt(V, KT[64:96, k0:k0 + n], ta[64:96, 0:n], tb[64:96, 0:n], ALU.add, [Rta, Rtb],
                                    [RKTr[kt0 + i] for i in range(nt)])
                        need_q = (seg == "own") or (seg == "ctx" and not last)
                        if need_q:
                            qc0 = (0 if seg == "ctx" else c.CTX) + t0 * 128
                            qt0 = (0 if seg == "ctx" else c.NC) + t0
                            pq = [pp[3], pp[0]]
                            Rpq = [Rpp[3], Rpp[0]]
                            for ch in range(2):
                                self.proj_fm(pq[ch], Rpq[ch], wts[16 + ch], hT, RhT, 0, n)
                                self.act(sq[:, ch, 0:n], pq[ch][:, 0:n], AF.Square, [Rpq[ch]], [Rsq])
                            for ch in range(2):
                                self.mm(pss[1][:, 0:n], self.onesb[:], sq[:, ch, 0:n], ch == 0, ch == 1, [Rsq, self.Rc], [Rpss[1]])
                            self.ts(V, rs[:, 0:n], pss[1][:, 0:n], 1.0 / 256, EPS, ALU.mult, ALU.add, [Rpss[1]], [Rrs])
                            self.rsqrt_(rs[:, 0:n], rs[:, 0:n], n, [Rrs], [Rrs])
                            for ch in range(2):
                                self.stt(cqT[:, ch, qc0:qc0 + n], pq[ch][:, 0:n], self.vecs[:, vo["qg"] + ch:vo["qg"] + ch + 1],
                                         rs[:, 0:n], ALU.mult, ALU.mult, [Rpq[ch], self.Rvec, Rrs],
                                         [Rcq[qt0 + i] for i in range(nt)])
                if self.kv_mode == "write":
                    em.dma_group(SP, [(self.ckvcache, ckvT[:]), (self.krcache, KT[64:96, :])],
                                 reads=list(Rckv) + list(RKTr), writes=[self.Rmlac], semkey="kvst")
                em.flush()
            with contextlib.ExitStack() as at:
                QT = self.T(at, [128, NQ], BF16, "QT")
                RQT = [Res() for _ in range(c.NT + c.NC)]
                Vh = self.T(at, [128, c.NKT, 65], BF16, "Vh")
                RVh = Res()
                mrope = self.T(at, [128, 2, NQ], F32, "mrope")
                Rmr = Res()
                em.dma_group(SP, [(mrope[64:96, :, 0:c.CTX], self.cur_rope[64:96, 2:4, 0:c.CTX]),
                                  (mrope[64:96, :, c.CTX:NQ], self.cur_rope[64:96, 2:4, c.CTX + c.TH:c.NK])],
                             writes=[Rmr], semkey="mrope")
                Otok = self.T(at, [128, c.NT + c.NC, 512], BF16, "Otok")
                ROt = [Res() for _ in range(c.NT + c.NC)]
                wuq = self.T(at, [128, 2, 2048], BF16, "wuq")
                wukv = self.T(at, [128, 1024], BF16, "wukv")
                Rwu = Res()
                em.dma_group(G, [(wuq[:, k, :], self.W["wuq"][k]) for k in range(2)] + [(wukv[:], self.W["wukv"])],
                             writes=[Rwu], semkey="wuq")
                em.op(G, lambda e: e.memset(Vh[:, :, 64:65], 1.0), writes=[RVh])
                with contextlib.ExitStack() as sp:
                    pstp = [self.P(sp, [128, 1024], F32, "pstp") for _ in range(2)]
                    Rpst = [Res(), Res()]
                    po = [self.P(sp, [128, 512], F32, "po") for _ in range(GT)]
                    Rpo = [Res() for _ in range(GT)]
                    Rpw = Rpst
                    NP_ = 3
                    pb_ = [self.T(sp, [128, 2, 512], BF16, "pbuf") for _ in range(NP_)]
                    Rpb = [Res() for _ in range(NP_)]
                    ta = self.T(sp, [128, QG], F32, "ta")
                    tb = self.T(sp, [128, QG], F32, "tb")
                    Rta, Rtb = Res(), Res()
                    rin = self.T(sp, [128, 2], F32, "rin")
                    Rrin = Res()
                    scale = 96.0 ** -0.5
                    cnt = 0
                    wcnt = 0
                    qsegs = [("own", c.NG, list(range(c.NKT)))] + ([] if last else [("ctx", 1, list(range(c.NC)))])
                    for h in range(8):
                        for k0 in range(0, c.NK, 512):
                            n = min(512, c.NK - k0)
                            w_ = wcnt % 2
                            wcnt += 1
                            self.mm(pstp[w_][0:64, 0:n], wukv[:, h * 128:h * 128 + 64], ckvT[:, k0:k0 + n], True, True,
                                    [Rwu] + [Rckv[k0 // 128 + i] for i in range(n // 128)], [Rpw[w_]])
                            self.cp(S if (wcnt % 2) else V, KT[0:64, k0:k0 + n], pstp[w_][0:64, 0:n], [Rpw[w_]], [RKTn])
                        for kt0 in range(0, c.NKT, 8):
                            nt8 = min(8, c.NKT - kt0)
                            w_ = wcnt % 2
                            wcnt += 1
                            for i in range(nt8):
                                kt = kt0 + i
                                self.mm(pstp[w_][:, i * 64:(i + 1) * 64], ckvT[:, kt * 128:(kt + 1) * 128],
                                        wukv[:, h * 128 + 64:(h + 1) * 128], True, True, [Rwu, Rckv[kt]], [Rpw[w_]])
                            self.cp(V, Vh[:, kt0:kt0 + nt8, 0:64], pstp[w_][:, 0:nt8 * 64].rearrange("p (t d) -> p t d", d=64),
                                    [Rpw[w_]], [RVh])
                        for seg, ng, keys in qsegs:
                            for gidx in range(ng):
                                nt = GT if seg == "own" else c.NC
                                n = nt * 128
                                qc0 = (c.CTX + gidx * QG) if seg == "own" else 0
                                qt0 = (c.NC + gidx * GT) if seg == "own" else 0
                                Rq_ = [RQT[qt0 + i] for i in range(nt)]
                                pa, Rpa = pstp[0], Rpw[0]
                                pb2, Rpb2 = pstp[1], Rpw[1]
                                for (pq, Rpq, s_) in ((pa, Rpa, 0), (pb2, Rpb2, 1)):
                                    c0 = (h * 2 + s_) * 128
                                    for k in range(2):
                                        self.mm(pq[0:96, 0:n], wuq[:, k, c0:c0 + 96], cqT[:, k, qc0:qc0 + n], k == 0, k == 1,
                                                [Rwu] + [Rcq[qt0 + i] for i in range(nt)], [Rpq])
                                self.cp(S, QT[0:64, qc0:qc0 + n], pa[0:64, 0:n], [Rpa], Rq_)
                                self.tt(V, ta[64:96, 0:n], pa[64:96, 0:n], mrope[64:96, 0, qc0:qc0 + n], ALU.mult, [Rpa, Rmr], [Rta])
                                self.tt(V, tb[64:96, 0:n], pb2[64:96, 0:n], mrope[64:96, 1, qc0:qc0 + n], ALU.mult, [Rpb2, Rmr], [Rtb])
                                self.tt(V, QT[64:96, qc0:qc0 + n], ta[64:96, 0:n], tb[64:96, 0:n], ALU.add, [Rta, Rtb], Rq_)
                        for seg, ng, keys in qsegs:
                            for gidx in range(ng):
                                nt = GT if seg == "own" else c.NC
                                n = nt * 128
                                qc0 = (c.CTX + gidx * QG) if seg == "own" else 0
                                qt0 = (c.NC + gidx * GT) if seg == "own" else 0
                                Rq_ = [RQT[qt0 + i] for i in range(nt)]
                                pairs = [keys[i:i + 2] for i in range(0, len(keys), 2)]
                                base = cnt
                                cnt += len(pairs)

                                def st_pair(pi):
                                    sb = (base + pi) % 2
                                    for rep in range(1 + DUP_ST_MLA):
                                        for j, kt in enumerate(pairs[pi]):
                                            self.mm(pstp[sb][:, j * 512:j * 512 + n], KT[0:96, kt * 128:(kt + 1) * 128],
                                                    QT[0:96, qc0:qc0 + n], True, True, [RKTn, RKTr[kt]] + Rq_, [Rpst[sb]])
                                st_pair(0)
                                if len(pairs) > 1:
                                    st_pair(1)
                                for pi, pr in enumerate(pairs):
                                    sb = (base + pi) % 2
                                    pbi = (base + pi) % NP_
                                    np_ = len(pr)
                                    self.act(pb_[pbi][:, 0:np_, 0:n],
                                             pstp[sb][:, :].rearrange("p (j c) -> p j c", j=2)[:, 0:np_, 0:n], AF.Exp,
                                             [Rpst[sb]], [Rpb[pbi]], scale=scale)
                                    if pi + 2 < len(pairs):
                                        st_pair(pi + 2)
                                    for j, kt in enumerate(pr):
                                        first = (pi == 0 and j == 0)
                                        lastk = (pi == len(pairs) - 1 and j == np_ - 1)
                                        for i in range(nt):
                                            self.mm(po[i][:, 0:65], pb_[pbi][:, j, i * 128:(i + 1) * 128], Vh[:, kt, 0:65],
                                                    first, lastk, [Rpb[pbi], RVh], [Rpo[i]])
                                for i in range(nt):
                                    self.em.op(V, lambda e_, i=i: e_.reciprocal(out=rin[:, 0:1], in_=po[i][:, 64:65]), [Rpo[i]], [Rrin])
                                    self.ts(V, Otok[:, qt0 + i, h * 64:(h + 1) * 64], po[i][:, 0:64], rin[:, 0:1], None, ALU.mult,
                                            None, [Rpo[i], Rrin], [ROt[qt0 + i]])
                    em.flush()
                with contextlib.ExitStack() as sp:
                    ptb = [self.P(sp, [128, 4, 128], BF16, "ptr") for _ in range(2)]
                    Rptb = [Res(), Res()]
                    for ti in range(c.NT + (0 if last else c.NC)):
                        b = ti % 2
                        tq = (ti + c.NC) if ti < c.NT else (ti - c.NT)
                        for q in range(4):
                            self.tr(ptb[b][:, q, :], Otok[:, tq, q * 128:(q + 1) * 128], self.ident[:], [ROt[tq], self.Rc], [Rptb[b]])
                        if ti < c.NT:
                            self.cp(S if ti % 2 else V, self.mixT[:, 4:8, ti * 128:(ti + 1) * 128], ptb[b][:], [Rptb[b]],
                                    [self.Rmix[4 + q][ti] for q in range(4)])
                        else:
                            tc_ = ti - c.NT
                            self.cp(S if ti % 2 else V, self.mixTc[:, 4:8, tc_ * 128:(tc_ + 1) * 128], ptb[b][:], [Rptb[b]],
                                    [self.Rmixc[4 + q][tc_] for q in range(4)])
                    em.flush()

    def resid_update(self, sp, py, Rpy, xt, Rxt, gbc, Rgbc, st, Rst, junk, Rjunk, tmp, Rtmp):
        c = self.c
        hw = c.D // 2
        for j in range(2):
            self.act(junk[:, 0:hw], py[j][:, 0:hw], AF.Square, [Rpy[j]], [Rjunk, Rst], accum_out=st[:, j:j + 1])
        self.tt(V, st[:, 2:3], st[:, 0:1], st[:, 1:2], ALU.add, [Rst], [Rst])
        self.ts(V, st[:, 2:3], st[:, 2:3], 1.0 / c.D, EPS, ALU.mult, ALU.add, [Rst], [Rst])
        self.rsqrt_(st[:, 3:4], st[:, 2:3], 1, [Rst], [Rst])
        for j in range(2):
            self.stt(tmp[:, 0:hw], py[j][:, 0:hw], st[:, 3:4], gbc[:, j * hw:(j + 1) * hw], ALU.mult, ALU.mult,
                     [Rpy[j], Rst, Rgbc], [Rtmp])
            self.tt(V, xt[:, j * hw:(j + 1) * hw], xt[:, j * hw:(j + 1) * hw], tmp[:, 0:hw], ALU.add, [Rxt, Rtmp], [Rxt])

    def phase_out(self, last):
        c, em = self.c, self.em
        hw = c.D // 2
        with contextlib.ExitStack() as sp:
            wout = self.T(sp, [128, 8, c.D], BF16, "wout")
            Rw = Res()
            em.dma_group(G, [(wout[:, k, :], self.W["wout"][k]) for k in range(8)], writes=[Rw], semkey="wout")
            pbank = self.P(sp, [128, 512], F32, "pbk")
            Rpbk = Res()
            py = [[self.P(sp, [128, 512], F32, "py") for _ in range(2)] for _ in range(2)]
            Rpy = [[Res(), Res()], [Res(), Res()]]
            st = self.T(sp, [128, 4], F32, "st")
            Rst = Res()
            junk = self.T(sp, [128, 512], BF16, "junk")
            Rjunk = Res()
            tmp = self.T(sp, [128, 512], F32, "tmp")
            Rtmp = Res()
            segs = [("own", 0, c.NT)] + ([] if last else [("ctx", 1, c.NC)])
            ti = 0
            for seg, s_, ntile in segs:
                gbc, Rgbc = self.bcast_vec(sp, 2, s_, pbank, Rpbk)
                mixT, Rmix = (self.mixT, self.Rmix) if seg == "own" else (self.mixTc, self.Rmixc)
                for t in range(ntile):
                    b = ti % 2
                    ti += 1
                    for j in range(2):
                        for k in range(8):
                            self.mm(py[b][j][:, 0:hw], mixT[:, k, t * 128:(t + 1) * 128], wout[:, k, j * hw:(j + 1) * hw],
                                    k == 0, k == 7, [Rmix[k][t], Rw], [Rpy[b][j]])
                    xt, Rxt = (self.x[:, t, :], self.Rx[t]) if seg == "own" else (self.xc[:, t, :], self.Rxc[t])
                    self.resid_update(sp, py[b], Rpy[b], xt, Rxt, gbc, Rgbc, st, Rst, junk, Rjunk, tmp, Rtmp)
            em.flush()

    def phase_ffn(self, last, post_tile=None):
        c, em = self.c, self.em
        hw = c.D // 2
        QG = c.QG
        with contextlib.ExitStack() as sp:
            w2 = self.T(sp, [128, c.FC, c.D], BF16, "w2")
            Rw2 = Res()
            em.dma_group(G, [(w2[:, f, :], self.W["w2"][f]) for f in range(c.FC)], writes=[Rw2], semkey="w2")
            NW = 3
            w1 = [self.T(sp, [128, c.KC, 256], BF16, "w1") for _ in range(NW)]
            Rw1 = [Res() for _ in range(NW)]
            nctx = self.make_norm_ctx(sp)
            hT = self.T(sp, [128, c.KC, QG], BF16, "hT")
            RhT = Res()
            hid = self.T(sp, [128, c.FC, QG], BF16, "hid")
            Rhid = [Res() for _ in range(c.FC)]
            pu = [self.P(sp, [128, 512], F32, "pu") for _ in range(2)]
            Rpu = [Res(), Res()]
            pgs = [self.P(sp, [128, 512], F32, "pg") for _ in range(2)]
            Rpgs = [Res(), Res()]
            pg, Rpg = pgs[0], Rpgs[0]
            py = [self.P(sp, [128, 512], F32, "py") for _ in range(2)]
            Rpy = [Res(), Res()]
            sgt = [self.T(sp, [128, QG], BF16, "sg") for _ in range(2)]
            Rsg = [Res(), Res()]
            st = self.T(sp, [128, 4], F32, "st")
            Rst = Res()
            junk = self.T(sp, [128, 512], BF16, "junk")
            Rjunk = Res()
            tmp = self.T(sp, [128, 512], F32, "tmp")
            Rtmp = Res()
            segs = [("own", 0)] + ([] if last else [("ctx", 1)])
            wi = 0
            hT2 = self.T(sp, [128, c.KC, QG], BF16, "hT2")
            hTs, RhTs = [hT, hT2], [RhT, Res()]
            glist = []
            for seg, s_ in segs:
                for (t0, nt) in self.seg_groups(seg):
                    glist.append((seg, s_, t0, nt))

            def emit_norm(gi):
                seg, s_, t0, nt = glist[gi]
                self.norm_items(nctx, [(seg, t0 + i, hTs[gi % 2], i * 128, RhTs[gi % 2], 3, None) for i in range(nt)])
            emit_norm(0)
            gbc = Rgbc = None
            cur_seg = None
            for gi, (seg, s_, t0, nt) in enumerate(glist):
                if seg != cur_seg:
                    gbc, Rgbc = self.bcast_vec(sp, 5, s_, pg, Rpg)
                    cur_seg = seg
                hTg, RhTg = hTs[gi % 2], RhTs[gi % 2]
                n = nt * 128
                for f in range(c.FC):
                    wb = wi % NW
                    wi += 1
                    self.load_w(w1[wb][:], self.W["w1"][f].rearrange("p (k n) -> p k n", k=c.KC), Rw1[wb], f"w1_{wb}")
                    ub = f % 2
                    for k in range(c.KC):
                        self.mm(pu[ub][:, 0:n], w1[wb][:, k, 0:128], hTg[:, k, 0:n], k == 0, k == c.KC - 1, [Rw1[wb], RhTg],
                                [Rpu[ub]])
                    for k in range(c.KC):
                        self.mm(pgs[ub][:, 0:n], w1[wb][:, k, 128:256], hTg[:, k, 0:n], k == 0, k == c.KC - 1, [Rw1[wb], RhTg],
                                [Rpgs[ub]])
                    self.act(sgt[ub][:, 0:n], pgs[ub][:, 0:n], AF.Silu, [Rpgs[ub]], [Rsg[ub]])
                    self.tt(V, hid[:, f, 0:n], pu[ub][:, 0:n], sgt[ub][:, 0:n], ALU.mult, [Rpu[ub], Rsg[ub]], [Rhid[f]])
                if gi + 1 < len(glist):
                    emit_norm(gi + 1)
                for i in range(nt):
                    t = t0 + i
                    for j in range(2):
                        for f in range(c.FC):
                            self.mm(py[j][:, 0:hw], hid[:, f, i * 128:(i + 1) * 128], w2[:, f, j * hw:(j + 1) * hw],
                                    f == 0, f == c.FC - 1, [Rhid[f], Rw2], [Rpy[j]])
                    xt, Rxt = (self.x[:, t, :], self.Rx[t]) if seg == "own" else (self.xc[:, t, :], self.Rxc[t])
                    self.resid_update(sp, py, Rpy, xt, Rxt, gbc, Rgbc, st, Rst, junk, Rjunk, tmp, Rtmp)
                    if post_tile is not None and seg == "own":
                        post_tile(t)
            em.flush()


_PROGRAMS = {}


def _get_program(cfg, layers, out_ctx, fused=False):
    key = (cfg.D, cfg.TH, cfg.CTX, cfg.DFF, cfg.GW, cfg.QG, tuple(layers), out_ctx, fused)
    if key not in _PROGRAMS:
        _PROGRAMS[key] = Builder(cfg, layers, out_ctx, fused).build()
    return _PROGRAMS[key]


def _core_inputs(cfg, inp, b, h, x_full, xc_full, layers):
    c = cfg
    d = _core_tables(c, h)
    d["x_own"] = np.ascontiguousarray(x_full[b, h * c.TH:(h + 1) * c.TH])
    d["x_oth"] = np.ascontiguousarray(x_full[b, (1 - h) * c.TH:(2 - h) * c.TH])
    d["xc"] = np.ascontiguousarray(xc_full[b])
    cc = np.stack([inp["c"][b], inp["c_ctx"]], -1)
    d["cc"] = np.ascontiguousarray(cc.reshape(c.KC, 128, 2).transpose(1, 0, 2))
    return d


def run_layers(cfg, inp, layer_groups, runner=None, fused=False):
    c = cfg
    inp = {k: np.asarray(v, dtype=np.float32) for k, v in inp.items()}
    x = inp["x"]
    xc = inp["ctx"]
    for layers in layer_groups:
        out_ctx = 0 if layers[-1] == c.DEPTH - 1 else 1
        nc = _get_program(c, layers, out_ctx, fused)
        wl = {}
        for l in layers:
            wl.update(_layer_weights(c, inp, l))
        in_maps = []
        for core in range(2 * c.BATCH):
            b, h = core // 2, core % 2
            d = _core_inputs(c, inp, b, h, x, xc, layers)
            if fused:
                tb = _core_tables(c, 1 - h)
                d["flB"], d["ropetabB"], d["cxB"] = tb["fl"], tb["ropetab"], tb["cx"]
            d.update(wl)
            in_maps.append(d)
        if runner is None:
            res = run_bass_kernel_spmd(nc, in_maps, core_ids=list(range(2 * c.BATCH))).results
        else:
            res = runner(nc, in_maps)
        xn = np.zeros_like(x)
        xcn = xc.copy()
        for core in range(2 * c.BATCH):
            b, h = core // 2, core % 2
            xn[b, h * c.TH:(h + 1) * c.TH] = res[core]["x_out"]
            if out_ctx and h == 0:
                xcn[b] = res[core]["xc_out"]
        x, xc = xn, xcn
    return x


def kernel(**inputs):
    cfg = Cfg()
    out = run_layers(cfg, inputs, [list(range(cfg.DEPTH))], fused=True)
    return np.asarray(out, dtype=np.float32)
```
